# Optimizing a Trainium2 kernel written in Bass

```python
import math
import jax, jax.numpy as jnp
from jax import lax
import numpy as np

D_MODEL = 1024
BATCH = 32
SEQ = 2048
DEPTH = 1

D_MIX = D_MODEL
D_HYENA = D_MIX // 2
D_ATTN = D_MIX - D_HYENA
HEAD_DIM = 64
N_Q_HEADS = D_ATTN // HEAD_DIM
N_KV_HEADS = N_Q_HEADS // 4
GQA_GROUP = N_Q_HEADS // N_KV_HEADS
WINDOW = 128
BLOCK = 128
ROPE_THETA = 10000.0
HYENA_ORDER = 2
SHORT_CONV = 3
FILTER_EMB = 33
FILTER_HIDDEN = 64
N_DIRS = 2
DECAY_TARGET = 1e-2
FAST_DECAY_PCT = 0.3
SLOW_DECAY_PCT = 1.5
EPS = 1e-6

N_HY_PROJ = (HYENA_ORDER + 1) * D_HYENA
D_KV = N_KV_HEADS * HEAD_DIM
D_IN = N_HY_PROJ + D_HYENA + D_ATTN + 2 * D_KV + D_ATTN

kernel_name = "hymba_hyena_swa_hybrid_encoder"

F32 = jnp.float32


def rms_norm(x, g):
    xf = x.astype(F32)
    y = xf * lax.rsqrt(jnp.mean(xf * xf, axis=-1, keepdims=True) + EPS)
    return (y * g.astype(F32)).astype(x.dtype)


def centred_short_conv(u, w, b):
    L = u.shape[1]
    half = SHORT_CONV // 2
    up = jnp.pad(u, ((0, 0), (half, half), (0, 0)))
    out = b
    for j in range(SHORT_CONV):
        out = out + up[:, j:j + L] * w[j]
    return out


def hyena_filters(L, w1, b1, w2, b2, w3, b3, w4, sin_freq):
    t = jnp.linspace(0.0, 1.0, L, dtype=F32)[:, None]
    bands = (FILTER_EMB - 1) // 2
    f = jnp.linspace(1e-4, bands - 1, bands, dtype=F32)[None, :]
    w = 2.0 * math.pi * jnp.arange(L, dtype=F32)[:, None] / L
    z = jnp.concatenate([t, jnp.cos(f * w), -jnp.sin(f * w)], axis=-1)
    fr = sin_freq.astype(F32)
    h = jnp.sin(fr * (z @ w1.astype(F32) + b1.astype(F32)))
    h = jnp.sin(fr * (h @ w2.astype(F32) + b2.astype(F32)))
    h = jnp.sin(fr * (h @ w3.astype(F32) + b3.astype(F32)))
    h = (h @ w4.astype(F32)).reshape(L, HYENA_ORDER, N_DIRS, D_HYENA)
    max_decay = math.log(DECAY_TARGET) / FAST_DECAY_PCT
    min_decay = math.log(DECAY_TARGET) / SLOW_DECAY_PCT
    deltas = jnp.linspace(min_decay, max_decay, D_HYENA, dtype=F32)
    decay = jnp.exp(-t * jnp.abs(deltas)[None, :])
    return h * decay[:, None, None, :]


def two_sided_spectrum(h):
    L = h.shape[0]
    zero = jnp.zeros((1,) + h.shape[1:2] + h.shape[3:], F32)
    k = jnp.concatenate([h[:, :, 0], zero, h[1:, :, 1][::-1]], axis=0)
    return jnp.fft.rfft(k, n=2 * L, axis=0)


def long_conv(u, k_f, bias):
    L = u.shape[1]
    y = jnp.fft.irfft(jnp.fft.rfft(u, n=2 * L, axis=1) * k_f[None], n=2 * L, axis=1)[:, :L]
    return y + u * bias


def rope(x, pos):
    half = HEAD_DIM // 2
    inv = ROPE_THETA ** (-jnp.arange(half, dtype=F32) / half)
    ang = pos[:, None] * inv[None, :]
    cos = jnp.cos(ang)[None, :, None, :]
    sin = jnp.sin(ang)[None, :, None, :]
    xf = x.astype(F32)
    x1, x2 = xf[..., :half], xf[..., half:]
    return jnp.concatenate([x1 * cos - x2 * sin, x2 * cos + x1 * sin], axis=-1).astype(x.dtype)


def windowed_sink_attention(q, k, v, sink):
    B, S = q.shape[0], q.shape[1]
    nb = S // BLOCK
    span = BLOCK + 2 * WINDOW
    kp = jnp.pad(k, ((0, 0), (WINDOW, WINDOW), (0, 0), (0, 0)))
    vp = jnp.pad(v, ((0, 0), (WINDOW, WINDOW), (0, 0), (0, 0)))
    qb = q.reshape(B, nb, BLOCK, N_KV_HEADS, GQA_GROUP, HEAD_DIM).transpose(1, 0, 2, 3, 4, 5)
    scale = HEAD_DIM ** -0.5
    sink32 = sink.astype(F32)[None, :, :, None, None]

    def block(args):
        qi, i = args
        start = i * BLOCK
        kw = lax.dynamic_slice_in_dim(kp, start, span, axis=1)
        vw = lax.dynamic_slice_in_dim(vp, start, span, axis=1)
        s = jnp.einsum('bqkgd,bskd->bkgqs', qi.astype(F32), kw.astype(F32)) * scale
        qpos = start + jnp.arange(BLOCK)
        kpos = start - WINDOW + jnp.arange(span)
        valid = ((jnp.abs(kpos[None, :] - qpos[:, None]) <= WINDOW)
                 & (kpos >= 0)[None, :] & (kpos < S)[None, :])
        s = jnp.where(valid, s, -jnp.inf)
        m = jnp.maximum(jnp.max(s, axis=-1, keepdims=True), sink32)
        p = jnp.exp(s - m)
        denom = jnp.sum(p, axis=-1, keepdims=True) + jnp.exp(sink32 - m)
        o = jnp.einsum('bkgqs,bskd->bqkgd', p / denom, vw.astype(F32))
        return o.astype(q.dtype)

    out = lax.map(block, (qb, jnp.arange(nb)))
    return out.transpose(1, 0, 2, 3, 4, 5).reshape(B, S, N_Q_HEADS * HEAD_DIM)


def setup_inputs(seed: int = 0) -> dict:
    key = jax.random.key(seed)
    ks = jax.random.split(key, 24)
    nrm = lambda k, shape, s: jax.random.normal(k, shape, F32) * s
    return {
        'x': nrm(ks[0], (BATCH, SEQ, D_MODEL), 1.0),
        'norm_g': 1.0 + nrm(ks[1], (DEPTH, D_MODEL), 0.02),
        'w_in': nrm(ks[2], (DEPTH, D_MODEL, D_IN), D_MODEL ** -0.5),
        'conv_w': nrm(ks[3], (DEPTH, SHORT_CONV, N_HY_PROJ), SHORT_CONV ** -0.5),
        'conv_b': nrm(ks[4], (DEPTH, N_HY_PROJ), 0.01),
        'filt_w1': nrm(ks[5], (DEPTH, FILTER_EMB, FILTER_HIDDEN), FILTER_EMB ** -0.5),
        'filt_b1': nrm(ks[6], (DEPTH, FILTER_HIDDEN), 0.1),
        'filt_w2': nrm(ks[7], (DEPTH, FILTER_HIDDEN, FILTER_HIDDEN), FILTER_HIDDEN ** -0.5),
        'filt_b2': nrm(ks[8], (DEPTH, FILTER_HIDDEN), 0.1),
        'filt_w3': nrm(ks[9], (DEPTH, FILTER_HIDDEN, FILTER_HIDDEN), FILTER_HIDDEN ** -0.5),
        'filt_b3': nrm(ks[10], (DEPTH, FILTER_HIDDEN), 0.1),
        'filt_w4': nrm(ks[11], (DEPTH, FILTER_HIDDEN, HYENA_ORDER * N_DIRS * D_HYENA), FILTER_HIDDEN ** -0.5),
        'filt_sin_freq': 1.0 + nrm(ks[12], (DEPTH, FILTER_HIDDEN), 0.02),
        'hyena_bias': nrm(ks[13], (DEPTH, HYENA_ORDER, D_HYENA), 1.0),
        'q_norm_g': 1.0 + nrm(ks[14], (DEPTH, HEAD_DIM), 0.02),
        'k_norm_g': 1.0 + nrm(ks[15], (DEPTH, HEAD_DIM), 0.02),
        'attn_sink': nrm(ks[16], (DEPTH, N_Q_HEADS), 0.5),
        'hy_out_norm_g': 1.0 + nrm(ks[17], (DEPTH, D_HYENA), 0.02),
        'attn_out_norm_g': 1.0 + nrm(ks[18], (DEPTH, D_ATTN), 0.02),
        'w_out': nrm(ks[19], (DEPTH, D_MIX, D_MODEL), D_MIX ** -0.5),
    }


def reference(x, norm_g, w_in, conv_w, conv_b, filt_w1, filt_b1, filt_w2, filt_b2,
              filt_w3, filt_b3, filt_w4, filt_sin_freq, hyena_bias, q_norm_g, k_norm_g,
              attn_sink, hy_out_norm_g, attn_out_norm_g, w_out):
    B, S = x.shape[0], x.shape[1]
    pos = jnp.arange(S, dtype=F32)
    for l in range(DEPTH):
        h = rms_norm(x, norm_g[l])
        proj = h @ w_in[l]
        o0 = N_HY_PROJ
        o1 = o0 + D_HYENA
        o2 = o1 + D_ATTN
        o3 = o2 + D_KV
        o4 = o3 + D_KV
        hy_in, g_h = proj[..., :o0], proj[..., o0:o1]
        q, k, v = proj[..., o1:o2], proj[..., o2:o3], proj[..., o3:o4]
        g_a = proj[..., o4:]

        u = centred_short_conv(hy_in, conv_w[l], conv_b[l]).astype(F32)
        hv, hx1, hx2 = jnp.split(u, HYENA_ORDER + 1, axis=-1)
        filt = hyena_filters(S, filt_w1[l], filt_b1[l], filt_w2[l], filt_b2[l],
                             filt_w3[l], filt_b3[l], filt_w4[l], filt_sin_freq[l])
        k_f = two_sided_spectrum(filt)
        hb = hyena_bias[l].astype(F32)
        z = hx1 * long_conv(hv, k_f[:, 0], hb[0])
        z = hx2 * long_conv(z, k_f[:, 1], hb[1])
        y_h = z.astype(x.dtype)

        q = rms_norm(q.reshape(B, S, N_Q_HEADS, HEAD_DIM), q_norm_g[l])
        k = rms_norm(k.reshape(B, S, N_KV_HEADS, HEAD_DIM), k_norm_g[l])
        q = rope(q, pos).reshape(B, S, N_KV_HEADS, GQA_GROUP, HEAD_DIM)
        k = rope(k, pos)
        v = v.reshape(B, S, N_KV_HEADS, HEAD_DIM)
        y_a = windowed_sink_attention(q, k, v, attn_sink[l].reshape(N_KV_HEADS, GQA_GROUP))

        y = jnp.concatenate([rms_norm(y_h, hy_out_norm_g[l]) * jax.nn.silu(g_h),
                             rms_norm(y_a, attn_out_norm_g[l]) * jax.nn.silu(g_a)], axis=-1)
        x = x + y @ w_out[l]
    return x
```

```python
import contextlib
import math
import numpy as np
import ml_dtypes
import concourse.bass as bass
import concourse.mybir as mybir
from concourse.bass_utils import run_bass_kernel_spmd

F32 = mybir.dt.float32
BF16 = mybir.dt.bfloat16
AF = mybir.ActivationFunctionType
ALU = mybir.AluOpType
AX = mybir.AxisListType

NCORES = 8
NB = 4
L = 2048
NT = 16
D = 1024
NFFT = 4096
EPS = 1e-6
DEBUG = False


class Sched:
    ENG = ('pe', 'act', 'dve', 'pool', 'sp')
    NRING = 8

    def __init__(self, nc, stack):
        self.nc = nc
        self.e = {'pe': nc.tensor, 'act': nc.scalar, 'dve': nc.vector,
                  'pool': nc.gpsimd, 'sp': nc.sync}
        self.sem = {}
        for k in self.ENG:
            self.sem[k] = stack.enter_context(nc.semaphore('s_' + k))
        self.cnt = {k: 0 for k in self.ENG}
        self.dq = {}
        for q in ('sp', 'pool', 'act'):
            for i in range(self.NRING):
                self.sem[('d', q, i)] = stack.enter_context(nc.semaphore('d_%s_%d' % (q, i)))
            self.dq[q] = 0
        self.seen = {k: {} for k in self.ENG}
        self.last_w = {}
        self.readers = {}
        self.all_dma = []

    def _wait(self, eng, key, val):
        if self.seen[eng].get(key, 0) >= val:
            return
        self.e[eng].wait_ge(self.sem[key], val)
        self.seen[eng][key] = val

    def _deps(self, eng, reads, writes):
        deps = {}

        def add(t, same_ok):
            if t is None:
                return
            key, val = t
            if key == eng and eng == 'pe':
                return
            if deps.get(key, 0) < val:
                deps[key] = val
        for b in reads:
            add(self.last_w.get(b), True)
        for b in writes:
            add(self.last_w.get(b), True)
            for t in self.readers.get(b, ()):
                add(t, False)
        for key, val in deps.items():
            self._wait(eng, key, val)

    def _record(self, ticket, reads, writes):
        for b in reads:
            self.readers.setdefault(b, []).append(ticket)
        for b in writes:
            self.last_w[b] = ticket
            self.readers[b] = []

    def op(self, eng, fn, reads=(), writes=(), signal=True):
        self._deps(eng, reads, writes)
        inst = fn(self.e[eng])
        if signal:
            self.cnt[eng] += 1
            inst.then_inc(self.sem[eng], 1)
            ticket = (eng, self.cnt[eng])
        else:
            ticket = (eng, self.cnt[eng] + 1)
        self._record(ticket, reads, writes)
        return ticket

    def dma(self, q, out, in_, reads=(), writes=(), **kw):
        i = self.dq[q]
        slot = i % self.NRING
        key = ('d', q, slot)
        if i >= self.NRING:
            self._wait(q, key, 16 * (i // self.NRING))
        self._deps(q, reads, writes)
        self.e[q].dma_start(out=out, in_=in_, **kw).then_inc(self.sem[key], 16)
        self.dq[q] = i + 1
        ticket = (key, 16 * (i // self.NRING + 1))
        self._record(ticket, reads, writes)
        self.all_dma.append(ticket)
        return ticket

    def _last_dma(self):
        last = {}
        for key, val in self.all_dma:
            if last.get(key, 0) < val:
                last[key] = val
        return last

    def barrier(self):
        last = self._last_dma()
        for eng in self.ENG:
            for other in self.ENG:
                if other != eng and self.cnt[other] > 0:
                    self._wait(eng, other, self.cnt[other])
            for key, val in last.items():
                self._wait(eng, key, val)
        self.all_dma = []
        self.last_w = {}
        self.readers = {}

    def finish(self, eng='sp'):
        for key, val in self._last_dma().items():
            self._wait(eng, key, val)


_CONST = None


def _consts():
    global _CONST
    if _CONST is not None:
        return _CONST
    bf = ml_dtypes.bfloat16
    m = np.arange(NFFT)
    pos = np.where(m < L, m, NFFT - m)
    pos_c = np.minimum(pos, L - 1)
    t = np.linspace(0.0, 1.0, L, dtype=np.float32)[:, None]
    bands = 16
    f = np.linspace(1e-4, bands - 1, bands, dtype=np.float32)[None, :]
    w = (2.0 * math.pi * np.arange(L, dtype=np.float32)[:, None] / L).astype(np.float32)
    z = np.concatenate([t, np.cos(f * w), -np.sin(f * w)], axis=-1).astype(np.float32)
    max_decay = math.log(1e-2) / 0.3
    min_decay = math.log(1e-2) / 1.5
    deltas = np.linspace(min_decay, max_decay, 512, dtype=np.float32)
    decay = np.exp(-t * np.abs(deltas)[None, :]).astype(np.float32)
    tt = np.arange(NFFT, dtype=np.int64)[:, None]
    g = np.arange(NFFT, dtype=np.int64)[None, :]
    gg = g % L
    ang = 2.0 * np.pi * ((gg * tt) % NFFT).astype(np.float64) / NFFT
    Fw = np.where(g < L, np.cos(ang), -np.sin(ang))
    Fw[:, L] = np.cos(np.pi * (np.arange(NFFT) % 2))
    Fw_t = Fw.reshape(32, 128, 32, 128).transpose(2, 1, 0, 3)
    Fw_t = np.ascontiguousarray(Fw_t).astype(bf)
    jj = np.arange(1024, dtype=np.int64)[:, None]
    ff = np.arange(L, dtype=np.int64)[None, :]
    ah = 2.0 * np.pi * ((ff * (2 * jj + 1)) % (2 * NFFT)).astype(np.float64) / (2 * NFFT)
    sgn = np.where(np.arange(1024) % 2 == 0, 1.0, -1.0)
    FhRe = np.cos(ah)
    FhIm = -np.sin(ah)
    FhIm[:, 0] = -sgn
    Fh = np.concatenate([FhRe, FhIm], axis=1)
    Fh_t = np.ascontiguousarray(Fh.reshape(8, 128, 2, 16, 128).transpose(3, 1, 2, 0, 4)).astype(bf)
    BhRe = np.cos(ah).T * (2.0 / NFFT)
    BhRe[0, :] = 1.0 / NFFT
    BhIm = np.sin(ah).T * (2.0 / NFFT)
    BhIm[0, :] = sgn / NFFT
    Bh = np.concatenate([BhRe, BhIm], axis=0)
    Bh_t = np.ascontiguousarray(Bh.reshape(32, 128, 8, 128).transpose(2, 1, 0, 3)).astype(bf)
    jmat = np.ascontiguousarray(np.eye(128, dtype=np.float32)[::-1]).astype(bf)
    half = 32
    inv = (10000.0 ** (-np.arange(half, dtype=np.float32) / half)).astype(np.float32)
    ang = np.arange(L, dtype=np.float32)[:, None] * inv[None, :]
    cos = np.cos(ang).astype(np.float32).reshape(NT, 128, half).transpose(1, 0, 2)
    sin = np.sin(ang).astype(np.float32).reshape(NT, 128, half).transpose(1, 0, 2)
    rope_cs = np.ascontiguousarray(np.stack([cos, sin], axis=1))
    ident = np.eye(128, dtype=np.float32).astype(bf)
    s_ = np.arange(128)[:, None]
    q_ = np.arange(128)[None, :]
    maskL = (q_ <= s_).astype(np.float32).astype(bf)
    maskU = (s_ <= q_).astype(np.float32).astype(bf)
    masks = np.ascontiguousarray(np.stack([maskL, maskU], axis=1))
    _CONST = dict(zT=np.ascontiguousarray(z.T), decay=decay, Fw_t=Fw_t, Fh_t=Fh_t, Bh_t=Bh_t, jmat=jmat,
                  rope_cs=rope_cs, ident=ident, masks=masks)
    return _CONST


def build_nc():
    nc = bass.Bass('TRN2', target_bir_lowering=False)

    def din(name, shape, dt=F32):
        return nc.dram_tensor(name, list(shape), dt, kind='ExternalInput')

    def dscr(name, shape, dt):
        return nc.dram_tensor(name, list(shape), dt, kind='ExternalOutput' if DEBUG else 'Internal')

    xT_d = din('xT', [D, NB * L])
    x_d = din('x', [NB * L, D])
    win_d = din('w_in', [D, 3328])
    wout_d = din('w_out', [D, D])
    gcol_d = din('gcol', [128, 8])
    cwc_d = din('cwc', [128, 48])
    fw1_d = din('filt_w1', [33, 64])
    fw2_d = din('filt_w2', [64, 64])
    fw3_d = din('filt_w3', [64, 64])
    fw4_d = din('filt_w4', [64, 2048])
    fcol_d = din('fcols', [64, 4])
    hb_d = din('hyena_bias', [1, 1024])
    qg_d = din('q_norm_g', [1, 64])
    kg_d = din('k_norm_g', [1, 64])
    sink_d = din('attn_sink', [1, 8])
    hyg_d = din('hy_out_norm_g', [1, 512])
    atg_d = din('attn_out_norm_g', [1, 512])
    zT_d = din('zT', [33, L])
    dec_d = din('decay', [L, 512])
    Fw_d = din('Fw_t', [32, 128, 32 * 128], BF16)
    Fh_d = din('Fh_t', [16, 128, 2 * 8 * 128], BF16)
    Bh_d = din('Bh_t', [8, 128, 32 * 128], BF16)
    jmat_d = din('jmat', [128, 128], BF16)
    rope_d = din('rope_cs', [128, 2 * 16 * 32])
    ident_d = din('ident', [128, 128], BF16)
    masks_d = din('masks', [128, 256], BF16)
    out_d = nc.dram_tensor('out', [NB * L, D], F32, kind='ExternalOutput')

    hT_s = dscr('hT_s', [NB, 128, 8 * 2050], BF16)
    U_s = dscr('U_s', [3, NB * L, 512], BF16)
    SG_s = dscr('SG_s', [NB * L, 512], BF16)
    yT_s = dscr('yT_s', [NB * NT, 128, 8, 128], BF16)
    Kh_s = dscr('Kh_s', [16, 128, 2, 1024], BF16)

    def bc(handle, ncols, parts=128, off=0):
        return bass.AP(handle, off, [[0, parts], [1, ncols]])

    with contextlib.ExitStack() as G:
        S = Sched(nc, G)

        def sb(st, name, shape, dt):
            return st.enter_context(nc.sbuf_tensor('sb_' + name, list(shape), dt))

        def ps(st, name, shape, dt=F32):
            return st.enter_context(nc.psum_tensor('ps_' + name, list(shape), dt))

        ident = sb(G, 'ident', [128, 128], BF16)
        ones = sb(G, 'ones', [128, 128], BF16)
        S.dma('sp', ident[:], ident_d.ap(), writes=['ident'])
        S.op('dve', lambda e: e.memset(ones[:], 1.0), writes=['ones'])

        def compute_hT(st, b, hT, tagp):
            xs = [sb(st, tagp + 'xs%d' % i, [128, 8, 512], F32) for i in range(2)]
            sq = sb(st, tagp + 'sq', [128, 8, 512], BF16)
            rs = sb(st, tagp + 'rs', [128, 512], F32)
            psr = ps(st, tagp + 'psr', [128, 512])
            return xs, sq, rs, psr

        def emit_hT_stage(st_, b, tt, hT, bufs, hname):
            xs, sq, rs, psr = bufs
            xb = xs[tt % 2]
            xn = 'xs%d' % (tt % 2)
            if st_ == 0:
                src = xT_d.ap()[:, b * L + tt * 512: b * L + (tt + 1) * 512].rearrange('(kc p) n -> p kc n', p=128)
                S.dma('sp', xb[:], src, writes=[xn])
                S.op('act', lambda e: e.activation(out=sq[:], in_=xb[:], func=AF.Square), reads=[xn], writes=['sq'])
            elif st_ == 1:
                for kc in range(8):
                    S.op('pe', lambda e: e.matmul(psr[:], lhsT=ones[:], rhs=sq[:, kc, :], start=(kc == 0), stop=(kc == 7)),
                         reads=['sq', 'ones'], writes=['psr'], signal=(kc == 7))
                S.op('act', lambda e: e.activation(out=rs[:], in_=psr[:], func=AF.Sqrt, scale=1.0 / D, bias=eps_t[:, 0:1]),
                     reads=['psr', 'eps'], writes=['rs'])
                S.op('dve', lambda e: e.reciprocal(out=rs[:], in_=rs[:]), reads=['rs'], writes=['rs'])
            else:
                for kc in range(8):
                    eng = 'dve' if kc % 2 == 0 else 'pool'
                    S.op(eng, lambda e: e.tensor_tensor(out=hT[:, kc, 1 + tt * 512: 1 + (tt + 1) * 512], in0=xb[:, kc, :], in1=rs[:], op=ALU.mult),
                         reads=[xn, 'rs'], writes=['%s%d' % (hname, kc)])

        def emit_hT_tile(b, tt, hT, bufs, hname):
            for st_ in range(3):
                emit_hT_stage(st_, b, tt, hT, bufs, hname)

        def emit_hT(b, hT, bufs, hname):
            for tt in range(4):
                emit_hT_tile(b, tt, hT, bufs, hname)

        eps_t = sb(G, 'eps_t', [128, 1], F32)
        S.op('dve', lambda e: e.memset(eps_t[:], EPS), writes=['eps'])

        with contextlib.ExitStack() as P:
            Wh = sb(P, 'Wh', [128, 8, 1536], BF16)
            hTs = [sb(P, 'hT_%d' % i, [128, 8, 2050], BF16) for i in range(2)]
            cwc = sb(P, 'cwc', [128, 12, 4], F32)
            with contextlib.ExitStack() as P0:
                gcol = sb(P0, 'gcol', [128, 8], F32)
                wst = [sb(P0, 'wst%d' % i, [128, 1536], F32) for i in range(2)]
                S.dma('sp', gcol[:], gcol_d.ap(), writes=['gcol'])
                S.dma('sp', cwc[:].rearrange('p c j -> p (c j)'), cwc_d.ap(), writes=['cwc'])
                for kc in range(8):
                    wb = wst[kc % 2]
                    wn = 'wst%d' % (kc % 2)
                    S.dma('sp', wb[:], win_d.ap()[kc * 128:(kc + 1) * 128, 0:1536], writes=[wn])
                    if kc % 2 == 0:
                        S.op('act', lambda e: e.activation(out=Wh[:, kc, :], in_=wb[:], func=AF.Copy, scale=gcol[:, kc:kc + 1]), reads=[wn, 'gcol'], writes=['Wh'])
                    else:
                        S.op('dve', lambda e: e.tensor_scalar(out=Wh[:, kc, :], in0=wb[:], scalar1=gcol[:, kc:kc + 1], scalar2=None, op0=ALU.mult), reads=[wn, 'gcol'], writes=['Wh'])
                S.barrier()
            bufs = compute_hT(P, 0, None, 'p1')
            pf = [sb(P, 'pf%d' % i, [128, 2050], F32) for i in range(2)]
            t1 = [sb(P, 't1_0', [128, 2048], F32)] * 2
            ucb = [sb(P, 'ucb%d' % i, [128, 2048], BF16) for i in range(8)]
            ust = [sb(P, 'ust%d' % i, [128, 512], BF16) for i in range(3)]
            psp = [ps(P, 'psp%d' % i, [128, 512]) for i in range(4)]
            pst = [ps(P, 'pst%d' % i, [128, 4, 128], BF16) for i in range(3)]
            HTN = [['hT%s%d' % ('ab'[q], k) for k in range(8)] for q in range(2)]
            for q in range(2):
                S.op('pool', lambda e: e.memset(hTs[q][:, :, 0:1], 0.0), writes=HTN[q])
                S.op('pool', lambda e: e.memset(hTs[q][:, :, 2049:2050], 0.0), writes=HTN[q])
            for i in range(2):
                S.op('pool', lambda e: e.memset(pf[i][:, 0:1], 0.0), writes=['pf%d' % i])
                S.op('pool', lambda e: e.memset(pf[i][:, 2049:2050], 0.0), writes=['pf%d' % i])
            cnt = {'pf': 0, 'pp': 0, 'pt': 0, 'us': 0}

            def group_mm(b, gi, n, slots):
                hT = hTs[b % 2]
                hq = 'ab'[b % 2]
                for c4 in range(4):
                    ct = gi * 4 + c4
                    k = cnt['pf'] % 2
                    cnt['pf'] += 1
                    pfb, pfn = pf[k], 'pf%d' % k
                    t1b, t1n = t1[0], 't1_0'
                    ub = ucb[(n % 2) * 4 + c4]
                    ubn = 'ucb%d' % ((n % 2) * 4 + c4)
                    for tt in range(4):
                        kp = cnt['pp'] % 4
                        cnt['pp'] += 1
                        pp, ppn = psp[kp], 'psp%d' % kp
                        for kc in range(8):
                            S.op('pe', lambda e: e.matmul(pp[:], lhsT=Wh[:, kc, ct * 128:(ct + 1) * 128], rhs=hT[:, kc, 1 + tt * 512: 1 + (tt + 1) * 512], start=(kc == 0), stop=(kc == 7)),
                                 reads=['hT%s%d' % (hq, kc), 'Wh'], writes=[ppn], signal=(kc == 7))
                        S.op('act', lambda e: e.copy(out=pfb[:, 1 + tt * 512: 1 + (tt + 1) * 512], in_=pp[:]), reads=[ppn], writes=[pfn])
                    S.op('act', lambda e: e.activation(out=t1b[:], in_=pfb[:, 1:2049], func=AF.Identity, scale=cwc[:, ct, 1:2], bias=cwc[:, ct, 3:4]),
                         reads=[pfn, 'cwc'], writes=[t1n])
                    S.op('dve', lambda e: e.scalar_tensor_tensor(out=t1b[:], in0=pfb[:, 0:2048], scalar=cwc[:, ct, 0:1], in1=t1b[:], op0=ALU.mult, op1=ALU.add),
                         reads=[pfn, 'cwc', t1n], writes=[t1n])
                    S.op('dve', lambda e: e.scalar_tensor_tensor(out=ub[:], in0=pfb[:, 2:2050], scalar=cwc[:, ct, 2:3], in1=t1b[:], op0=ALU.mult, op1=ALU.add),
                         reads=[pfn, 'cwc', t1n], writes=[ubn])
                    for fn_s in slots[c4]:
                        fn_s()

            def group_tr(b, gi, n, irange):
                for i in irange:
                    kt_ = cnt['pt'] % 3
                    cnt['pt'] += 1
                    pt, ptn = pst[kt_], 'pst%d' % kt_
                    for c4 in range(4):
                        S.op('pe', lambda e: e.transpose(pt[:, c4, :], ucb[(n % 2) * 4 + c4][:, i * 128:(i + 1) * 128], ident[:]),
                             reads=['ucb%d' % ((n % 2) * 4 + c4), 'ident'], writes=[ptn], signal=(c4 == 3))
                    ku = cnt['us'] % 3
                    cnt['us'] += 1
                    us, un = ust[ku], 'ust%d' % ku
                    S.op('act', lambda e: e.copy(out=us[:], in_=pt[:]), reads=[ptn], writes=[un])
                    S.dma('pool', U_s.ap()[gi, b * L + i * 128: b * L + (i + 1) * 128, :], us[:], reads=[un])

            pending = None
            emit_hT(0, hTs[0], bufs, 'hTa')
            for b in range(NB):
                S.dma('pool', hT_s.ap()[b], hTs[b % 2][:].rearrange('p k n -> p (k n)'), reads=HTN[b % 2])
                for gi in range(3):
                    n = b * 3 + gi
                    slots = [[], [], [], []]
                    if b + 1 < NB:
                        nh = hTs[(b + 1) % 2]
                        nhn = 'hT' + 'ab'[(b + 1) % 2]
                        tts = [(0, 1), (2,), (3,)][gi]
                        for ti, tt in enumerate(tts):
                            for st_ in range(3):
                                slots[min(3, ti + st_)].append(lambda st_=st_, tt=tt: emit_hT_stage(st_, b + 1, tt, nh, bufs, nhn))
                    if pending is not None:
                        for q4 in range(4):
                            slots[q4].append(lambda q4=q4, pd=pending: group_tr(*pd, range(q4 * 4, q4 * 4 + 4)))
                    group_mm(b, gi, n, slots)
                    pending = (b, gi, n)
            group_tr(*pending, range(NT))
            S.barrier()

        with contextlib.ExitStack() as P:
            Wr = sb(P, 'Wr', [128, 8, 1792], BF16)
            hT = sb(P, 'hT2', [128, 8, 2050], BF16)
            ropeT = sb(P, 'ropeT', [128, 8, 16, 32], F32)
            esink = sb(P, 'esink', [128, 8], F32)
            atg = sb(P, 'atg', [128, 512], F32)
            masks = sb(P, 'masks', [128, 2, 128], BF16)
            mhalf = sb(P, 'mhalf', [128, 16], F32)
            S.op('pool', lambda e: e.memset(mhalf[:], -0.5), writes=['mhalf'])
            with contextlib.ExitStack() as P0:
                gcol = sb(P0, 'gcol2', [128, 8], F32)
                wst = [sb(P0, 'wsr%d' % i, [128, 1792], F32) for i in range(2)]
                rcs = sb(P0, 'rcs', [128, 2, 16, 32], F32)
                qkg = sb(P0, 'qkg', [128, 2, 64], F32)
                S.dma('sp', gcol[:], gcol_d.ap(), writes=['gcol'])
                S.dma('sp', rcs[:].rearrange('p a i j -> p (a i j)'), rope_d.ap(), writes=['rcs'])
                S.dma('sp', qkg[:, 0, :], bc(qg_d, 64), writes=['qkg'])
                S.dma('sp', qkg[:, 1, :], bc(kg_d, 64), writes=['qkg'])
                S.dma('sp', esink[:], bc(sink_d, 8), writes=['esink'])
                S.dma('sp', atg[:], bc(atg_d, 512), writes=['atg'])
                S.dma('sp', masks[:].rearrange('p a n -> p (a n)'), masks_d.ap(), writes=['masks'])
                S.op('act', lambda e: e.activation(out=esink[:], in_=esink[:], func=AF.Exp), reads=['esink'], writes=['esink'])
                for qk in range(2):
                    sc = 0.125 if qk == 0 else 1.0
                    for ti, (cs, half) in enumerate([(0, 0), (1, 1), (0, 1), (1, 0)]):
                        gsl = qkg[:, qk, half * 32:(half + 1) * 32].unsqueeze(1).broadcast_to([128, 16, 32])
                        S.op('dve', lambda e: e.scalar_tensor_tensor(out=ropeT[:, qk * 4 + ti, :, :], in0=rcs[:, cs, :, :], scalar=sc, in1=gsl, op0=ALU.mult, op1=ALU.mult),
                             reads=['rcs', 'qkg'], writes=['ropeT'])
                for kc in range(8):
                    wb = wst[kc % 2]
                    wn = 'wsr%d' % (kc % 2)
                    S.dma('sp', wb[:], win_d.ap()[kc * 128:(kc + 1) * 128, 1536:3328], writes=[wn])
                    S.op('act', lambda e: e.activation(out=Wr[:, kc, 0:512], in_=wb[:, 0:512], func=AF.Copy, scale=gcol[:, kc:kc + 1]),
                         reads=[wn, 'gcol'], writes=['Wr'])
                    S.op('act', lambda e: e.activation(out=Wr[:, kc, 512:1024].rearrange('p (g k d) -> p g k d', g=4, k=2),
                                                       in_=wb[:, 512:1024].rearrange('p (k g d) -> p g k d', k=2, g=4),
                                                       func=AF.Copy, scale=gcol[:, kc:kc + 1]),
                         reads=[wn, 'gcol'], writes=['Wr'])
                    S.op('act', lambda e: e.activation(out=Wr[:, kc, 1024:1792], in_=wb[:, 1024:1792], func=AF.Copy, scale=gcol[:, kc:kc + 1]),
                         reads=[wn, 'gcol'], writes=['Wr'])
                S.barrier()
            QT = sb(P, 'QT', [128, 2, 4, L], BF16)
            KT = sb(P, 'KT', [128, L], BF16)
            V = sb(P, 'V', [128, NT, 2, 65], BF16)
            sga = sb(P, 'sga', [128, NT, 512], BF16)
            two = lambda name, shape, dt: [sb(P, '%s%d' % (name, i), shape, dt) for i in range(2)]
            sgh = two('sgh', [128, 512], BF16)
            qraw = two('qraw', [128, 640], F32)
            qsq = two('qsq', [128, 640], F32)
            ssq = two('ssq', [128, 10], F32)
            qn = two('qn', [128, 640], F32)
            tq = [two('tq%d' % k, [128, 256], F32) for k in range(4)]
            tk = [two('tk%d' % k, [128, 64], F32) for k in range(4)]
            qb = two('qb', [128, 640], BF16)
            Pm = [[sb(P, 'Pm%d_%d' % (i, j), [128, 512], BF16) for j in range(3)] for i in range(2)]
            den = two('den', [128, 4], F32)
            ya = two('ya', [128, 512], F32)
            yq = two('yq', [128, 512], F32)
            ssa = two('ssa', [128, 1], F32)
            yab = two('yab', [128, 512], BF16)
            yts = two('yts', [128, 4, 128], BF16)
            Bk = [ps(P, 'B%d' % i, [128, 512]) for i in range(6)]
            PT = ps(P, 'PT', [128, 4, 128], BF16)
            PK = ps(P, 'PK', [128, 128], BF16)
            S.op('pool', lambda e: e.memset(V[:, :, :, 64:65], 1.0), writes=['V'])
            S.op('pool', lambda e: e.memset(QT[64:128, 0, :, :], 0.0), writes=['QT'])
            S.op('pool', lambda e: e.memset(QT[0:64, 1, :, :], 0.0), writes=['QT'])

            def grp(bank, bname, lt, wcols):
                for kc in range(8):
                    S.op('pe', lambda e: e.matmul(bank, lhsT=lt(kc), rhs=Wr[:, kc, wcols], start=(kc == 0), stop=(kc == 7)),
                         reads=['hT', 'Wr'], writes=[bname], signal=(kc == 7))

            def proj_pe(i):
                par = i % 2
                lt = lambda kc: hT[:, kc, 1 + i * 128: 1 + (i + 1) * 128]
                grp(Bk[4][:, 0:256], 'B4', lt, slice(1024, 1280))
                grp(Bk[2][:], 'B2', lt, slice(512, 1024))
                grp(Bk[0][:], 'B0', lt, slice(0, 512))
                grp(Bk[1][:], 'B1', lt, slice(1280, 1792))

            def proj_ew(b, i):
                par = i % 2
                kvb = Bk[4][:, 0:256]
                kvn = 'B4'
                qbk = Bk[2]
                qbn = 'B2'
                sfx = str(par)
                S.op('act', lambda e: e.copy(out=qraw[par][:, 512:640], in_=kvb[:, 0:128]), reads=[kvn], writes=['qrawk' + sfx])
                S.op('act', lambda e: e.copy(out=V[:, i, :, 0:64], in_=kvb[:, 128:256].rearrange('p (k d) -> p k d', k=2)), reads=[kvn], writes=['V'])
                S.op('act', lambda e: e.copy(out=qraw[par][:, 0:512], in_=qbk[:]), reads=[qbn], writes=['qrawq' + sfx])
                S.op('dve', lambda e: e.tensor_tensor(out=qsq[par][:], in0=qraw[par][:], in1=qraw[par][:], op=ALU.mult),
                     reads=['qrawq' + sfx, 'qrawk' + sfx], writes=['qsq' + sfx])
                S.op('dve', lambda e: e.tensor_reduce(out=ssq[par][:], in_=qsq[par][:].rearrange('p (h d) -> p h d', d=64), axis=AX.X, op=ALU.add),
                     reads=['qsq' + sfx], writes=['ssq' + sfx])
                S.op('act', lambda e: e.activation(out=ssq[par][:], in_=ssq[par][:], func=AF.Sqrt, scale=1.0 / 64, bias=eps_t[:, 0:1]),
                     reads=['ssq' + sfx, 'eps'], writes=['ssq' + sfx])
                S.op('dve', lambda e: e.reciprocal(out=ssq[par][:], in_=ssq[par][:]), reads=['ssq' + sfx], writes=['ssq' + sfx])
                S.op('dve', lambda e: e.tensor_tensor(out=qn[par][:].rearrange('p (h d) -> p h d', d=64), in0=qraw[par][:].rearrange('p (h d) -> p h d', d=64),
                                                      in1=ssq[par][:].unsqueeze(2).broadcast_to([128, 10, 64]), op=ALU.mult),
                     reads=['qrawq' + sfx, 'qrawk' + sfx, 'ssq' + sfx], writes=['qn' + sfx])
                sg = sgh[par]
                sgn = 'sgh' + sfx
                S.op('act', lambda e: e.activation(out=sg[:], in_=Bk[0][:], func=AF.Silu), reads=['B0'], writes=[sgn])
                S.dma('pool', SG_s.ap()[b * L + i * 128: b * L + (i + 1) * 128, :], sg[:], reads=[sgn])
                S.op('act', lambda e: e.activation(out=sga[:, i, :], in_=Bk[1][:], func=AF.Silu), reads=['B1'], writes=['sga'])
                for qk, (c0, nh, tt_) in enumerate([(0, 8, tq), (512, 2, tk)]):
                    v4 = qn[par][:, c0:c0 + nh * 64].rearrange('p (h t j) -> p h t j', t=2, j=32)
                    o4 = qb[par][:, c0:c0 + nh * 64].rearrange('p (h t j) -> p h t j', t=2, j=32)
                    q1 = v4[:, :, 0, :]
                    q2 = v4[:, :, 1, :]
                    tb = lambda ti: ropeT[:, qk * 4 + ti, i, :].unsqueeze(1).broadcast_to([128, nh, 32])
                    a = [tt_[k][par][:, 0:nh * 32].rearrange('p (h j) -> p h j', j=32) for k in range(4)]
                    an = ['t%d%d%s' % (qk, k, sfx) for k in range(4)]
                    qbn2 = 'qb%d%s' % (qk, sfx)
                    S.op('dve', lambda e: e.tensor_tensor(out=a[0], in0=q1, in1=tb(0), op=ALU.mult), reads=['qn' + sfx, 'ropeT'], writes=[an[0]])
                    S.op('pool', lambda e: e.tensor_tensor(out=a[1], in0=q2, in1=tb(1), op=ALU.mult), reads=['qn' + sfx, 'ropeT'], writes=[an[1]])
                    S.op('pool', lambda e: e.tensor_tensor(out=a[2], in0=q2, in1=tb(2), op=ALU.mult), reads=['qn' + sfx, 'ropeT'], writes=[an[2]])
                    S.op('dve', lambda e: e.tensor_tensor(out=a[3], in0=q1, in1=tb(3), op=ALU.mult), reads=['qn' + sfx, 'ropeT'], writes=[an[3]])
                    S.op('dve', lambda e: e.tensor_tensor(out=o4[:, :, 0, :], in0=a[0], in1=a[1], op=ALU.subtract), reads=[an[0], an[1]], writes=[qbn2 + 'a'])
                    S.op('pool', lambda e: e.tensor_tensor(out=o4[:, :, 1, :], in0=a[2], in1=a[3], op=ALU.add), reads=[an[2], an[3]], writes=[qbn2 + 'b'])

            def proj_tr(i):
                par = i % 2
                sfx = str(par)
                for g in range(4):
                    S.op('pe', lambda e: e.transpose(PT[:, g, :], qb[par][:, g * 128:(g + 1) * 128], ident[:]),
                         reads=['qb0%sa' % sfx, 'qb0%sb' % sfx, 'ident'], writes=['PT'], signal=(g == 3))
                S.op('pe', lambda e: e.transpose(PK[:], qb[par][:, 512:640], ident[:]),
                     reads=['qb1%sa' % sfx, 'qb1%sb' % sfx, 'ident'], writes=['PK'])
                S.op('act', lambda e: e.copy(out=QT[0:64, 0, :, i * 128:(i + 1) * 128], in_=PT[0:64, :, :]), reads=['PT'], writes=['QT'])
                S.op('act', lambda e: e.copy(out=QT[64:128, 1, :, i * 128:(i + 1) * 128], in_=PT[64:128, :, :]), reads=['PT'], writes=['QT'])
                S.op('act', lambda e: e.copy(out=KT[:, i * 128:(i + 1) * 128], in_=PK[:]), reads=['PK'], writes=['KT'])

            def s_jobs():
                jobs = []
                for u in range(2 * NT):
                    i, kv = divmod(u, 2)
                    ccs = [c for c in (i - 1, i, i + 1) if 0 <= c < NT]
                    for ci, c in enumerate(ccs):
                        jobs.append((u, ci, c, ci == len(ccs) - 1))
                return jobs

            def att_S1(k, job):
                u, ci, c, last = job
                i, kv = divmod(u, 2)
                pr = slice(kv * 64, (kv + 1) * 64)
                bi = k % 4
                S.op('pe', lambda e: e.matmul(Bk[bi][:].rearrange('p (g q) -> p g q', g=4), lhsT=KT[:, c * 128:(c + 1) * 128], rhs=QT[:, kv, :, i * 128:(i + 1) * 128], start=True, stop=True),
                     reads=['KT', 'QT'], writes=['B%d' % bi])

            def att_E1(k, job):
                u, ci, c, last = job
                i, kv = divmod(u, 2)
                bi = k % 4
                pm = Pm[u % 2][ci]
                pmn = 'Pm%d_%d' % (u % 2, ci)
                S.op('act', lambda e: e.activation(out=pm[:], in_=Bk[bi][:], func=AF.Exp), reads=['B%d' % bi], writes=[pmn])
                if c != i:
                    mk = masks[:, 0 if c < i else 1, :].unsqueeze(1).broadcast_to([128, 4, 128])
                    S.op('dve', lambda e: e.tensor_tensor(out=pm[:].rearrange('p (g q) -> p g q', g=4), in0=pm[:].rearrange('p (g q) -> p g q', g=4), in1=mk, op=ALU.mult),
                         reads=[pmn, 'masks'], writes=[pmn])

            def att_PV(u):
                i, kv = divmod(u, 2)
                po = Bk[4 + u % 2][:, 0:260].rearrange('p (g d) -> p g d', g=4)
                pon = ['B%d' % (4 + u % 2)]
                lst = [c for c in (i - 1, i, i + 1) if 0 <= c < NT]
                for g in range(4):
                    for ci, c in enumerate(lst):
                        S.op('pe', lambda e: e.matmul(po[:, g, :], lhsT=Pm[u % 2][ci][:, g * 128:(g + 1) * 128], rhs=V[:, c, kv, :], start=(ci == 0), stop=(ci == len(lst) - 1)),
                             reads=['Pm%d_%d' % (u % 2, ci), 'V'], writes=pon, signal=(g == 3 and ci == len(lst) - 1))
                return po, pon

            def att_D(u, po, pon):
                i, kv = divmod(u, 2)
                d_ = den[u % 2]
                dn = 'den%d' % (u % 2)
                yan = 'ya%d_%d' % (i % 2, kv)
                S.op('dve', lambda e: e.tensor_tensor(out=d_[:], in0=po[:, :, 64], in1=esink[:, kv * 4:(kv + 1) * 4], op=ALU.add),
                     reads=pon + ['esink'], writes=[dn])
                S.op('dve', lambda e: e.reciprocal(out=d_[:], in_=d_[:]), reads=[dn], writes=[dn])
                S.op('dve', lambda e: e.tensor_tensor(out=ya[i % 2][:, kv * 256:(kv + 1) * 256].rearrange('p (g d) -> p g d', g=4), in0=po[:, :, 0:64],
                                                      in1=d_[:].unsqueeze(2).broadcast_to([128, 4, 64]), op=ALU.mult),
                     reads=pon + [dn], writes=[yan])

            def att_N(i):
                par = i % 2
                sfx = str(par)
                yr = ['ya%d_0' % par, 'ya%d_1' % par]
                S.op('dve', lambda e: e.scalar_tensor_tensor(out=yq[par][:], in0=ya[par][:], scalar=1.0, in1=ya[par][:], op0=ALU.mult, op1=ALU.mult, accum_out=ssa[par][:, 0:1]),
                     reads=yr, writes=['yq' + sfx, 'ssa' + sfx])
                S.op('act', lambda e: e.activation(out=ssa[par][:], in_=ssa[par][:], func=AF.Ln, scale=1.0 / 512, bias=eps_t[:, 0:1]),
                     reads=['ssa' + sfx, 'eps'], writes=['ssa' + sfx])
                S.op('act', lambda e: e.activation(out=ssa[par][:], in_=ssa[par][:], func=AF.Exp, scale=-0.5),
                     reads=['ssa' + sfx], writes=['ssa' + sfx])
                S.op('dve', lambda e: e.scalar_tensor_tensor(out=yq[par][:], in0=ya[par][:], scalar=ssa[par][:, 0:1], in1=atg[:], op0=ALU.mult, op1=ALU.mult),
                     reads=yr + ['ssa' + sfx, 'atg', 'yq' + sfx], writes=['yq' + sfx])
                S.op('pool', lambda e: e.tensor_tensor(out=yab[par][:], in0=yq[par][:], in1=sga[:, i, :], op=ALU.mult), reads=['yq' + sfx, 'sga'], writes=['yab' + sfx])

            def att_T(b, i):
                par = i % 2
                sfx = str(par)
                for g in range(4):
                    S.op('pe', lambda e: e.transpose(PT[:, g, :], yab[par][:, g * 128:(g + 1) * 128], ident[:]),
                         reads=['yab' + sfx, 'ident'], writes=['PT'], signal=(g == 3))
                yt = yts[par]
                ytn = 'yts' + sfx
                S.op('act', lambda e: e.copy(out=yt[:], in_=PT[:]), reads=['PT'], writes=[ytn])
                S.dma('pool', yT_s.ap()[b * NT + i, :, 4:8, :], yt[:], reads=[ytn])

            for b in range(NB):
                S.dma('sp', hT[:].rearrange('p k n -> p (k n)'), hT_s.ap()[b], writes=['hT'])
                for i in range(NT + 1):
                    if i < NT:
                        proj_pe(i)
                        proj_ew(b, i)
                    if i >= 1:
                        proj_tr(i - 1)
                jobs = s_jobs()
                AHEAD = 4
                for k in range(min(AHEAD, len(jobs))):
                    att_S1(k, jobs[k])
                deferred = []
                for k, job in enumerate(jobs):
                    att_E1(k, job)
                    if k + AHEAD < len(jobs):
                        att_S1(k + AHEAD, jobs[k + AHEAD])
                    for fn_d in deferred:
                        fn_d()
                    deferred = []
                    u, ci, c, last = job
                    if last:
                        po, pon = att_PV(u)
                        att_D(u, po, pon)
                        if u % 2 == 1:
                            deferred.append(lambda i_=u // 2: att_N(i_))
                            if u // 2 >= 1:
                                deferred.append(lambda i_=u // 2 - 1: att_T(b, i_))
                for fn_d in deferred:
                    fn_d()
                att_T(b, NT - 1)
            S.barrier()

        with contextlib.ExitStack() as P:
            ke = sb(P, 'ke', [128, 16, 1024], BF16)
            ko = sb(P, 'ko', [128, 16, 1024], BF16)
            with contextlib.ExitStack() as P0:
                zT = sb(P0, 'zT', [33, L], F32)
                w1 = sb(P0, 'w1', [33, 64], F32)
                w2 = sb(P0, 'w2', [64, 64], F32)
                w3 = sb(P0, 'w3', [64, 64], F32)
                w4 = sb(P0, 'w4', [64, 2048], F32)
                fcol = sb(P0, 'fcol', [64, 4], F32)
                fsc = sb(P0, 'fsc', [64, 4], F32)
                hb = sb(P0, 'hb', [1, 1024], F32)
                hA = sb(P0, 'hA', [64, L], F32)
                hB = sb(P0, 'hB', [64, L], F32)
                s1 = sb(P0, 's1', [64, 512], F32)
                s2 = sb(P0, 's2', [64, 512], F32)
                dct = [sb(P0, 'dct%d' % i, [128, 512], F32) for i in range(2)]
                kf = [sb(P0, 'kf%d' % i, [128, 512], F32) for i in range(2)]
                kb = [sb(P0, 'kb%d' % i, [128, 512], F32) for i in range(2)]
                psf = [ps(P0, 'psf%d' % i, [64, 512]) for i in range(2)]
                psk = [ps(P0, 'psk%d' % i, [128, 512]) for i in range(4)]
                S.dma('sp', zT[:], zT_d.ap(), writes=['zT'])
                S.dma('sp', w1[:], fw1_d.ap(), writes=['w1'])
                S.dma('sp', w2[:], fw2_d.ap(), writes=['w2'])
                S.dma('sp', w3[:], fw3_d.ap(), writes=['w3'])
                S.dma('sp', w4[:], fw4_d.ap(), writes=['w4'])
                S.dma('sp', fcol[:], fcol_d.ap(), writes=['fcol'])
                S.dma('sp', hb[:], hb_d.ap(), writes=['hb'])
                S.op('dve', lambda e: e.tensor_scalar(out=fsc[:, 0:1], in0=fcol[:, 3:4], scalar1=1.0 / 3.0, scalar2=None, op0=ALU.mult),
                     reads=['fcol'], writes=['fsc'])
                S.op('dve', lambda e: e.tensor_scalar(out=fsc[:, 1:4], in0=fcol[:, 0:3], scalar1=fsc[:, 0:1], scalar2=None, op0=ALU.mult),
                     reads=['fcol', 'fsc'], writes=['fsc'])
                layers = [(w1, 'w1', zT, 'zT', 33, hA, 'hA'), (w2, 'w2', hA, 'hA', 64, hB, 'hB'), (w3, 'w3', hB, 'hB', 64, hA, 'hA')]
                for li, (wt, wn, src, sn, kk, dst, dn) in enumerate(layers):
                    for ct in range(4):
                        pf = psf[ct % 2]
                        pfn = 'psf%d' % (ct % 2)
                        cs = slice(ct * 512, (ct + 1) * 512)
                        S.op('pe', lambda e: e.matmul(pf[:], lhsT=wt[0:kk, :], rhs=src[0:kk, cs], start=True, stop=True), reads=[wn, sn], writes=[pfn])
                        S.op('act', lambda e: e.activation(out=s1[:], in_=pf[:], func=AF.Sin, scale=fsc[:, 0:1], bias=fsc[:, li + 1:li + 2]),
                             reads=[pfn, 'fsc'], writes=['s1'])
                        S.op('dve', lambda e: e.tensor_tensor(out=s2[:], in0=s1[:], in1=s1[:], op=ALU.mult), reads=['s1'], writes=['s2'])
                        S.op('dve', lambda e: e.tensor_scalar(out=s2[:], in0=s2[:], scalar1=-4.0, scalar2=3.0, op0=ALU.mult, op1=ALU.add), reads=['s2'], writes=['s2'])
                        S.op('dve', lambda e: e.tensor_tensor(out=dst[:, cs], in0=s2[:], in1=s1[:], op=ALU.mult), reads=['s1', 's2', sn], writes=[dn])
                h3 = hA
                w4v = w4[:].rearrange('p (o r c) -> p o r c', o=2, r=2)
                nk = 0
                for mc in range(16):
                    dc = dct[mc % 2]
                    dcn = 'dct%d' % (mc % 2)
                    S.dma('sp', dc[:], dec_d.ap()[mc * 128:(mc + 1) * 128, :], writes=[dcn])
                    for o in range(2):
                        pkf = psk[o * 2]
                        pkb = psk[o * 2 + 1]
                        f_ = kf[nk % 2]
                        b_ = kb[nk % 2]
                        fn_ = 'kf%d' % (nk % 2)
                        bn_ = 'kb%d' % (nk % 2)
                        nk += 1
                        S.op('pe', lambda e: e.matmul(pkf[:], lhsT=h3[:, mc * 128:(mc + 1) * 128], rhs=w4v[:, o, 0, :], start=True, stop=True), reads=['hA', 'w4'], writes=['psk%d' % (o * 2)])
                        S.op('pe', lambda e: e.matmul(pkb[:], lhsT=h3[:, mc * 128:(mc + 1) * 128], rhs=w4v[:, o, 1, :], start=True, stop=True), reads=['hA', 'w4'], writes=['psk%d' % (o * 2 + 1)])
                        S.op('dve', lambda e: e.tensor_tensor(out=f_[:], in0=pkf[:], in1=dc[:], op=ALU.mult), reads=['psk%d' % (o * 2), dcn], writes=[fn_])
                        S.op('dve', lambda e: e.tensor_tensor(out=b_[:], in0=pkb[:], in1=dc[:], op=ALU.mult), reads=['psk%d' % (o * 2 + 1), dcn], writes=[bn_])
                        if mc == 0:
                            S.op('dve', lambda e: e.memset(b_[0:1, :], 0.0), reads=[bn_], writes=[bn_])
                            S.op('dve', lambda e: e.tensor_tensor(out=f_[0:1, :], in0=f_[0:1, :], in1=hb[0:1, o * 512:(o + 1) * 512], op=ALU.add), reads=[fn_, 'hb'], writes=[fn_])
                        S.op('pool', lambda e: e.tensor_tensor(out=ke[:, mc, o * 512:(o + 1) * 512], in0=f_[:], in1=b_[:], op=ALU.add), reads=[fn_, bn_], writes=['ke'])
                        S.op('pool', lambda e: e.tensor_tensor(out=ko[:, mc, o * 512:(o + 1) * 512], in0=f_[:], in1=b_[:], op=ALU.subtract), reads=[fn_, bn_], writes=['ko'])
                S.barrier()
            fwt = [sb(P, 'fwk%d' % i, [128, 16, 128], BF16) for i in range(2)]
            kst = [sb(P, 'kst%d' % i, [128, 1024], BF16) for i in range(2)]
            psK = [ps(P, 'psK%d' % i, [128, 512]) for i in range(4)]
            psN = ps(P, 'psN', [1, 1024])
            for gt in range(32):
                fw = fwt[gt % 2]
                fn_ = 'fwk%d' % (gt % 2)
                ks = kst[gt % 2]
                ksn = 'kst%d' % (gt % 2)
                src, srcn = (ke, 'ke') if gt < 16 else (ko, 'ko')
                S.dma('sp', fw[:].rearrange('p m g -> p (m g)'), Fw_d.ap()[gt, :, 0:2048], writes=[fn_])
                for o in range(2):
                    pk = psK[(gt % 2) * 2 + o]
                    pkn = 'psK%d' % ((gt % 2) * 2 + o)
                    for mc in range(16):
                        S.op('pe', lambda e: e.matmul(pk[:], lhsT=fw[:, mc, :], rhs=src[:, mc, o * 512:(o + 1) * 512], start=(mc == 0), stop=(mc == 15)),
                             reads=[fn_, srcn], writes=[pkn], signal=(mc == 15))
                    if o == 0:
                        S.op('act', lambda e: e.copy(out=ks[:, 0:512], in_=pk[:]), reads=[pkn], writes=[ksn])
                    else:
                        S.op('dve', lambda e: e.tensor_copy(out=ks[:, 512:1024], in_=pk[:]), reads=[pkn], writes=[ksn])
                if gt == 16:
                    for o in range(2):
                        for mc in range(16):
                            S.op('pe', lambda e: e.matmul(psN[0:1, o * 512:(o + 1) * 512], lhsT=fw[:, mc, 0:1], rhs=ke[:, mc, o * 512:(o + 1) * 512], start=(mc == 0), stop=(mc == 15)),
                                 reads=[fn_, 'ke'], writes=['psN'], signal=(mc == 15))
                    S.op('act', lambda e: e.copy(out=ks[0:1, :], in_=psN[0:1, :]), reads=['psN', ksn], writes=[ksn])
                S.dma('pool', Kh_s.ap()[gt % 16, :, gt // 16, :], ks[:], reads=[ksn])
            S.barrier()

        with contextlib.ExitStack() as P:
            uv = sb(P, 'uv', [128, NT, 512], BF16)
            x1 = sb(P, 'x1', [128, NT, 512], BF16)
            x2 = sb(P, 'x2', [128, NT, 512], BF16)
            x1r = sb(P, 'x1r', [128, 8, 512], BF16)
            x2r = sb(P, 'x2r', [128, 8, 512], BF16)
            vp = sb(P, 'vp', [128, 8, 512], BF16)
            vm = sb(P, 'vm', [128, 8, 512], BF16)
            Yh = sb(P, 'Yh', [128, 32, 512], BF16)
            hyg = sb(P, 'hyg', [128, 512], F32)
            Jm = sb(P, 'Jm', [128, 128], BF16)
            fwt = [sb(P, 'fwt%d' % i, [128, 2, 8, 128], BF16) for i in range(3)]
            bwt = [sb(P, 'bwt%d' % i, [128, 32, 128], BF16) for i in range(3)]
            kt = [sb(P, 'kt%d' % i, [128, 2, 512], BF16) for i in range(3)]
            ta = sb(P, 'ta', [128, 512], F32)
            tb_ = sb(P, 'tb', [128, 512], F32)
            tc = sb(P, 'tc', [128, 512], F32)
            td = sb(P, 'td', [128, 512], F32)
            Ac = [sb(P, 'Ac%d' % i, [128, 512], F32) for i in range(2)]
            dd = [sb(P, 'dd%d' % i, [128, 512], F32) for i in range(2)]
            ss = [sb(P, 'ss%d' % i, [128, 512], F32) for i in range(2)]
            yq2 = [sb(P, 'yq2_%d' % i, [128, 512], F32) for i in range(2)]
            ssh2 = [sb(P, 'ssh2_%d' % i, [128, 1], F32) for i in range(2)]
            yhb2 = [sb(P, 'yhb2_%d' % i, [128, 512], BF16) for i in range(2)]
            sgt = [sb(P, 'sgt%d' % i, [128, 512], BF16) for i in range(2)]
            yts = [sb(P, 'yth%d' % i, [128, 4, 128], BF16) for i in range(2)]
            Q = [ps(P, 'Q%d' % i, [128, 512]) for i in range(4)]
            Rb = ps(P, 'Rb', [128, 512])
            PT4 = [ps(P, 'PT4_%d' % i, [128, 4, 128]) for i in range(2)]
            S.dma('sp', hyg[:], bc(hyg_d, 512), writes=['hyg'])
            S.dma('sp', Jm[:], jmat_d.ap(), writes=['Jm'])

            def emit_T4(b, tok0, hl):
                sfx = str(hl)
                mv = ident if hl == 0 else Jm
                mvn = 'ident' if hl == 0 else 'Jm'
                for g in range(4):
                    S.op('pe', lambda e: e.matmul(PT4[hl][:, g, :], lhsT=yhb2[hl][:, g * 128:(g + 1) * 128], rhs=mv[:], start=True, stop=True),
                         reads=['yhb' + sfx, mvn], writes=['PT4' + sfx], signal=(g == 3))
                yt = yts[hl]
                ytn = 'yth' + sfx
                S.op('act', lambda e: e.copy(out=yt[:], in_=PT4[hl][:]), reads=['PT4' + sfx], writes=[ytn])
                S.dma('pool', yT_s.ap()[tok0 // 128, :, 0:4, :], yt[:], reads=[ytn])

            nf = 0
            nb_ = 0
            nq = 0
            def load_in(b, which):
                t_, tn = [(uv, 'uv'), (x1, 'x1'), (x2, 'x2')][which]
                S.dma('sp', t_[:], U_s.ap()[which, b * L:(b + 1) * L, :].rearrange('(i p) c -> p i c', p=128), writes=[tn])

            def prep(which):
                t_, tn = [(uv, 'uv'), (x1, 'x1'), (x2, 'x2')][which]
                for a_ in range(8):
                    c = 7 - a_
                    qb_ = Q[nqc[0] % 4]
                    qn_ = 'Q%d' % (nqc[0] % 4)
                    nqc[0] += 1
                    S.op('pe', lambda e: e.matmul(qb_[:], lhsT=Jm[:], rhs=t_[:, c, :], start=True, stop=True), reads=['Jm', tn], writes=[qn_])
                    if which == 0:
                        S.op('dve', lambda e: e.tensor_tensor(out=vp[:, a_, :], in0=qb_[:], in1=uv[:, 8 + a_, :], op=ALU.add), reads=[qn_, 'uv'], writes=['vp%d' % a_])
                        S.op('dve', lambda e: e.tensor_tensor(out=vm[:, a_, :], in0=uv[:, 8 + a_, :], in1=qb_[:], op=ALU.subtract), reads=[qn_, 'uv'], writes=['vm%d' % a_])
                    elif which == 1:
                        S.op('act', lambda e: e.copy(out=x1r[:, a_, :], in_=qb_[:]), reads=[qn_], writes=['x1r'])
                    else:
                        S.op('act', lambda e: e.copy(out=x2r[:, a_, :], in_=qb_[:]), reads=[qn_], writes=['x2r'])

            nqc = [0]
            for w_ in range(3):
                load_in(0, w_)
            for b in range(NB):
                prep(0)
                prep(1)
                for o in range(2):
                    for ft in range(16):
                        fw = fwt[nf % 3]
                        fn_ = 'fwt%d' % (nf % 3)
                        k_ = kt[nf % 3]
                        kn = 'kt%d' % (nf % 3)
                        pR = Q[(nf % 2) * 2]
                        pRn = 'Q%d' % ((nf % 2) * 2)
                        pI = Q[(nf % 2) * 2 + 1]
                        pIn = 'Q%d' % ((nf % 2) * 2 + 1)
                        nf += 1
                        S.dma('sp', fw[:].rearrange('p r m g -> p (r m g)'), Fh_d.ap()[ft], writes=[fn_])
                        S.dma('sp', k_[:], Kh_s.ap()[ft, :, :, o * 512:(o + 1) * 512], writes=[kn])
                        for mc in range(8):
                            S.op('pe', lambda e: e.matmul(pR[:], lhsT=fw[:, 0, mc, :], rhs=vp[:, mc, :], start=(mc == 0), stop=(mc == 7)),
                                 reads=[fn_, 'vp%d' % mc], writes=[pRn], signal=(mc == 7))
                        for mc in range(8):
                            S.op('pe', lambda e: e.matmul(pI[:], lhsT=fw[:, 1, mc, :], rhs=vm[:, mc, :], start=(mc == 0), stop=(mc == 7)),
                                 reads=[fn_, 'vm%d' % mc], writes=[pIn], signal=(mc == 7))
                        S.op('dve', lambda e: e.tensor_tensor(out=ta[:], in0=pR[:], in1=k_[:, 0, :], op=ALU.mult), reads=[pRn, kn], writes=['ta'])
                        S.op('dve', lambda e: e.tensor_tensor(out=tb_[:], in0=pI[:], in1=k_[:, 1, :], op=ALU.mult), reads=[pIn, kn], writes=['tb'])
                        S.op('pool', lambda e: e.tensor_tensor(out=Yh[:, ft, :], in0=ta[:], in1=tb_[:], op=ALU.subtract), reads=['ta', 'tb'], writes=['Yh%d' % ft])
                        S.op('dve', lambda e: e.tensor_tensor(out=tc[:], in0=pR[:], in1=k_[:, 1, :], op=ALU.mult), reads=[pRn, kn], writes=['tc'])
                        S.op('dve', lambda e: e.tensor_tensor(out=td[:], in0=pI[:], in1=k_[:, 0, :], op=ALU.mult), reads=[pIn, kn], writes=['td'])
                        S.op('pool', lambda e: e.tensor_tensor(out=Yh[:, 16 + ft, :], in0=tc[:], in1=td[:], op=ALU.add), reads=['tc', 'td'], writes=['Yh%d' % (16 + ft)])
                        if ft == 0:
                            S.op('pool', lambda e: e.tensor_copy(out=Yh[0:1, 0, :], in_=ta[0:1, :]), reads=['ta', 'Yh0'], writes=['Yh0'])
                            S.op('pool', lambda e: e.tensor_copy(out=Yh[0:1, 16, :], in_=tb_[0:1, :]), reads=['tb', 'Yh16'], writes=['Yh16'])
                    if o == 0:
                        prep(2)
                        if b + 1 < NB:
                            load_in(b + 1, 0)
                    pend = None
                    for jt in range(8):
                        bw = bwt[nb_ % 3]
                        bn = 'bwt%d' % (nb_ % 3)
                        par = nb_ % 2
                        sfx = str(par)
                        pA = Q[par * 2]
                        pAn = 'Q%d' % (par * 2)
                        pB = Q[par * 2 + 1]
                        pBn = 'Q%d' % (par * 2 + 1)
                        nb_ += 1
                        S.dma('sp', bw[:].rearrange('p g t -> p (g t)'), Bh_d.ap()[jt], writes=[bn])
                        for gt in range(16):
                            S.op('pe', lambda e: e.matmul(pA[:], lhsT=bw[:, gt, :], rhs=Yh[:, gt, :], start=(gt == 0), stop=(gt == 15)),
                                 reads=[bn, 'Yh%d' % gt], writes=[pAn], signal=(gt == 15))
                        for gt in range(16, 32):
                            S.op('pe', lambda e: e.matmul(pB[:], lhsT=bw[:, gt, :], rhs=Yh[:, gt, :], start=(gt == 16), stop=(gt == 31)),
                                 reads=[bn, 'Yh%d' % gt], writes=[pBn], signal=(gt == 31))
                        S.op('act', lambda e: e.copy(out=Ac[par][:], in_=pA[:]), reads=[pAn], writes=['Ac' + sfx])
                        S.op('dve', lambda e: e.tensor_tensor(out=dd[par][:], in0=Ac[par][:], in1=pB[:], op=ALU.subtract), reads=['Ac' + sfx, pBn], writes=['dd' + sfx])
                        S.op('dve', lambda e: e.tensor_tensor(out=ss[par][:], in0=Ac[par][:], in1=pB[:], op=ALU.add), reads=['Ac' + sfx, pBn], writes=['ss' + sfx])
                        if o == 0:
                            S.op('pool', lambda e: e.tensor_tensor(out=dd[par][:], in0=dd[par][:], in1=x1[:, 8 + jt, :], op=ALU.mult), reads=['dd' + sfx, 'x1'], writes=['dd' + sfx])
                            S.op('pool', lambda e: e.tensor_tensor(out=ss[par][:], in0=ss[par][:], in1=x1r[:, jt, :], op=ALU.mult), reads=['ss' + sfx, 'x1r'], writes=['ss' + sfx])
                            S.op('dve', lambda e: e.tensor_tensor(out=vp[:, jt, :], in0=dd[par][:], in1=ss[par][:], op=ALU.add), reads=['dd' + sfx, 'ss' + sfx], writes=['vp%d' % jt])
                            S.op('dve', lambda e: e.tensor_tensor(out=vm[:, jt, :], in0=dd[par][:], in1=ss[par][:], op=ALU.subtract), reads=['dd' + sfx, 'ss' + sfx], writes=['vm%d' % jt])
                        else:
                            if pend is not None:
                                for args in pend:
                                    emit_T4(*args)
                            pend = []
                            for hl in range(2):
                                hs = str(hl)
                                ysrc, ysn = (dd[par], 'dd' + sfx) if hl == 0 else (ss[par], 'ss' + sfx)
                                ck = 8 + jt if hl == 0 else 7 - jt
                                tok0 = b * L + ck * 128
                                xg, xgn = (x2[:, 8 + jt, :], 'x2') if hl == 0 else (x2r[:, jt, :], 'x2r')
                                sg = sgt[hl]
                                sgn = 'sgt' + hs
                                S.dma('sp', sg[:], SG_s.ap()[tok0: tok0 + 128, :], writes=[sgn])
                                S.op('pool', lambda e: e.tensor_tensor(out=ysrc[:], in0=ysrc[:], in1=xg, op=ALU.mult), reads=[ysn, xgn], writes=[ysn])
                                S.op('dve', lambda e: e.scalar_tensor_tensor(out=yq2[hl][:], in0=ysrc[:], scalar=1.0, in1=ysrc[:], op0=ALU.mult, op1=ALU.mult, accum_out=ssh2[hl][:, 0:1]),
                                     reads=[ysn], writes=['yq4' + hs, 'ssh' + hs])
                                S.op('act', lambda e: e.activation(out=ssh2[hl][:], in_=ssh2[hl][:], func=AF.Sqrt, scale=1.0 / 512, bias=eps_t[:, 0:1]),
                                     reads=['ssh' + hs, 'eps'], writes=['ssh' + hs])
                                S.op('dve', lambda e: e.reciprocal(out=ssh2[hl][:], in_=ssh2[hl][:]), reads=['ssh' + hs], writes=['ssh' + hs])
                                S.op('dve', lambda e: e.scalar_tensor_tensor(out=yq2[hl][:], in0=ysrc[:], scalar=ssh2[hl][:, 0:1], in1=hyg[:], op0=ALU.mult, op1=ALU.mult),
                                     reads=[ysn, 'ssh' + hs, 'hyg', 'yq4' + hs], writes=['yq4' + hs])
                                if hl == 0:
                                    S.op('pool', lambda e: e.tensor_tensor(out=yhb2[hl][:], in0=yq2[hl][:], in1=sg[:], op=ALU.mult), reads=['yq4' + hs, sgn], writes=['yhb' + hs])
                                else:
                                    S.op('pe', lambda e: e.matmul(Rb[:], lhsT=Jm[:], rhs=sg[:], start=True, stop=True), reads=['Jm', sgn], writes=['Rb'])
                                    S.op('dve', lambda e: e.tensor_tensor(out=yhb2[hl][:], in0=yq2[hl][:], in1=Rb[:], op=ALU.mult), reads=['yq4' + hs, 'Rb'], writes=['yhb' + hs])
                                pend.append((b, tok0, hl))
                    if pend is not None:
                        for args in pend:
                            emit_T4(*args)
                    if b + 1 < NB:
                        load_in(b + 1, 1 if o == 0 else 2)
            S.barrier()

        with contextlib.ExitStack() as P:
            Wo = sb(P, 'Wo', [128, 8, D], BF16)
            wst = [sb(P, 'wso%d' % i, [128, D], F32) for i in range(2)]
            yt = [sb(P, 'yt%d' % i, [128, 8, 128], BF16) for i in range(3)]
            xr = [sb(P, 'xr%d' % i, [128, D], F32) for i in range(3)]
            ot = [sb(P, 'ot%d' % i, [128, D], F32) for i in range(2)]
            psO = [ps(P, 'psW%d' % i, [128, 512]) for i in range(4)]
            for kc in range(8):
                wb = wst[kc % 2]
                wn = 'wso%d' % (kc % 2)
                S.dma('sp', wb[:], wout_d.ap()[kc * 128:(kc + 1) * 128, :], writes=[wn])
                if kc % 2 == 0:
                    S.op('act', lambda e: e.copy(out=Wo[:, kc, :], in_=wb[:]), reads=[wn], writes=['Wo'])
                else:
                    S.op('dve', lambda e: e.tensor_copy(out=Wo[:, kc, :], in_=wb[:]), reads=[wn], writes=['Wo'])
            for c in range(NB * NT):
                y_ = yt[c % 3]
                yn = 'yt%d' % (c % 3)
                x_ = xr[c % 3]
                xn = 'xr%d' % (c % 3)
                o_ = ot[c % 2]
                on = 'ot%d' % (c % 2)
                S.dma('sp', y_[:], yT_s.ap()[c], writes=[yn])
                S.dma('sp', x_[:], x_d.ap()[c * 128:(c + 1) * 128, :], writes=[xn])
                for hf in range(2):
                    po = psO[(c % 2) * 2 + hf]
                    pon = 'psW%d' % ((c % 2) * 2 + hf)
                    for fc in range(8):
                        S.op('pe', lambda e: e.matmul(po[:], lhsT=y_[:, fc, :], rhs=Wo[:, fc, hf * 512:(hf + 1) * 512], start=(fc == 0), stop=(fc == 7)),
                             reads=[yn, 'Wo'], writes=[pon], signal=(fc == 7))
                    S.op('dve', lambda e: e.tensor_tensor(out=o_[:, hf * 512:(hf + 1) * 512], in0=po[:], in1=x_[:, hf * 512:(hf + 1) * 512], op=ALU.add),
                         reads=[pon, xn], writes=[on])
                S.dma('pool', out_d.ap()[c * 128:(c + 1) * 128, :], o_[:], reads=[on])
            S.finish('sp')
            S.finish('pool')
    return nc


_NC = None


def kernel(x, norm_g, w_in, conv_w, conv_b, filt_w1, filt_b1, filt_w2, filt_b2, filt_w3, filt_b3,
           filt_w4, filt_sin_freq, hyena_bias, q_norm_g, k_norm_g, attn_sink, hy_out_norm_g,
           attn_out_norm_g, w_out):
    global _NC
    f32 = lambda a: np.ascontiguousarray(np.asarray(a, dtype=np.float32))
    x = f32(x)
    C = _consts()
    shared = dict(
        w_in=f32(w_in)[0], w_out=f32(w_out)[0],
        gcol=np.ascontiguousarray(f32(norm_g)[0].reshape(8, 128).T),
        cwc=np.ascontiguousarray(np.concatenate([f32(conv_w)[0], f32(conv_b)], axis=0).reshape(4, 12, 128).transpose(2, 1, 0)).reshape(128, 48),
        filt_w1=f32(filt_w1)[0], filt_w2=f32(filt_w2)[0], filt_w3=f32(filt_w3)[0], filt_w4=f32(filt_w4)[0],
        fcols=np.ascontiguousarray(np.stack([f32(filt_b1)[0], f32(filt_b2)[0], f32(filt_b3)[0], f32(filt_sin_freq)[0]], axis=1)),
        hyena_bias=f32(hyena_bias)[0].reshape(1, 1024),
        q_norm_g=f32(q_norm_g)[0].reshape(1, 64), k_norm_g=f32(k_norm_g)[0].reshape(1, 64),
        attn_sink=f32(attn_sink)[0].reshape(1, 8),
        hy_out_norm_g=f32(hy_out_norm_g)[0].reshape(1, 512), attn_out_norm_g=f32(attn_out_norm_g)[0].reshape(1, 512),
        zT=C['zT'], decay=C['decay'],
        Fw_t=C['Fw_t'].reshape(32, 128, 32 * 128), Fh_t=C['Fh_t'].reshape(16, 128, 2 * 8 * 128), Bh_t=C['Bh_t'].reshape(8, 128, 32 * 128), jmat=C['jmat'],
        rope_cs=C['rope_cs'].reshape(128, 2 * 16 * 32), ident=C['ident'], masks=C['masks'].reshape(128, 256),
    )
    in_maps = []
    for c in range(NCORES):
        xc = x[c * NB:(c + 1) * NB].reshape(NB * L, D)
        m = dict(shared)
        m['x'] = np.ascontiguousarray(xc)
        m['xT'] = np.ascontiguousarray(xc.T)
        in_maps.append(m)
    if _NC is None:
        _NC = build_nc()
    res = run_bass_kernel_spmd(_NC, in_maps, core_ids=list(range(NCORES)))
    kernel.last_results = res
    out = np.concatenate([r['out'].reshape(NB, L, D) for r in res.results], axis=0)
    return out.astype(np.float32)
```

```python
import contextlib
import math
import numpy as np
import ml_dtypes
import concourse.bass as bass
import concourse.mybir as mybir
from concourse.bass_utils import run_bass_kernel_spmd

F32 = mybir.dt.float32
BF16 = mybir.dt.bfloat16
AF = mybir.ActivationFunctionType
ALU = mybir.AluOpType
AX = mybir.AxisListType

NCORES = 8
NB = 4
L = 2048
NT = 16
D = 1024
NFFT = 4096
EPS = 1e-6
DEBUG = False


class Sched:
    ENG = ('pe', 'act', 'dve', 'pool', 'sp')
    NRING = 8

    def __init__(self, nc, stack):
        self.nc = nc
        self.e = {'pe': nc.tensor, 'act': nc.scalar, 'dve': nc.vector,
                  'pool': nc.gpsimd, 'sp': nc.sync}
        self.sem = {}
        for k in self.ENG:
            self.sem[k] = stack.enter_context(nc.semaphore('s_' + k))
        self.cnt = {k: 0 for k in self.ENG}
        self.dq = {}
        for q in ('sp', 'pool', 'act'):
            for i in range(self.NRING):
                self.sem[('d', q, i)] = stack.enter_context(nc.semaphore('d_%s_%d' % (q, i)))
            self.dq[q] = 0
        self.seen = {k: {} for k in self.ENG}
        self.last_w = {}
        self.readers = {}
        self.all_dma = []

    def _wait(self, eng, key, val):
        if self.seen[eng].get(key, 0) >= val:
            return
        self.e[eng].wait_ge(self.sem[key], val)
        self.seen[eng][key] = val

    def _deps(self, eng, reads, writes):
        deps = {}

        def add(t, same_ok):
            if t is None:
                return
            key, val = t
            if key == eng and eng == 'pe':
                return
            if deps.get(key, 0) < val:
                deps[key] = val
        for b in reads:
            add(self.last_w.get(b), True)
        for b in writes:
            add(self.last_w.get(b), True)
            for t in self.readers.get(b, ()):
                add(t, False)
        for key, val in deps.items():
            self._wait(eng, key, val)

    def _record(self, ticket, reads, writes):
        for b in reads:
            self.readers.setdefault(b, []).append(ticket)
        for b in writes:
            self.last_w[b] = ticket
            self.readers[b] = []

    def op(self, eng, fn, reads=(), writes=(), signal=True):
        self._deps(eng, reads, writes)
        inst = fn(self.e[eng])
        if signal:
            self.cnt[eng] += 1
            inst.then_inc(self.sem[eng], 1)
            ticket = (eng, self.cnt[eng])
        else:
            ticket = (eng, self.cnt[eng] + 1)
        self._record(ticket, reads, writes)
        return ticket

    def dma(self, q, out, in_, reads=(), writes=(), **kw):
        i = self.dq[q]
        slot = i % self.NRING
        key = ('d', q, slot)
        if i >= self.NRING:
            self._wait(q, key, 16 * (i // self.NRING))
        self._deps(q, reads, writes)
        self.e[q].dma_start(out=out, in_=in_, **kw).then_inc(self.sem[key], 16)
        self.dq[q] = i + 1
        ticket = (key, 16 * (i // self.NRING + 1))
        self._record(ticket, reads, writes)
        self.all_dma.append(ticket)
        return ticket

    def _last_dma(self):
        last = {}
        for key, val in self.all_dma:
            if last.get(key, 0) < val:
                last[key] = val
        return last

    def barrier(self):
        last = self._last_dma()
        for eng in self.ENG:
            for other in self.ENG:
                if other != eng and self.cnt[other] > 0:
                    self._wait(eng, other, self.cnt[other])
            for key, val in last.items():
                self._wait(eng, key, val)
        self.all_dma = []
        self.last_w = {}
        self.readers = {}

    def finish(self, eng='sp'):
        for key, val in self._last_dma().items():
            self._wait(eng, key, val)


_CONST = None


def _consts():
    global _CONST
    if _CONST is not None:
        return _CONST
    bf = ml_dtypes.bfloat16
    m = np.arange(NFFT)
    pos = np.where(m < L, m, NFFT - m)
    pos_c = np.minimum(pos, L - 1)
    t = np.linspace(0.0, 1.0, L, dtype=np.float32)[:, None]
    bands = 16
    f = np.linspace(1e-4, bands - 1, bands, dtype=np.float32)[None, :]
    w = (2.0 * math.pi * np.arange(L, dtype=np.float32)[:, None] / L).astype(np.float32)
    z = np.concatenate([t, np.cos(f * w), -np.sin(f * w)], axis=-1).astype(np.float32)
    max_decay = math.log(1e-2) / 0.3
    min_decay = math.log(1e-2) / 1.5
    deltas = np.linspace(min_decay, max_decay, 512, dtype=np.float32)
    decay = np.exp(-t * np.abs(deltas)[None, :]).astype(np.float32)
    tt = np.arange(NFFT, dtype=np.int64)[:, None]
    g = np.arange(NFFT, dtype=np.int64)[None, :]
    gg = g % L
    ang = 2.0 * np.pi * ((gg * tt) % NFFT).astype(np.float64) / NFFT
    Fw = np.where(g < L, np.cos(ang), -np.sin(ang))
    Fw[:, L] = np.cos(np.pi * (np.arange(NFFT) % 2))
    Fw_t = Fw.reshape(32, 128, 32, 128).transpose(2, 1, 0, 3)
    Fw_t = np.ascontiguousarray(Fw_t).astype(bf)
    jj = np.arange(1024, dtype=np.int64)[:, None]
    ff = np.arange(L, dtype=np.int64)[None, :]
    ah = 2.0 * np.pi * ((ff * (2 * jj + 1)) % (2 * NFFT)).astype(np.float64) / (2 * NFFT)
    sgn = np.where(np.arange(1024) % 2 == 0, 1.0, -1.0)
    FhRe = np.cos(ah)
    FhIm = -np.sin(ah)
    FhIm[:, 0] = -sgn
    Fh = np.concatenate([FhRe, FhIm], axis=1)
    Fh_t = np.ascontiguousarray(Fh.reshape(8, 128, 2, 16, 128).transpose(3, 1, 2, 0, 4)).astype(bf)
    BhRe = np.cos(ah).T * (2.0 / NFFT)
    BhRe[0, :] = 1.0 / NFFT
    BhIm = np.sin(ah).T * (2.0 / NFFT)
    BhIm[0, :] = sgn / NFFT
    Bh = np.concatenate([BhRe, BhIm], axis=0)
    Bh_t = np.ascontiguousarray(Bh.reshape(32, 128, 8, 128).transpose(2, 1, 0, 3)).astype(bf)
    jmat = np.ascontiguousarray(np.eye(128, dtype=np.float32)[::-1]).astype(bf)
    half = 32
    inv = (10000.0 ** (-np.arange(half, dtype=np.float32) / half)).astype(np.float32)
    ang = np.arange(L, dtype=np.float32)[:, None] * inv[None, :]
    cos = np.cos(ang).astype(np.float32).reshape(NT, 128, half).transpose(1, 0, 2)
    sin = np.sin(ang).astype(np.float32).reshape(NT, 128, half).transpose(1, 0, 2)
    rope_cs = np.ascontiguousarray(np.stack([cos, sin], axis=1))
    ident = np.eye(128, dtype=np.float32).astype(bf)
    s_ = np.arange(128)[:, None]
    q_ = np.arange(128)[None, :]
    maskL = (q_ <= s_).astype(np.float32).astype(bf)
    maskU = (s_ <= q_).astype(np.float32).astype(bf)
    masks = np.ascontiguousarray(np.stack([maskL, maskU], axis=1))
    _CONST = dict(zT=np.ascontiguousarray(z.T), decay=decay, Fw_t=Fw_t, Fh_t=Fh_t, Bh_t=Bh_t, jmat=jmat,
                  rope_cs=rope_cs, ident=ident, masks=masks)
    return _CONST


def build_nc():
    nc = bass.Bass('TRN2', target_bir_lowering=False)

    def din(name, shape, dt=F32):
        return nc.dram_tensor(name, list(shape), dt, kind='ExternalInput')

    def dscr(name, shape, dt):
        return nc.dram_tensor(name, list(shape), dt, kind='ExternalOutput' if DEBUG else 'Internal')

    xT_d = din('xT', [D, NB * L])
    x_d = din('x', [NB * L, D])
    win_d = din('w_in', [D, 3328])
    wout_d = din('w_out', [D, D])
    gcol_d = din('gcol', [128, 8])
    cwc_d = din('cwc', [128, 48])
    fw1_d = din('filt_w1', [33, 64])
    fw2_d = din('filt_w2', [64, 64])
    fw3_d = din('filt_w3', [64, 64])
    fw4_d = din('filt_w4', [64, 2048])
    fcol_d = din('fcols', [64, 4])
    hb_d = din('hyena_bias', [1, 1024])
    qg_d = din('q_norm_g', [1, 64])
    kg_d = din('k_norm_g', [1, 64])
    sink_d = din('attn_sink', [1, 8])
    hyg_d = din('hy_out_norm_g', [1, 512])
    atg_d = din('attn_out_norm_g', [1, 512])
    zT_d = din('zT', [33, L])
    dec_d = din('decay', [L, 512])
    Fw_d = din('Fw_t', [32, 128, 32 * 128], BF16)
    Fh_d = din('Fh_t', [16, 128, 2 * 8 * 128], BF16)
    Bh_d = din('Bh_t', [8, 128, 32 * 128], BF16)
    jmat_d = din('jmat', [128, 128], BF16)
    rope_d = din('rope_cs', [128, 2 * 16 * 32])
    ident_d = din('ident', [128, 128], BF16)
    masks_d = din('masks', [128, 256], BF16)
    out_d = nc.dram_tensor('out', [NB * L, D], F32, kind='ExternalOutput')

    hT_s = dscr('hT_s', [NB, 128, 8 * 2050], BF16)
    U_s = dscr('U_s', [3, NB * L, 512], BF16)
    SG_s = dscr('SG_s', [NB * L, 512], BF16)
    yT_s = dscr('yT_s', [NB * NT, 128, 8, 128], BF16)
    Kh_s = dscr('Kh_s', [16, 128, 2, 1024], BF16)

    def bc(handle, ncols, parts=128, off=0):
        return bass.AP(handle, off, [[0, parts], [1, ncols]])

    with contextlib.ExitStack() as G:
        S = Sched(nc, G)

        def sb(st, name, shape, dt):
            return st.enter_context(nc.sbuf_tensor('sb_' + name, list(shape), dt))

        def ps(st, name, shape, dt=F32):
            return st.enter_context(nc.psum_tensor('ps_' + name, list(shape), dt))

        ident = sb(G, 'ident', [128, 128], BF16)
        ones = sb(G, 'ones', [128, 128], BF16)
        S.dma('sp', ident[:], ident_d.ap(), writes=['ident'])
        S.op('dve', lambda e: e.memset(ones[:], 1.0), writes=['ones'])

        def compute_hT(st, b, hT, tagp):
            xs = [sb(st, tagp + 'xs%d' % i, [128, 8, 512], F32) for i in range(2)]
            sq = sb(st, tagp + 'sq', [128, 8, 512], BF16)
            rs = sb(st, tagp + 'rs', [128, 512], F32)
            psr = ps(st, tagp + 'psr', [128, 512])
            return xs, sq, rs, psr

        def emit_hT_stage(st_, b, tt, hT, bufs, hname):
            xs, sq, rs, psr = bufs
            xb = xs[tt % 2]
            xn = 'xs%d' % (tt % 2)
            if st_ == 0:
                src = xT_d.ap()[:, b * L + tt * 512: b * L + (tt + 1) * 512].rearrange('(kc p) n -> p kc n', p=128)
                S.dma('sp', xb[:], src, writes=[xn])
                S.op('act', lambda e: e.activation(out=sq[:], in_=xb[:], func=AF.Square), reads=[xn], writes=['sq'])
            elif st_ == 1:
                for kc in range(8):
                    S.op('pe', lambda e: e.matmul(psr[:], lhsT=ones[:], rhs=sq[:, kc, :], start=(kc == 0), stop=(kc == 7)),
                         reads=['sq', 'ones'], writes=['psr'], signal=(kc == 7))
                S.op('act', lambda e: e.activation(out=rs[:], in_=psr[:], func=AF.Sqrt, scale=1.0 / D, bias=eps_t[:, 0:1]),
                     reads=['psr', 'eps'], writes=['rs'])
                S.op('dve', lambda e: e.reciprocal(out=rs[:], in_=rs[:]), reads=['rs'], writes=['rs'])
            else:
                for kc in range(8):
                    eng = 'dve' if kc % 2 == 0 else 'pool'
                    S.op(eng, lambda e: e.tensor_tensor(out=hT[:, kc, 1 + tt * 512: 1 + (tt + 1) * 512], in0=xb[:, kc, :], in1=rs[:], op=ALU.mult),
                         reads=[xn, 'rs'], writes=['%s%d' % (hname, kc)])

        def emit_hT_tile(b, tt, hT, bufs, hname):
            for st_ in range(3):
                emit_hT_stage(st_, b, tt, hT, bufs, hname)

        def emit_hT(b, hT, bufs, hname):
            for tt in range(4):
                emit_hT_tile(b, tt, hT, bufs, hname)

        eps_t = sb(G, 'eps_t', [128, 1], F32)
        S.op('dve', lambda e: e.memset(eps_t[:], EPS), writes=['eps'])

        with contextlib.ExitStack() as P:
            Wh = sb(P, 'Wh', [128, 8, 1536], BF16)
            hTs = [sb(P, 'hT_%d' % i, [128, 8, 2050], BF16) for i in range(2)]
            cwc = sb(P, 'cwc', [128, 12, 4], F32)
            with contextlib.ExitStack() as P0:
                gcol = sb(P0, 'gcol', [128, 8], F32)
                wst = [sb(P0, 'wst%d' % i, [128, 1536], F32) for i in range(2)]
                S.dma('sp', gcol[:], gcol_d.ap(), writes=['gcol'])
                S.dma('sp', cwc[:].rearrange('p c j -> p (c j)'), cwc_d.ap(), writes=['cwc'])
                for kc in range(8):
                    wb = wst[kc % 2]
                    wn = 'wst%d' % (kc % 2)
                    S.dma('sp', wb[:], win_d.ap()[kc * 128:(kc + 1) * 128, 0:1536], writes=[wn])
                    if kc % 2 == 0:
                        S.op('act', lambda e: e.activation(out=Wh[:, kc, :], in_=wb[:], func=AF.Copy, scale=gcol[:, kc:kc + 1]), reads=[wn, 'gcol'], writes=['Wh'])
                    else:
                        S.op('dve', lambda e: e.tensor_scalar(out=Wh[:, kc, :], in0=wb[:], scalar1=gcol[:, kc:kc + 1], scalar2=None, op0=ALU.mult), reads=[wn, 'gcol'], writes=['Wh'])
                S.barrier()
            bufs = compute_hT(P, 0, None, 'p1')
            pf = [sb(P, 'pf%d' % i, [128, 2050], F32) for i in range(2)]
            t1 = [sb(P, 't1_0', [128, 2048], F32)] * 2
            ucb = [sb(P, 'ucb%d' % i, [128, 2048], BF16) for i in range(8)]
            ust = [sb(P, 'ust%d' % i, [128, 512], BF16) for i in range(3)]
            psp = [ps(P, 'psp%d' % i, [128, 512]) for i in range(4)]
            pst = [ps(P, 'pst%d' % i, [128, 4, 128], BF16) for i in range(3)]
            HTN = [['hT%s%d' % ('ab'[q], k) for k in range(8)] for q in range(2)]
            for q in range(2):
                S.op('pool', lambda e: e.memset(hTs[q][:, :, 0:1], 0.0), writes=HTN[q])
                S.op('pool', lambda e: e.memset(hTs[q][:, :, 2049:2050], 0.0), writes=HTN[q])
            for i in range(2):
                S.op('pool', lambda e: e.memset(pf[i][:, 0:1], 0.0), writes=['pf%d' % i])
                S.op('pool', lambda e: e.memset(pf[i][:, 2049:2050], 0.0), writes=['pf%d' % i])
            cnt = {'pf': 0, 'pp': 0, 'pt': 0, 'us': 0}

            def group_mm(b, gi, n, slots):
                hT = hTs[b % 2]
                hq = 'ab'[b % 2]
                for c4 in range(4):
                    ct = gi * 4 + c4
                    k = cnt['pf'] % 2
                    cnt['pf'] += 1
                    pfb, pfn = pf[k], 'pf%d' % k
                    t1b, t1n = t1[0], 't1_0'
                    ub = ucb[(n % 2) * 4 + c4]
                    ubn = 'ucb%d' % ((n % 2) * 4 + c4)
                    for tt in range(4):
                        kp = cnt['pp'] % 4
                        cnt['pp'] += 1
                        pp, ppn = psp[kp], 'psp%d' % kp
                        for kc in range(8):
                            S.op('pe', lambda e: e.matmul(pp[:], lhsT=Wh[:, kc, ct * 128:(ct + 1) * 128], rhs=hT[:, kc, 1 + tt * 512: 1 + (tt + 1) * 512], start=(kc == 0), stop=(kc == 7)),
                                 reads=['hT%s%d' % (hq, kc), 'Wh'], writes=[ppn], signal=(kc == 7))
                        S.op('act', lambda e: e.copy(out=pfb[:, 1 + tt * 512: 1 + (tt + 1) * 512], in_=pp[:]), reads=[ppn], writes=[pfn])
                    S.op('act', lambda e: e.activation(out=t1b[:], in_=pfb[:, 1:2049], func=AF.Identity, scale=cwc[:, ct, 1:2], bias=cwc[:, ct, 3:4]),
                         reads=[pfn, 'cwc'], writes=[t1n])
                    S.op('dve', lambda e: e.scalar_tensor_tensor(out=t1b[:], in0=pfb[:, 0:2048], scalar=cwc[:, ct, 0:1], in1=t1b[:], op0=ALU.mult, op1=ALU.add),
                         reads=[pfn, 'cwc', t1n], writes=[t1n])
                    S.op('dve', lambda e: e.scalar_tensor_tensor(out=ub[:], in0=pfb[:, 2:2050], scalar=cwc[:, ct, 2:3], in1=t1b[:], op0=ALU.mult, op1=ALU.add),
                         reads=[pfn, 'cwc', t1n], writes=[ubn])
                    for fn_s in slots[c4]:
                        fn_s()

            def group_tr(b, gi, n, irange):
                for i in irange:
                    kt_ = cnt['pt'] % 3
                    cnt['pt'] += 1
                    pt, ptn = pst[kt_], 'pst%d' % kt_
                    for c4 in range(4):
                        S.op('pe', lambda e: e.transpose(pt[:, c4, :], ucb[(n % 2) * 4 + c4][:, i * 128:(i + 1) * 128], ident[:]),
                             reads=['ucb%d' % ((n % 2) * 4 + c4), 'ident'], writes=[ptn], signal=(c4 == 3))
                    ku = cnt['us'] % 3
                    cnt['us'] += 1
                    us, un = ust[ku], 'ust%d' % ku
                    S.op('act', lambda e: e.copy(out=us[:], in_=pt[:]), reads=[ptn], writes=[un])
                    S.dma('pool', U_s.ap()[gi, b * L + i * 128: b * L + (i + 1) * 128, :], us[:], reads=[un])

            pending = None
            emit_hT(0, hTs[0], bufs, 'hTa')
            for b in range(NB):
                S.dma('pool', hT_s.ap()[b], hTs[b % 2][:].rearrange('p k n -> p (k n)'), reads=HTN[b % 2])
                for gi in range(3):
                    n = b * 3 + gi
                    slots = [[], [], [], []]
                    if b + 1 < NB:
                        nh = hTs[(b + 1) % 2]
                        nhn = 'hT' + 'ab'[(b + 1) % 2]
                        tts = [(0, 1), (2,), (3,)][gi]
                        for ti, tt in enumerate(tts):
                            for st_ in range(3):
                                slots[min(3, ti + st_)].append(lambda st_=st_, tt=tt: emit_hT_stage(st_, b + 1, tt, nh, bufs, nhn))
                    if pending is not None:
                        for q4 in range(4):
                            slots[q4].append(lambda q4=q4, pd=pending: group_tr(*pd, range(q4 * 4, q4 * 4 + 4)))
                    group_mm(b, gi, n, slots)
                    pending = (b, gi, n)
            group_tr(*pending, range(NT))
            S.barrier()

        with contextlib.ExitStack() as P:
            Wr = sb(P, 'Wr', [128, 8, 1792], BF16)
            hT = sb(P, 'hT2', [128, 8, 2050], BF16)
            ropeT = sb(P, 'ropeT', [128, 8, 16, 32], F32)
            esink = sb(P, 'esink', [128, 8], F32)
            atg = sb(P, 'atg', [128, 512], F32)
            masks = sb(P, 'masks', [128, 2, 128], BF16)
            mhalf = sb(P, 'mhalf', [128, 16], F32)
            S.op('pool', lambda e: e.memset(mhalf[:], -0.5), writes=['mhalf'])
            with contextlib.ExitStack() as P0:
                gcol = sb(P0, 'gcol2', [128, 8], F32)
                wst = [sb(P0, 'wsr%d' % i, [128, 1792], F32) for i in range(2)]
                rcs = sb(P0, 'rcs', [128, 2, 16, 32], F32)
                qkg = sb(P0, 'qkg', [128, 2, 64], F32)
                S.dma('sp', gcol[:], gcol_d.ap(), writes=['gcol'])
                S.dma('sp', rcs[:].rearrange('p a i j -> p (a i j)'), rope_d.ap(), writes=['rcs'])
                S.dma('sp', qkg[:, 0, :], bc(qg_d, 64), writes=['qkg'])
                S.dma('sp', qkg[:, 1, :], bc(kg_d, 64), writes=['qkg'])
                S.dma('sp', esink[:], bc(sink_d, 8), writes=['esink'])
                S.dma('sp', atg[:], bc(atg_d, 512), writes=['atg'])
                S.dma('sp', masks[:].rearrange('p a n -> p (a n)'), masks_d.ap(), writes=['masks'])
                S.op('act', lambda e: e.activation(out=esink[:], in_=esink[:], func=AF.Exp), reads=['esink'], writes=['esink'])
                for qk in range(2):
                    sc = 0.125 if qk == 0 else 1.0
                    for ti, (cs, half) in enumerate([(0, 0), (1, 1), (0, 1), (1, 0)]):
                        gsl = qkg[:, qk, half * 32:(half + 1) * 32].unsqueeze(1).broadcast_to([128, 16, 32])
                        S.op('dve', lambda e: e.scalar_tensor_tensor(out=ropeT[:, qk * 4 + ti, :, :], in0=rcs[:, cs, :, :], scalar=sc, in1=gsl, op0=ALU.mult, op1=ALU.mult),
                             reads=['rcs', 'qkg'], writes=['ropeT'])
                for kc in range(8):
                    wb = wst[kc % 2]
                    wn = 'wsr%d' % (kc % 2)
                    S.dma('sp', wb[:], win_d.ap()[kc * 128:(kc + 1) * 128, 1536:3328], writes=[wn])
                    S.op('act', lambda e: e.activation(out=Wr[:, kc, 0:512], in_=wb[:, 0:512], func=AF.Copy, scale=gcol[:, kc:kc + 1]),
                         reads=[wn, 'gcol'], writes=['Wr'])
                    S.op('act', lambda e: e.activation(out=Wr[:, kc, 512:1024].rearrange('p (g k d) -> p g k d', g=4, k=2),
                                                       in_=wb[:, 512:1024].rearrange('p (k g d) -> p g k d', k=2, g=4),
                                                       func=AF.Copy, scale=gcol[:, kc:kc + 1]),
                         reads=[wn, 'gcol'], writes=['Wr'])
                    S.op('act', lambda e: e.activation(out=Wr[:, kc, 1024:1792], in_=wb[:, 1024:1792], func=AF.Copy, scale=gcol[:, kc:kc + 1]),
                         reads=[wn, 'gcol'], writes=['Wr'])
                S.barrier()
            QT = sb(P, 'QT', [128, 2, 4, L], BF16)
            KT = sb(P, 'KT', [128, L], BF16)
            V = sb(P, 'V', [128, NT, 2, 65], BF16)
            sga = sb(P, 'sga', [128, NT, 512], BF16)
            two = lambda name, shape, dt: [sb(P, '%s%d' % (name, i), shape, dt) for i in range(2)]
            sgh = two('sgh', [128, 512], BF16)
            qraw = two('qraw', [128, 640], F32)
            qsq = two('qsq', [128, 640], F32)
            ssq = two('ssq', [128, 10], F32)
            qn = two('qn', [128, 640], F32)
            tq = [two('tq%d' % k, [128, 256], F32) for k in range(4)]
            tk = [two('tk%d' % k, [128, 64], F32) for k in range(4)]
            qb = two('qb', [128, 640], BF16)
            Pm = [[sb(P, 'Pm%d_%d' % (i, j), [128, 512], BF16) for j in range(3)] for i in range(2)]
            den = two('den', [128, 4], F32)
            ya = two('ya', [128, 512], F32)
            yq = two('yq', [128, 512], F32)
            ssa = two('ssa', [128, 1], F32)
            yab = two('yab', [128, 512], BF16)
            yts = two('yts', [128, 4, 128], BF16)
            Bk = [ps(P, 'B%d' % i, [128, 512]) for i in range(6)]
            PT = ps(P, 'PT', [128, 4, 128], BF16)
            PK = ps(P, 'PK', [128, 128], BF16)
            S.op('pool', lambda e: e.memset(V[:, :, :, 64:65], 1.0), writes=['V'])
            S.op('pool', lambda e: e.memset(QT[64:128, 0, :, :], 0.0), writes=['QT'])
            S.op('pool', lambda e: e.memset(QT[0:64, 1, :, :], 0.0), writes=['QT'])

            def grp(bank, bname, lt, wcols):
                for kc in range(8):
                    S.op('pe', lambda e: e.matmul(bank, lhsT=lt(kc), rhs=Wr[:, kc, wcols], start=(kc == 0), stop=(kc == 7)),
                         reads=['hT', 'Wr'], writes=[bname], signal=(kc == 7))

            def proj_pe(i):
                par = i % 2
                lt = lambda kc: hT[:, kc, 1 + i * 128: 1 + (i + 1) * 128]
                grp(Bk[4][:, 0:256], 'B4', lt, slice(1024, 1280))
                grp(Bk[2][:], 'B2', lt, slice(512, 1024))
                grp(Bk[0][:], 'B0', lt, slice(0, 512))
                grp(Bk[1][:], 'B1', lt, slice(1280, 1792))

            def proj_ew(b, i):
                par = i % 2
                kvb = Bk[4][:, 0:256]
                kvn = 'B4'
                qbk = Bk[2]
                qbn = 'B2'
                sfx = str(par)
                S.op('act', lambda e: e.copy(out=qraw[par][:, 512:640], in_=kvb[:, 0:128]), reads=[kvn], writes=['qrawk' + sfx])
                S.op('act', lambda e: e.copy(out=V[:, i, :, 0:64], in_=kvb[:, 128:256].rearrange('p (k d) -> p k d', k=2)), reads=[kvn], writes=['V'])
                S.op('act', lambda e: e.copy(out=qraw[par][:, 0:512], in_=qbk[:]), reads=[qbn], writes=['qrawq' + sfx])
                S.op('dve', lambda e: e.tensor_tensor(out=qsq[par][:], in0=qraw[par][:], in1=qraw[par][:], op=ALU.mult),
                     reads=['qrawq' + sfx, 'qrawk' + sfx], writes=['qsq' + sfx])
                S.op('dve', lambda e: e.tensor_reduce(out=ssq[par][:], in_=qsq[par][:].rearrange('p (h d) -> p h d', d=64), axis=AX.X, op=ALU.add),
                     reads=['qsq' + sfx], writes=['ssq' + sfx])
                S.op('act', lambda e: e.activation(out=ssq[par][:], in_=ssq[par][:], func=AF.Sqrt, scale=1.0 / 64, bias=eps_t[:, 0:1]),
                     reads=['ssq' + sfx, 'eps'], writes=['ssq' + sfx])
                S.op('dve', lambda e: e.reciprocal(out=ssq[par][:], in_=ssq[par][:]), reads=['ssq' + sfx], writes=['ssq' + sfx])
                S.op('dve', lambda e: e.tensor_tensor(out=qn[par][:].rearrange('p (h d) -> p h d', d=64), in0=qraw[par][:].rearrange('p (h d) -> p h d', d=64),
                                                      in1=ssq[par][:].unsqueeze(2).broadcast_to([128, 10, 64]), op=ALU.mult),
                     reads=['qrawq' + sfx, 'qrawk' + sfx, 'ssq' + sfx], writes=['qn' + sfx])
                sg = sgh[par]
                sgn = 'sgh' + sfx
                S.op('act', lambda e: e.activation(out=sg[:], in_=Bk[0][:], func=AF.Silu), reads=['B0'], writes=[sgn])
                S.dma('pool', SG_s.ap()[b * L + i * 128: b * L + (i + 1) * 128, :], sg[:], reads=[sgn])
                S.op('act', lambda e: e.activation(out=sga[:, i, :], in_=Bk[1][:], func=AF.Silu), reads=['B1'], writes=['sga'])
                for qk, (c0, nh, tt_) in enumerate([(0, 8, tq), (512, 2, tk)]):
                    v4 = qn[par][:, c0:c0 + nh * 64].rearrange('p (h t j) -> p h t j', t=2, j=32)
                    o4 = qb[par][:, c0:c0 + nh * 64].rearrange('p (h t j) -> p h t j', t=2, j=32)
                    q1 = v4[:, :, 0, :]
                    q2 = v4[:, :, 1, :]
                    tb = lambda ti: ropeT[:, qk * 4 + ti, i, :].unsqueeze(1).broadcast_to([128, nh, 32])
                    a = [tt_[k][par][:, 0:nh * 32].rearrange('p (h j) -> p h j', j=32) for k in range(4)]
                    an = ['t%d%d%s' % (qk, k, sfx) for k in range(4)]
                    qbn2 = 'qb%d%s' % (qk, sfx)
                    S.op('dve', lambda e: e.tensor_tensor(out=a[0], in0=q1, in1=tb(0), op=ALU.mult), reads=['qn' + sfx, 'ropeT'], writes=[an[0]])
                    S.op('pool', lambda e: e.tensor_tensor(out=a[1], in0=q2, in1=tb(1), op=ALU.mult), reads=['qn' + sfx, 'ropeT'], writes=[an[1]])
                    S.op('pool', lambda e: e.tensor_tensor(out=a[2], in0=q2, in1=tb(2), op=ALU.mult), reads=['qn' + sfx, 'ropeT'], writes=[an[2]])
                    S.op('dve', lambda e: e.tensor_tensor(out=a[3], in0=q1, in1=tb(3), op=ALU.mult), reads=['qn' + sfx, 'ropeT'], writes=[an[3]])
                    S.op('dve', lambda e: e.tensor_tensor(out=o4[:, :, 0, :], in0=a[0], in1=a[1], op=ALU.subtract), reads=[an[0], an[1]], writes=[qbn2 + 'a'])
                    S.op('pool', lambda e: e.tensor_tensor(out=o4[:, :, 1, :], in0=a[2], in1=a[3], op=ALU.add), reads=[an[2], an[3]], writes=[qbn2 + 'b'])

            def proj_tr(i):
                par = i % 2
                sfx = str(par)
                for g in range(4):
                    S.op('pe', lambda e: e.transpose(PT[:, g, :], qb[par][:, g * 128:(g + 1) * 128], ident[:]),
                         reads=['qb0%sa' % sfx, 'qb0%sb' % sfx, 'ident'], writes=['PT'], signal=(g == 3))
                S.op('pe', lambda e: e.transpose(PK[:], qb[par][:, 512:640], ident[:]),
                     reads=['qb1%sa' % sfx, 'qb1%sb' % sfx, 'ident'], writes=['PK'])
                S.op('act', lambda e: e.copy(out=QT[0:64, 0, :, i * 128:(i + 1) * 128], in_=PT[0:64, :, :]), reads=['PT'], writes=['QT'])
                S.op('act', lambda e: e.copy(out=QT[64:128, 1, :, i * 128:(i + 1) * 128], in_=PT[64:128, :, :]), reads=['PT'], writes=['QT'])
                S.op('act', lambda e: e.copy(out=KT[:, i * 128:(i + 1) * 128], in_=PK[:]), reads=['PK'], writes=['KT'])

            def s_jobs():
                jobs = []
                for u in range(2 * NT):
                    i, kv = divmod(u, 2)
                    ccs = [c for c in (i - 1, i, i + 1) if 0 <= c < NT]
                    for ci, c in enumerate(ccs):
                        jobs.append((u, ci, c, ci == len(ccs) - 1))
                return jobs

            def att_S1(k, job):
                u, ci, c, last = job
                i, kv = divmod(u, 2)
                pr = slice(kv * 64, (kv + 1) * 64)
                bi = k % 4
                S.op('pe', lambda e: e.matmul(Bk[bi][:].rearrange('p (g q) -> p g q', g=4), lhsT=KT[:, c * 128:(c + 1) * 128], rhs=QT[:, kv, :, i * 128:(i + 1) * 128], start=True, stop=True),
                     reads=['KT', 'QT'], writes=['B%d' % bi])

            def att_E1(k, job):
                u, ci, c, last = job
                i, kv = divmod(u, 2)
                bi = k % 4
                pm = Pm[u % 2][ci]
                pmn = 'Pm%d_%d' % (u % 2, ci)
                S.op('act', lambda e: e.activation(out=pm[:], in_=Bk[bi][:], func=AF.Exp), reads=['B%d' % bi], writes=[pmn])
                if c != i:
                    mk = masks[:, 0 if c < i else 1, :].unsqueeze(1).broadcast_to([128, 4, 128])
                    S.op('dve', lambda e: e.tensor_tensor(out=pm[:].rearrange('p (g q) -> p g q', g=4), in0=pm[:].rearrange('p (g q) -> p g q', g=4), in1=mk, op=ALU.mult),
                         reads=[pmn, 'masks'], writes=[pmn])

            def att_PV(u):
                i, kv = divmod(u, 2)
                po = Bk[4 + u % 2][:, 0:260].rearrange('p (g d) -> p g d', g=4)
                pon = ['B%d' % (4 + u % 2)]
                lst = [c for c in (i - 1, i, i + 1) if 0 <= c < NT]
                for g in range(4):
                    for ci, c in enumerate(lst):
                        S.op('pe', lambda e: e.matmul(po[:, g, :], lhsT=Pm[u % 2][ci][:, g * 128:(g + 1) * 128], rhs=V[:, c, kv, :], start=(ci == 0), stop=(ci == len(lst) - 1)),
                             reads=['Pm%d_%d' % (u % 2, ci), 'V'], writes=pon, signal=(g == 3 and ci == len(lst) - 1))
                return po, pon

            def att_D(u, po, pon):
                i, kv = divmod(u, 2)
                d_ = den[u % 2]
                dn = 'den%d' % (u % 2)
                yan = 'ya%d_%d' % (i % 2, kv)
                S.op('dve', lambda e: e.tensor_tensor(out=d_[:], in0=po[:, :, 64], in1=esink[:, kv * 4:(kv + 1) * 4], op=ALU.add),
                     reads=pon + ['esink'], writes=[dn])
                S.op('dve', lambda e: e.reciprocal(out=d_[:], in_=d_[:]), reads=[dn], writes=[dn])
                S.op('dve', lambda e: e.tensor_tensor(out=ya[i % 2][:, kv * 256:(kv + 1) * 256].rearrange('p (g d) -> p g d', g=4), in0=po[:, :, 0:64],
                                                      in1=d_[:].unsqueeze(2).broadcast_to([128, 4, 64]), op=ALU.mult),
                     reads=pon + [dn], writes=[yan])

            def att_N(i):
                par = i % 2
                sfx = str(par)
                yr = ['ya%d_0' % par, 'ya%d_1' % par]
                S.op('dve', lambda e: e.scalar_tensor_tensor(out=yq[par][:], in0=ya[par][:], scalar=1.0, in1=ya[par][:], op0=ALU.mult, op1=ALU.mult, accum_out=ssa[par][:, 0:1]),
                     reads=yr, writes=['yq' + sfx, 'ssa' + sfx])
                S.op('act', lambda e: e.activation(out=ssa[par][:], in_=ssa[par][:], func=AF.Ln, scale=1.0 / 512, bias=eps_t[:, 0:1]),
                     reads=['ssa' + sfx, 'eps'], writes=['ssa' + sfx])
                S.op('act', lambda e: e.activation(out=ssa[par][:], in_=ssa[par][:], func=AF.Exp, scale=-0.5),
                     reads=['ssa' + sfx], writes=['ssa' + sfx])
                S.op('dve', lambda e: e.scalar_tensor_tensor(out=yq[par][:], in0=ya[par][:], scalar=ssa[par][:, 0:1], in1=atg[:], op0=ALU.mult, op1=ALU.mult),
                     reads=yr + ['ssa' + sfx, 'atg', 'yq' + sfx], writes=['yq' + sfx])
                S.op('pool', lambda e: e.tensor_tensor(out=yab[par][:], in0=yq[par][:], in1=sga[:, i, :], op=ALU.mult), reads=['yq' + sfx, 'sga'], writes=['yab' + sfx])

            def att_T(b, i):
                par = i % 2
                sfx = str(par)
                for g in range(4):
                    S.op('pe', lambda e: e.transpose(PT[:, g, :], yab[par][:, g * 128:(g + 1) * 128], ident[:]),
                         reads=['yab' + sfx, 'ident'], writes=['PT'], signal=(g == 3))
                yt = yts[par]
                ytn = 'yts' + sfx
                S.op('act', lambda e: e.copy(out=yt[:], in_=PT[:]), reads=['PT'], writes=[ytn])
                S.dma('pool', yT_s.ap()[b * NT + i, :, 4:8, :], yt[:], reads=[ytn])

            for b in range(NB):
                S.dma('sp', hT[:].rearrange('p k n -> p (k n)'), hT_s.ap()[b], writes=['hT'])
                for i in range(NT + 1):
                    if i < NT:
                        proj_pe(i)
                        proj_ew(b, i)
                    if i >= 1:
                        proj_tr(i - 1)
                jobs = s_jobs()
                AHEAD = 4
                for k in range(min(AHEAD, len(jobs))):
                    att_S1(k, jobs[k])
                deferred = []
                for k, job in enumerate(jobs):
                    att_E1(k, job)
                    if k + AHEAD < len(jobs):
                        att_S1(k + AHEAD, jobs[k + AHEAD])
                    for fn_d in deferred:
                        fn_d()
                    deferred = []
                    u, ci, c, last = job
                    if last:
                        po, pon = att_PV(u)
                        att_D(u, po, pon)
                        if u % 2 == 1:
                            deferred.append(lambda i_=u // 2: att_N(i_))
                            if u // 2 >= 1:
                                deferred.append(lambda i_=u // 2 - 1: att_T(b, i_))
                for fn_d in deferred:
                    fn_d()
                att_T(b, NT - 1)
            S.barrier()

        with contextlib.ExitStack() as P:
            ke = sb(P, 'ke', [128, 16, 1024], BF16)
            ko = sb(P, 'ko', [128, 16, 1024], BF16)
            with contextlib.ExitStack() as P0:
                zT = sb(P0, 'zT', [33, L], F32)
                w1 = sb(P0, 'w1', [33, 64], F32)
                w2 = sb(P0, 'w2', [64, 64], F32)
                w3 = sb(P0, 'w3', [64, 64], F32)
                w4 = sb(P0, 'w4', [64, 2048], F32)
                fcol = sb(P0, 'fcol', [64, 4], F32)
                fsc = sb(P0, 'fsc', [64, 4], F32)
                hb = sb(P0, 'hb', [1, 1024], F32)
                hA = sb(P0, 'hA', [64, L], F32)
                hB = sb(P0, 'hB', [64, L], F32)
                s1 = sb(P0, 's1', [64, 512], F32)
                s2 = sb(P0, 's2', [64, 512], F32)
                dct = [sb(P0, 'dct%d' % i, [128, 512], F32) for i in range(2)]
                kf = [sb(P0, 'kf%d' % i, [128, 512], F32) for i in range(2)]
                kb = [sb(P0, 'kb%d' % i, [128, 512], F32) for i in range(2)]
                psf = [ps(P0, 'psf%d' % i, [64, 512]) for i in range(2)]
                psk = [ps(P0, 'psk%d' % i, [128, 512]) for i in range(4)]
                S.dma('sp', zT[:], zT_d.ap(), writes=['zT'])
                S.dma('sp', w1[:], fw1_d.ap(), writes=['w1'])
                S.dma('sp', w2[:], fw2_d.ap(), writes=['w2'])
                S.dma('sp', w3[:], fw3_d.ap(), writes=['w3'])
                S.dma('sp', w4[:], fw4_d.ap(), writes=['w4'])
                S.dma('sp', fcol[:], fcol_d.ap(), writes=['fcol'])
                S.dma('sp', hb[:], hb_d.ap(), writes=['hb'])
                S.op('dve', lambda e: e.tensor_scalar(out=fsc[:, 0:1], in0=fcol[:, 3:4], scalar1=1.0 / 3.0, scalar2=None, op0=ALU.mult),
                     reads=['fcol'], writes=['fsc'])
                S.op('dve', lambda e: e.tensor_scalar(out=fsc[:, 1:4], in0=fcol[:, 0:3], scalar1=fsc[:, 0:1], scalar2=None, op0=ALU.mult),
                     reads=['fcol', 'fsc'], writes=['fsc'])
                layers = [(w1, 'w1', zT, 'zT', 33, hA, 'hA'), (w2, 'w2', hA, 'hA', 64, hB, 'hB'), (w3, 'w3', hB, 'hB', 64, hA, 'hA')]
                for li, (wt, wn, src, sn, kk, dst, dn) in enumerate(layers):
                    for ct in range(4):
                        pf = psf[ct % 2]
                        pfn = 'psf%d' % (ct % 2)
                        cs = slice(ct * 512, (ct + 1) * 512)
                        S.op('pe', lambda e: e.matmul(pf[:], lhsT=wt[0:kk, :], rhs=src[0:kk, cs], start=True, stop=True), reads=[wn, sn], writes=[pfn])
                        S.op('act', lambda e: e.activation(out=s1[:], in_=pf[:], func=AF.Sin, scale=fsc[:, 0:1], bias=fsc[:, li + 1:li + 2]),
                             reads=[pfn, 'fsc'], writes=['s1'])
                        S.op('dve', lambda e: e.tensor_tensor(out=s2[:], in0=s1[:], in1=s1[:], op=ALU.mult), reads=['s1'], writes=['s2'])
                        S.op('dve', lambda e: e.tensor_scalar(out=s2[:], in0=s2[:], scalar1=-4.0, scalar2=3.0, op0=ALU.mult, op1=ALU.add), reads=['s2'], writes=['s2'])
                        S.op('dve', lambda e: e.tensor_tensor(out=dst[:, cs], in0=s2[:], in1=s1[:], op=ALU.mult), reads=['s1', 's2', sn], writes=[dn])
                h3 = hA
                w4v = w4[:].rearrange('p (o r c) -> p o r c', o=2, r=2)
                nk = 0
                for mc in range(16):
                    dc = dct[mc % 2]
                    dcn = 'dct%d' % (mc % 2)
                    S.dma('sp', dc[:], dec_d.ap()[mc * 128:(mc + 1) * 128, :], writes=[dcn])
                    for o in range(2):
                        pkf = psk[o * 2]
                        pkb = psk[o * 2 + 1]
                        f_ = kf[nk % 2]
                        b_ = kb[nk % 2]
                        fn_ = 'kf%d' % (nk % 2)
                        bn_ = 'kb%d' % (nk % 2)
                        nk += 1
                        S.op('pe', lambda e: e.matmul(pkf[:], lhsT=h3[:, mc * 128:(mc + 1) * 128], rhs=w4v[:, o, 0, :], start=True, stop=True), reads=['hA', 'w4'], writes=['psk%d' % (o * 2)])
                        S.op('pe', lambda e: e.matmul(pkb[:], lhsT=h3[:, mc * 128:(mc + 1) * 128], rhs=w4v[:, o, 1, :], start=True, stop=True), reads=['hA', 'w4'], writes=['psk%d' % (o * 2 + 1)])
                        S.op('dve', lambda e: e.tensor_tensor(out=f_[:], in0=pkf[:], in1=dc[:], op=ALU.mult), reads=['psk%d' % (o * 2), dcn], writes=[fn_])
                        S.op('dve', lambda e: e.tensor_tensor(out=b_[:], in0=pkb[:], in1=dc[:], op=ALU.mult), reads=['psk%d' % (o * 2 + 1), dcn], writes=[bn_])
                        if mc == 0:
                            S.op('dve', lambda e: e.memset(b_[0:1, :], 0.0), reads=[bn_], writes=[bn_])
                            S.op('dve', lambda e: e.tensor_tensor(out=f_[0:1, :], in0=f_[0:1, :], in1=hb[0:1, o * 512:(o + 1) * 512], op=ALU.add), reads=[fn_, 'hb'], writes=[fn_])
                        S.op('pool', lambda e: e.tensor_tensor(out=ke[:, mc, o * 512:(o + 1) * 512], in0=f_[:], in1=b_[:], op=ALU.add), reads=[fn_, bn_], writes=['ke'])
                        S.op('pool', lambda e: e.tensor_tensor(out=ko[:, mc, o * 512:(o + 1) * 512], in0=f_[:], in1=b_[:], op=ALU.subtract), reads=[fn_, bn_], writes=['ko'])
                S.barrier()
            fwt = [sb(P, 'fwk%d' % i, [128, 16, 128], BF16) for i in range(2)]
            kst = [sb(P, 'kst%d' % i, [128, 1024], BF16) for i in range(2)]
            psK = [ps(P, 'psK%d' % i, [128, 512]) for i in range(4)]
            psN = ps(P, 'psN', [1, 1024])
            for gt in range(32):
                fw = fwt[gt % 2]
                fn_ = 'fwk%d' % (gt % 2)
                ks = kst[gt % 2]
                ksn = 'kst%d' % (gt % 2)
                src, srcn = (ke, 'ke') if gt < 16 else (ko, 'ko')
                S.dma('sp', fw[:].rearrange('p m g -> p (m g)'), Fw_d.ap()[gt, :, 0:2048], writes=[fn_])
                for o in range(2):
                    pk = psK[(gt % 2) * 2 + o]
                    pkn = 'psK%d' % ((gt % 2) * 2 + o)
                    for mc in range(16):
                        S.op('pe', lambda e: e.matmul(pk[:], lhsT=fw[:, mc, :], rhs=src[:, mc, o * 512:(o + 1) * 512], start=(mc == 0), stop=(mc == 15)),
                             reads=[fn_, srcn], writes=[pkn], signal=(mc == 15))
                    if o == 0:
                        S.op('act', lambda e: e.copy(out=ks[:, 0:512], in_=pk[:]), reads=[pkn], writes=[ksn])
                    else:
                        S.op('dve', lambda e: e.tensor_copy(out=ks[:, 512:1024], in_=pk[:]), reads=[pkn], writes=[ksn])
                if gt == 16:
                    for o in range(2):
                        for mc in range(16):
                            S.op('pe', lambda e: e.matmul(psN[0:1, o * 512:(o + 1) * 512], lhsT=fw[:, mc, 0:1], rhs=ke[:, mc, o * 512:(o + 1) * 512], start=(mc == 0), stop=(mc == 15)),
                                 reads=[fn_, 'ke'], writes=['psN'], signal=(mc == 15))
                    S.op('act', lambda e: e.copy(out=ks[0:1, :], in_=psN[0:1, :]), reads=['psN', ksn], writes=[ksn])
                S.dma('pool', Kh_s.ap()[gt % 16, :, gt // 16, :], ks[:], reads=[ksn])
            S.barrier()

        with contextlib.ExitStack() as P:
            uv = sb(P, 'uv', [128, NT, 512], BF16)
            x1 = sb(P, 'x1', [128, NT, 512], BF16)
            x2 = sb(P, 'x2', [128, NT, 512], BF16)
            x1r = sb(P, 'x1r', [128, 8, 512], BF16)
            x2r = sb(P, 'x2r', [128, 8, 512], BF16)
            vp = sb(P, 'vp', [128, 8, 512], BF16)
            vm = sb(P, 'vm', [128, 8, 512], BF16)
            Yh = sb(P, 'Yh', [128, 32, 512], BF16)
            hyg = sb(P, 'hyg', [128, 512], F32)
            Jm = sb(P, 'Jm', [128, 128], BF16)
            fwt = [sb(P, 'fwt%d' % i, [128, 2, 8, 128], BF16) for i in range(3)]
            bwt = [sb(P, 'bwt%d' % i, [128, 32, 128], BF16) for i in range(3)]
            kt = [sb(P, 'kt%d' % i, [128, 2, 512], BF16) for i in range(3)]
            ta = sb(P, 'ta', [128, 512], F32)
            tb_ = sb(P, 'tb', [128, 512], F32)
            tc = sb(P, 'tc', [128, 512], F32)
            td = sb(P, 'td', [128, 512], F32)
            Ac = [sb(P, 'Ac%d' % i, [128, 512], F32) for i in range(2)]
            dd = [sb(P, 'dd%d' % i, [128, 512], F32) for i in range(2)]
            ss = [sb(P, 'ss%d' % i, [128, 512], F32) for i in range(2)]
            yq2 = [sb(P, 'yq2_%d' % i, [128, 512], F32) for i in range(2)]
            ssh2 = [sb(P, 'ssh2_%d' % i, [128, 1], F32) for i in range(2)]
            yhb2 = [sb(P, 'yhb2_%d' % i, [128, 512], BF16) for i in range(2)]
            sgt = [sb(P, 'sgt%d' % i, [128, 512], BF16) for i in range(2)]
            yts = [sb(P, 'yth%d' % i, [128, 4, 128], BF16) for i in range(2)]
            Q = [ps(P, 'Q%d' % i, [128, 512]) for i in range(4)]
            Rb = ps(P, 'Rb', [128, 512])
            PT4 = [ps(P, 'PT4_%d' % i, [128, 4, 128]) for i in range(2)]
            S.dma('sp', hyg[:], bc(hyg_d, 512), writes=['hyg'])
            S.dma('sp', Jm[:], jmat_d.ap(), writes=['Jm'])

            def emit_T4(b, tok0, hl):
                sfx = str(hl)
                mv = ident if hl == 0 else Jm
                mvn = 'ident' if hl == 0 else 'Jm'
                for g in range(4):
                    S.op('pe', lambda e: e.matmul(PT4[hl][:, g, :], lhsT=yhb2[hl][:, g * 128:(g + 1) * 128], rhs=mv[:], start=True, stop=True),
                         reads=['yhb' + sfx, mvn], writes=['PT4' + sfx], signal=(g == 3))
                yt = yts[hl]
                ytn = 'yth' + sfx
                S.op('act', lambda e: e.copy(out=yt[:], in_=PT4[hl][:]), reads=['PT4' + sfx], writes=[ytn])
                S.dma('pool', yT_s.ap()[tok0 // 128, :, 0:4, :], yt[:], reads=[ytn])

            nf = 0
            nb_ = 0
            nq = 0
            def load_in(b, which):
                t_, tn = [(uv, 'uv'), (x1, 'x1'), (x2, 'x2')][which]
                S.dma('sp', t_[:], U_s.ap()[which, b * L:(b + 1) * L, :].rearrange('(i p) c -> p i c', p=128), writes=[tn])

            def prep(which):
                t_, tn = [(uv, 'uv'), (x1, 'x1'), (x2, 'x2')][which]
                for a_ in range(8):
                    c = 7 - a_
                    qb_ = Q[nqc[0] % 4]
                    qn_ = 'Q%d' % (nqc[0] % 4)
                    nqc[0] += 1
                    S.op('pe', lambda e: e.matmul(qb_[:], lhsT=Jm[:], rhs=t_[:, c, :], start=True, stop=True), reads=['Jm', tn], writes=[qn_])
                    if which == 0:
                        S.op('dve', lambda e: e.tensor_tensor(out=vp[:, a_, :], in0=qb_[:], in1=uv[:, 8 + a_, :], op=ALU.add), reads=[qn_, 'uv'], writes=['vp%d' % a_])
                        S.op('dve', lambda e: e.tensor_tensor(out=vm[:, a_, :], in0=uv[:, 8 + a_, :], in1=qb_[:], op=ALU.subtract), reads=[qn_, 'uv'], writes=['vm%d' % a_])
                    elif which == 1:
                        S.op('act', lambda e: e.copy(out=x1r[:, a_, :], in_=qb_[:]), reads=[qn_], writes=['x1r'])
                    else:
                        S.op('act', lambda e: e.copy(out=x2r[:, a_, :], in_=qb_[:]), reads=[qn_], writes=['x2r'])

            nqc = [0]
            for w_ in range(3):
                load_in(0, w_)
            for b in range(NB):
                prep(0)
                prep(1)
                for o in range(2):
                    for ft in range(16):
                        fw = fwt[nf % 3]
                        fn_ = 'fwt%d' % (nf % 3)
                        k_ = kt[nf % 3]
                        kn = 'kt%d' % (nf % 3)
                        pR = Q[(nf % 2) * 2]
                        pRn = 'Q%d' % ((nf % 2) * 2)
                        pI = Q[(nf % 2) * 2 + 1]
                        pIn = 'Q%d' % ((nf % 2) * 2 + 1)
                        nf += 1
                        S.dma('sp', fw[:].rearrange('p r m g -> p (r m g)'), Fh_d.ap()[ft], writes=[fn_])
                        if ft == 6 and o == 1 and b + 1 < NB:
                            load_in(b + 1, 1)
                        if ft == 6 and o == 0 and b >= 1:
                            load_in(b, 2)
                        S.dma('sp', k_[:], Kh_s.ap()[ft, :, :, o * 512:(o + 1) * 512], writes=[kn])
                        for mc in range(8):
                            S.op('pe', lambda e: e.matmul(pR[:], lhsT=fw[:, 0, mc, :], rhs=vp[:, mc, :], start=(mc == 0), stop=(mc == 7)),
                                 reads=[fn_, 'vp%d' % mc], writes=[pRn], signal=(mc == 7))
                        for mc in range(8):
                            S.op('pe', lambda e: e.matmul(pI[:], lhsT=fw[:, 1, mc, :], rhs=vm[:, mc, :], start=(mc == 0), stop=(mc == 7)),
                                 reads=[fn_, 'vm%d' % mc], writes=[pIn], signal=(mc == 7))
                        S.op('dve', lambda e: e.tensor_tensor(out=ta[:], in0=pR[:], in1=k_[:, 0, :], op=ALU.mult), reads=[pRn, kn], writes=['ta'])
                        S.op('dve', lambda e: e.tensor_tensor(out=tb_[:], in0=pI[:], in1=k_[:, 1, :], op=ALU.mult), reads=[pIn, kn], writes=['tb'])
                        S.op('pool', lambda e: e.tensor_tensor(out=Yh[:, ft, :], in0=ta[:], in1=tb_[:], op=ALU.subtract), reads=['ta', 'tb'], writes=['Yh%d' % ft])
                        S.op('dve', lambda e: e.tensor_tensor(out=tc[:], in0=pR[:], in1=k_[:, 1, :], op=ALU.mult), reads=[pRn, kn], writes=['tc'])
                        S.op('dve', lambda e: e.tensor_tensor(out=td[:], in0=pI[:], in1=k_[:, 0, :], op=ALU.mult), reads=[pIn, kn], writes=['td'])
                        S.op('pool', lambda e: e.tensor_tensor(out=Yh[:, 16 + ft, :], in0=tc[:], in1=td[:], op=ALU.add), reads=['tc', 'td'], writes=['Yh%d' % (16 + ft)])
                        if ft == 0:
                            S.op('pool', lambda e: e.tensor_copy(out=Yh[0:1, 0, :], in_=ta[0:1, :]), reads=['ta', 'Yh0'], writes=['Yh0'])
                            S.op('pool', lambda e: e.tensor_copy(out=Yh[0:1, 16, :], in_=tb_[0:1, :]), reads=['tb', 'Yh16'], writes=['Yh16'])
                    if o == 0:
                        prep(2)
                    pend = None
                    for jt in range(8):
                        bw = bwt[nb_ % 3]
                        bn = 'bwt%d' % (nb_ % 3)
                        par = nb_ % 2
                        sfx = str(par)
                        pA = Q[par * 2]
                        pAn = 'Q%d' % (par * 2)
                        pB = Q[par * 2 + 1]
                        pBn = 'Q%d' % (par * 2 + 1)
                        nb_ += 1
                        S.dma('sp', bw[:].rearrange('p g t -> p (g t)'), Bh_d.ap()[jt], writes=[bn])
                        if jt == 3 and o == 0 and b + 1 < NB:
                            load_in(b + 1, 0)
                        for gt in range(16):
                            S.op('pe', lambda e: e.matmul(pA[:], lhsT=bw[:, gt, :], rhs=Yh[:, gt, :], start=(gt == 0), stop=(gt == 15)),
                                 reads=[bn, 'Yh%d' % gt], writes=[pAn], signal=(gt == 15))
                        for gt in range(16, 32):
                            S.op('pe', lambda e: e.matmul(pB[:], lhsT=bw[:, gt, :], rhs=Yh[:, gt, :], start=(gt == 16), stop=(gt == 31)),
                                 reads=[bn, 'Yh%d' % gt], writes=[pBn], signal=(gt == 31))
                        S.op('act', lambda e: e.copy(out=Ac[par][:], in_=pA[:]), reads=[pAn], writes=['Ac' + sfx])
                        S.op('dve', lambda e: e.tensor_tensor(out=dd[par][:], in0=Ac[par][:], in1=pB[:], op=ALU.subtract), reads=['Ac' + sfx, pBn], writes=['dd' + sfx])
                        S.op('dve', lambda e: e.tensor_tensor(out=ss[par][:], in0=Ac[par][:], in1=pB[:], op=ALU.add), reads=['Ac' + sfx, pBn], writes=['ss' + sfx])
                        if o == 0:
                            S.op('pool', lambda e: e.tensor_tensor(out=dd[par][:], in0=dd[par][:], in1=x1[:, 8 + jt, :], op=ALU.mult), reads=['dd' + sfx, 'x1'], writes=['dd' + sfx])
                            S.op('pool', lambda e: e.tensor_tensor(out=ss[par][:], in0=ss[par][:], in1=x1r[:, jt, :], op=ALU.mult), reads=['ss' + sfx, 'x1r'], writes=['ss' + sfx])
                            S.op('dve', lambda e: e.tensor_tensor(out=vp[:, jt, :], in0=dd[par][:], in1=ss[par][:], op=ALU.add), reads=['dd' + sfx, 'ss' + sfx], writes=['vp%d' % jt])
                            S.op('dve', lambda e: e.tensor_tensor(out=vm[:, jt, :], in0=dd[par][:], in1=ss[par][:], op=ALU.subtract), reads=['dd' + sfx, 'ss' + sfx], writes=['vm%d' % jt])
                        else:
                            if pend is not None:
                                for args in pend:
                                    emit_T4(*args)
                            pend = []
                            for hl in range(2):
                                hs = str(hl)
                                ysrc, ysn = (dd[par], 'dd' + sfx) if hl == 0 else (ss[par], 'ss' + sfx)
                                ck = 8 + jt if hl == 0 else 7 - jt
                                tok0 = b * L + ck * 128
                                xg, xgn = (x2[:, 8 + jt, :], 'x2') if hl == 0 else (x2r[:, jt, :], 'x2r')
                                sg = sgt[hl]
                                sgn = 'sgt' + hs
                                S.dma('sp', sg[:], SG_s.ap()[tok0: tok0 + 128, :], writes=[sgn])
                                S.op('pool', lambda e: e.tensor_tensor(out=ysrc[:], in0=ysrc[:], in1=xg, op=ALU.mult), reads=[ysn, xgn], writes=[ysn])
                                S.op('dve', lambda e: e.scalar_tensor_tensor(out=yq2[hl][:], in0=ysrc[:], scalar=1.0, in1=ysrc[:], op0=ALU.mult, op1=ALU.mult, accum_out=ssh2[hl][:, 0:1]),
                                     reads=[ysn], writes=['yq4' + hs, 'ssh' + hs])
                                S.op('act', lambda e: e.activation(out=ssh2[hl][:], in_=ssh2[hl][:], func=AF.Sqrt, scale=1.0 / 512, bias=eps_t[:, 0:1]),
                                     reads=['ssh' + hs, 'eps'], writes=['ssh' + hs])
                                S.op('dve', lambda e: e.reciprocal(out=ssh2[hl][:], in_=ssh2[hl][:]), reads=['ssh' + hs], writes=['ssh' + hs])
                                S.op('dve', lambda e: e.scalar_tensor_tensor(out=yq2[hl][:], in0=ysrc[:], scalar=ssh2[hl][:, 0:1], in1=hyg[:], op0=ALU.mult, op1=ALU.mult),
                                     reads=[ysn, 'ssh' + hs, 'hyg', 'yq4' + hs], writes=['yq4' + hs])
                                if hl == 0:
                                    S.op('pool', lambda e: e.tensor_tensor(out=yhb2[hl][:], in0=yq2[hl][:], in1=sg[:], op=ALU.mult), reads=['yq4' + hs, sgn], writes=['yhb' + hs])
                                else:
                                    S.op('pe', lambda e: e.matmul(Rb[:], lhsT=Jm[:], rhs=sg[:], start=True, stop=True), reads=['Jm', sgn], writes=['Rb'])
                                    S.op('dve', lambda e: e.tensor_tensor(out=yhb2[hl][:], in0=yq2[hl][:], in1=Rb[:], op=ALU.mult), reads=['yq4' + hs, 'Rb'], writes=['yhb' + hs])
                                pend.append((b, tok0, hl))
                    if pend is not None:
                        for args in pend:
                            emit_T4(*args)
            S.barrier()

        with contextlib.ExitStack() as P:
            Wo = sb(P, 'Wo', [128, 8, D], BF16)
            wst = [sb(P, 'wso%d' % i, [128, D], F32) for i in range(2)]
            yt = [sb(P, 'yt%d' % i, [128, 8, 128], BF16) for i in range(3)]
            xr = [sb(P, 'xr%d' % i, [128, D], F32) for i in range(3)]
            ot = [sb(P, 'ot%d' % i, [128, D], F32) for i in range(2)]
            psO = [ps(P, 'psW%d' % i, [128, 512]) for i in range(4)]
            for kc in range(8):
                wb = wst[kc % 2]
                wn = 'wso%d' % (kc % 2)
                S.dma('sp', wb[:], wout_d.ap()[kc * 128:(kc + 1) * 128, :], writes=[wn])
                if kc % 2 == 0:
                    S.op('act', lambda e: e.copy(out=Wo[:, kc, :], in_=wb[:]), reads=[wn], writes=['Wo'])
                else:
                    S.op('dve', lambda e: e.tensor_copy(out=Wo[:, kc, :], in_=wb[:]), reads=[wn], writes=['Wo'])
            for c in range(NB * NT):
                y_ = yt[c % 3]
                yn = 'yt%d' % (c % 3)
                x_ = xr[c % 3]
                xn = 'xr%d' % (c % 3)
                o_ = ot[c % 2]
                on = 'ot%d' % (c % 2)
                S.dma('sp', y_[:], yT_s.ap()[c], writes=[yn])
                S.dma('sp', x_[:], x_d.ap()[c * 128:(c + 1) * 128, :], writes=[xn])
                for hf in range(2):
                    po = psO[(c % 2) * 2 + hf]
                    pon = 'psW%d' % ((c % 2) * 2 + hf)
                    for fc in range(8):
                        S.op('pe', lambda e: e.matmul(po[:], lhsT=y_[:, fc, :], rhs=Wo[:, fc, hf * 512:(hf + 1) * 512], start=(fc == 0), stop=(fc == 7)),
                             reads=[yn, 'Wo'], writes=[pon], signal=(fc == 7))
                    S.op('dve', lambda e: e.tensor_tensor(out=o_[:, hf * 512:(hf + 1) * 512], in0=po[:], in1=x_[:, hf * 512:(hf + 1) * 512], op=ALU.add),
                         reads=[pon, xn], writes=[on])
                S.dma('pool', out_d.ap()[c * 128:(c + 1) * 128, :], o_[:], reads=[on])
            S.finish('sp')
            S.finish('pool')
    return nc


_NC = None


def kernel(x, norm_g, w_in, conv_w, conv_b, filt_w1, filt_b1, filt_w2, filt_b2, filt_w3, filt_b3,
           filt_w4, filt_sin_freq, hyena_bias, q_norm_g, k_norm_g, attn_sink, hy_out_norm_g,
           attn_out_norm_g, w_out):
    global _NC
    f32 = lambda a: np.ascontiguousarray(np.asarray(a, dtype=np.float32))
    x = f32(x)
    C = _consts()
    shared = dict(
        w_in=f32(w_in)[0], w_out=f32(w_out)[0],
        gcol=np.ascontiguousarray(f32(norm_g)[0].reshape(8, 128).T),
        cwc=np.ascontiguousarray(np.concatenate([f32(conv_w)[0], f32(conv_b)], axis=0).reshape(4, 12, 128).transpose(2, 1, 0)).reshape(128, 48),
        filt_w1=f32(filt_w1)[0], filt_w2=f32(filt_w2)[0], filt_w3=f32(filt_w3)[0], filt_w4=f32(filt_w4)[0],
        fcols=np.ascontiguousarray(np.stack([f32(filt_b1)[0], f32(filt_b2)[0], f32(filt_b3)[0], f32(filt_sin_freq)[0]], axis=1)),
        hyena_bias=f32(hyena_bias)[0].reshape(1, 1024),
        q_norm_g=f32(q_norm_g)[0].reshape(1, 64), k_norm_g=f32(k_norm_g)[0].reshape(1, 64),
        attn_sink=f32(attn_sink)[0].reshape(1, 8),
        hy_out_norm_g=f32(hy_out_norm_g)[0].reshape(1, 512), attn_out_norm_g=f32(attn_out_norm_g)[0].reshape(1, 512),
        zT=C['zT'], decay=C['decay'],
        Fw_t=C['Fw_t'].reshape(32, 128, 32 * 128), Fh_t=C['Fh_t'].reshape(16, 128, 2 * 8 * 128), Bh_t=C['Bh_t'].reshape(8, 128, 32 * 128), jmat=C['jmat'],
        rope_cs=C['rope_cs'].reshape(128, 2 * 16 * 32), ident=C['ident'], masks=C['masks'].reshape(128, 256),
    )
    in_maps = []
    for c in range(NCORES):
        xc = x[c * NB:(c + 1) * NB].reshape(NB * L, D)
        m = dict(shared)
        m['x'] = np.ascontiguousarray(xc)
        m['xT'] = np.ascontiguousarray(xc.T)
        in_maps.append(m)
    if _NC is None:
        _NC = build_nc()
    res = run_bass_kernel_spmd(_NC, in_maps, core_ids=list(range(NCORES)))
    kernel.last_results = res
    out = np.concatenate([r['out'].reshape(NB, L, D) for r in res.results], axis=0)
    return out.astype(np.float32)
```

```python
import contextlib
import math
import numpy as np
import ml_dtypes
import concourse.bass as bass
import concourse.mybir as mybir
from concourse.bass_utils import run_bass_kernel_spmd

F32 = mybir.dt.float32
BF16 = mybir.dt.bfloat16
AF = mybir.ActivationFunctionType
ALU = mybir.AluOpType
AX = mybir.AxisListType

NCORES = 8
NB = 4
L = 2048
NT = 16
D = 1024
NFFT = 4096
EPS = 1e-6
DEBUG = False


class Sched:
    ENG = ('pe', 'act', 'dve', 'pool', 'sp')
    NRING = 8

    def __init__(self, nc, stack):
        self.nc = nc
        self.e = {'pe': nc.tensor, 'act': nc.scalar, 'dve': nc.vector,
                  'pool': nc.gpsimd, 'sp': nc.sync}
        self.sem = {}
        for k in self.ENG:
            self.sem[k] = stack.enter_context(nc.semaphore('s_' + k))
        self.cnt = {k: 0 for k in self.ENG}
        self.dq = {}
        for q in ('sp', 'pool', 'act'):
            for i in range(self.NRING):
                self.sem[('d', q, i)] = stack.enter_context(nc.semaphore('d_%s_%d' % (q, i)))
            self.dq[q] = 0
        self.seen = {k: {} for k in self.ENG}
        self.last_w = {}
        self.readers = {}
        self.all_dma = []

    def _wait(self, eng, key, val):
        if self.seen[eng].get(key, 0) >= val:
            return
        self.e[eng].wait_ge(self.sem[key], val)
        self.seen[eng][key] = val

    def _deps(self, eng, reads, writes):
        deps = {}

        def add(t, same_ok):
            if t is None:
                return
            key, val = t
            if key == eng and eng == 'pe':
                return
            if deps.get(key, 0) < val:
                deps[key] = val
        for b in reads:
            add(self.last_w.get(b), True)
        for b in writes:
            add(self.last_w.get(b), True)
            for t in self.readers.get(b, ()):
                add(t, False)
        for key, val in deps.items():
            self._wait(eng, key, val)

    def _record(self, ticket, reads, writes):
        for b in reads:
            self.readers.setdefault(b, []).append(ticket)
        for b in writes:
            self.last_w[b] = ticket
            self.readers[b] = []

    def op(self, eng, fn, reads=(), writes=(), signal=True):
        self._deps(eng, reads, writes)
        inst = fn(self.e[eng])
        if signal:
            self.cnt[eng] += 1
            inst.then_inc(self.sem[eng], 1)
            ticket = (eng, self.cnt[eng])
        else:
            ticket = (eng, self.cnt[eng] + 1)
        self._record(ticket, reads, writes)
        return ticket

    def dma(self, q, out, in_, reads=(), writes=(), **kw):
        i = self.dq[q]
        slot = i % self.NRING
        key = ('d', q, slot)
        if i >= self.NRING:
            self._wait(q, key, 16 * (i // self.NRING))
        self._deps(q, reads, writes)
        self.e[q].dma_start(out=out, in_=in_, **kw).then_inc(self.sem[key], 16)
        self.dq[q] = i + 1
        ticket = (key, 16 * (i // self.NRING + 1))
        self._record(ticket, reads, writes)
        self.all_dma.append(ticket)
        return ticket

    def _last_dma(self):
        last = {}
        for key, val in self.all_dma:
            if last.get(key, 0) < val:
                last[key] = val
        return last

    def barrier(self):
        last = self._last_dma()
        for eng in self.ENG:
            for other in self.ENG:
                if other != eng and self.cnt[other] > 0:
                    self._wait(eng, other, self.cnt[other])
            for key, val in last.items():
                self._wait(eng, key, val)
        self.all_dma = []
        self.last_w = {}
        self.readers = {}

    def finish(self, eng='sp'):
        for key, val in self._last_dma().items():
            self._wait(eng, key, val)


_CONST = None


def _consts():
    global _CONST
    if _CONST is not None:
        return _CONST
    bf = ml_dtypes.bfloat16
    m = np.arange(NFFT)
    pos = np.where(m < L, m, NFFT - m)
    pos_c = np.minimum(pos, L - 1)
    t = np.linspace(0.0, 1.0, L, dtype=np.float32)[:, None]
    bands = 16
    f = np.linspace(1e-4, bands - 1, bands, dtype=np.float32)[None, :]
    w = (2.0 * math.pi * np.arange(L, dtype=np.float32)[:, None] / L).astype(np.float32)
    z = np.concatenate([t, np.cos(f * w), -np.sin(f * w)], axis=-1).astype(np.float32)
    max_decay = math.log(1e-2) / 0.3
    min_decay = math.log(1e-2) / 1.5
    deltas = np.linspace(min_decay, max_decay, 512, dtype=np.float32)
    decay = np.exp(-t * np.abs(deltas)[None, :]).astype(np.float32)
    tt = np.arange(NFFT, dtype=np.int64)[:, None]
    g = np.arange(NFFT, dtype=np.int64)[None, :]
    gg = g % L
    ang = 2.0 * np.pi * ((gg * tt) % NFFT).astype(np.float64) / NFFT
    Fw = np.where(g < L, np.cos(ang), -np.sin(ang))
    Fw[:, L] = np.cos(np.pi * (np.arange(NFFT) % 2))
    Fw_t = Fw.reshape(32, 128, 32, 128).transpose(2, 1, 0, 3)
    Fw_t = np.ascontiguousarray(Fw_t).astype(bf)
    jj = np.arange(1024, dtype=np.int64)[:, None]
    ff = np.arange(L, dtype=np.int64)[None, :]
    ah = 2.0 * np.pi * ((ff * (2 * jj + 1)) % (2 * NFFT)).astype(np.float64) / (2 * NFFT)
    sgn = np.where(np.arange(1024) % 2 == 0, 1.0, -1.0)
    FhRe = np.cos(ah)
    FhIm = -np.sin(ah)
    FhIm[:, 0] = -sgn
    Fh = np.concatenate([FhRe, FhIm], axis=1)
    Fh_t = np.ascontiguousarray(Fh.reshape(8, 128, 2, 16, 128).transpose(3, 1, 2, 0, 4)).astype(bf)
    BhRe = np.cos(ah).T * (2.0 / NFFT)
    BhRe[0, :] = 1.0 / NFFT
    BhIm = np.sin(ah).T * (2.0 / NFFT)
    BhIm[0, :] = sgn / NFFT
    Bh = np.concatenate([BhRe, BhIm], axis=0)
    Bh_t = np.ascontiguousarray(Bh.reshape(32, 128, 8, 128).transpose(2, 1, 0, 3)).astype(bf)
    jmat = np.ascontiguousarray(np.eye(128, dtype=np.float32)[::-1]).astype(bf)
    half = 32
    inv = (10000.0 ** (-np.arange(half, dtype=np.float32) / half)).astype(np.float32)
    ang = np.arange(L, dtype=np.float32)[:, None] * inv[None, :]
    cos = np.cos(ang).astype(np.float32).reshape(NT, 128, half).transpose(1, 0, 2)
    sin = np.sin(ang).astype(np.float32).reshape(NT, 128, half).transpose(1, 0, 2)
    rope_cs = np.ascontiguousarray(np.stack([cos, sin], axis=1))
    ident = np.eye(128, dtype=np.float32).astype(bf)
    s_ = np.arange(128)[:, None]
    q_ = np.arange(128)[None, :]
    maskL = (q_ <= s_).astype(np.float32).astype(bf)
    maskU = (s_ <= q_).astype(np.float32).astype(bf)
    masks = np.ascontiguousarray(np.stack([maskL, maskU], axis=1))
    _CONST = dict(zT=np.ascontiguousarray(z.T), decay=decay, Fw_t=Fw_t, Fh_t=Fh_t, Bh_t=Bh_t, jmat=jmat,
                  rope_cs=rope_cs, ident=ident, masks=masks)
    return _CONST


def build_nc():
    nc = bass.Bass('TRN2', target_bir_lowering=False)

    def din(name, shape, dt=F32):
        return nc.dram_tensor(name, list(shape), dt, kind='ExternalInput')

    def dscr(name, shape, dt):
        return nc.dram_tensor(name, list(shape), dt, kind='ExternalOutput' if DEBUG else 'Internal')

    xT_d = din('xT', [D, NB * L])
    x_d = din('x', [NB * L, D])
    win_d = din('w_in', [D, 3328])
    wout_d = din('w_out', [D, D])
    gcol_d = din('gcol', [128, 8])
    cwc_d = din('cwc', [128, 48])
    fw1_d = din('filt_w1', [33, 64])
    fw2_d = din('filt_w2', [64, 64])
    fw3_d = din('filt_w3', [64, 64])
    fw4_d = din('filt_w4', [64, 2048])
    fcol_d = din('fcols', [64, 4])
    hb_d = din('hyena_bias', [1, 1024])
    qg_d = din('q_norm_g', [1, 64])
    kg_d = din('k_norm_g', [1, 64])
    sink_d = din('attn_sink', [1, 8])
    hyg_d = din('hy_out_norm_g', [1, 512])
    atg_d = din('attn_out_norm_g', [1, 512])
    zT_d = din('zT', [33, L])
    dec_d = din('decay', [L, 512])
    Fw_d = din('Fw_t', [32, 128, 32 * 128], BF16)
    Fh_d = din('Fh_t', [16, 128, 2 * 8 * 128], BF16)
    Bh_d = din('Bh_t', [8, 128, 32 * 128], BF16)
    jmat_d = din('jmat', [128, 128], BF16)
    rope_d = din('rope_cs', [128, 2 * 16 * 32])
    ident_d = din('ident', [128, 128], BF16)
    masks_d = din('masks', [128, 256], BF16)
    out_d = nc.dram_tensor('out', [NB * L, D], F32, kind='ExternalOutput')

    hT_s = dscr('hT_s', [NB, 128, 8 * 2050], BF16)
    U_s = dscr('U_s', [3, NB * L, 512], BF16)
    SG_s = dscr('SG_s', [NB * L, 512], BF16)
    yT_s = dscr('yT_s', [NB * NT, 128, 8, 128], BF16)
    Kh_s = dscr('Kh_s', [16, 128, 2, 1024], BF16)

    def bc(handle, ncols, parts=128, off=0):
        return bass.AP(handle, off, [[0, parts], [1, ncols]])

    with contextlib.ExitStack() as G:
        S = Sched(nc, G)

        def sb(st, name, shape, dt):
            return st.enter_context(nc.sbuf_tensor('sb_' + name, list(shape), dt))

        def ps(st, name, shape, dt=F32):
            return st.enter_context(nc.psum_tensor('ps_' + name, list(shape), dt))

        ident = sb(G, 'ident', [128, 128], BF16)
        ones = sb(G, 'ones', [128, 128], BF16)
        S.dma('sp', ident[:], ident_d.ap(), writes=['ident'])
        S.op('dve', lambda e: e.memset(ones[:], 1.0), writes=['ones'])

        def compute_hT(st, b, hT, tagp):
            xs = [sb(st, tagp + 'xs%d' % i, [128, 8, 512], F32) for i in range(2)]
            sq = sb(st, tagp + 'sq', [128, 8, 512], BF16)
            rs = sb(st, tagp + 'rs', [128, 512], F32)
            psr = ps(st, tagp + 'psr', [128, 512])
            return xs, sq, rs, psr

        def emit_hT_stage(st_, b, tt, hT, bufs, hname):
            xs, sq, rs, psr = bufs
            xb = xs[tt % 2]
            xn = 'xs%d' % (tt % 2)
            if st_ == 0:
                src = xT_d.ap()[:, b * L + tt * 512: b * L + (tt + 1) * 512].rearrange('(kc p) n -> p kc n', p=128)
                S.dma('sp', xb[:], src, writes=[xn])
                S.op('act', lambda e: e.activation(out=sq[:], in_=xb[:], func=AF.Square), reads=[xn], writes=['sq'])
            elif st_ == 1:
                for kc in range(8):
                    S.op('pe', lambda e: e.matmul(psr[:], lhsT=ones[:], rhs=sq[:, kc, :], start=(kc == 0), stop=(kc == 7)),
                         reads=['sq', 'ones'], writes=['psr'], signal=(kc == 7))
                S.op('act', lambda e: e.activation(out=rs[:], in_=psr[:], func=AF.Sqrt, scale=1.0 / D, bias=eps_t[:, 0:1]),
                     reads=['psr', 'eps'], writes=['rs'])
                S.op('dve', lambda e: e.reciprocal(out=rs[:], in_=rs[:]), reads=['rs'], writes=['rs'])
            else:
                for kc in range(8):
                    eng = 'dve' if kc % 2 == 0 else 'pool'
                    S.op(eng, lambda e: e.tensor_tensor(out=hT[:, kc, 1 + tt * 512: 1 + (tt + 1) * 512], in0=xb[:, kc, :], in1=rs[:], op=ALU.mult),
                         reads=[xn, 'rs'], writes=['%s%d' % (hname, kc)])

        def emit_hT_tile(b, tt, hT, bufs, hname):
            for st_ in range(3):
                emit_hT_stage(st_, b, tt, hT, bufs, hname)

        def emit_hT(b, hT, bufs, hname):
            for tt in range(4):
                emit_hT_tile(b, tt, hT, bufs, hname)

        eps_t = sb(G, 'eps_t', [128, 1], F32)
        S.op('dve', lambda e: e.memset(eps_t[:], EPS), writes=['eps'])

        with contextlib.ExitStack() as P:
            Wh = sb(P, 'Wh', [128, 8, 1536], BF16)
            hTs = [sb(P, 'hT_%d' % i, [128, 8, 2050], BF16) for i in range(2)]
            cwc = sb(P, 'cwc', [128, 12, 4], F32)
            with contextlib.ExitStack() as P0:
                gcol = sb(P0, 'gcol', [128, 8], F32)
                wst = [sb(P0, 'wst%d' % i, [128, 1536], F32) for i in range(2)]
                S.dma('sp', gcol[:], gcol_d.ap(), writes=['gcol'])
                S.dma('sp', cwc[:].rearrange('p c j -> p (c j)'), cwc_d.ap(), writes=['cwc'])
                for kc in range(8):
                    wb = wst[kc % 2]
                    wn = 'wst%d' % (kc % 2)
                    S.dma('sp', wb[:], win_d.ap()[kc * 128:(kc + 1) * 128, 0:1536], writes=[wn])
                    if kc % 2 == 0:
                        S.op('act', lambda e: e.activation(out=Wh[:, kc, :], in_=wb[:], func=AF.Copy, scale=gcol[:, kc:kc + 1]), reads=[wn, 'gcol'], writes=['Wh'])
                    else:
                        S.op('dve', lambda e: e.tensor_scalar(out=Wh[:, kc, :], in0=wb[:], scalar1=gcol[:, kc:kc + 1], scalar2=None, op0=ALU.mult), reads=[wn, 'gcol'], writes=['Wh'])
                S.barrier()
            bufs = compute_hT(P, 0, None, 'p1')
            pf = [sb(P, 'pf%d' % i, [128, 2050], F32) for i in range(2)]
            t1 = [sb(P, 't1_0', [128, 2048], F32)] * 2
            ucb = [sb(P, 'ucb%d' % i, [128, 2048], BF16) for i in range(8)]
            ust = [sb(P, 'ust%d' % i, [128, 512], BF16) for i in range(3)]
            psp = [ps(P, 'psp%d' % i, [128, 512]) for i in range(4)]
            pst = [ps(P, 'pst%d' % i, [128, 4, 128], BF16) for i in range(3)]
            HTN = [['hT%s%d' % ('ab'[q], k) for k in range(8)] for q in range(2)]
            for q in range(2):
                S.op('pool', lambda e: e.memset(hTs[q][:, :, 0:1], 0.0), writes=HTN[q])
                S.op('pool', lambda e: e.memset(hTs[q][:, :, 2049:2050], 0.0), writes=HTN[q])
            for i in range(2):
                S.op('pool', lambda e: e.memset(pf[i][:, 0:1], 0.0), writes=['pf%d' % i])
                S.op('pool', lambda e: e.memset(pf[i][:, 2049:2050], 0.0), writes=['pf%d' % i])
            cnt = {'pf': 0, 'pp': 0, 'pt': 0, 'us': 0}

            def group_mm(b, gi, n, slots):
                hT = hTs[b % 2]
                hq = 'ab'[b % 2]
                for c4 in range(4):
                    ct = gi * 4 + c4
                    k = cnt['pf'] % 2
                    cnt['pf'] += 1
                    pfb, pfn = pf[k], 'pf%d' % k
                    t1b, t1n = t1[0], 't1_0'
                    ub = ucb[(n % 2) * 4 + c4]
                    ubn = 'ucb%d' % ((n % 2) * 4 + c4)
                    for tt in range(4):
                        kp = cnt['pp'] % 4
                        cnt['pp'] += 1
                        pp, ppn = psp[kp], 'psp%d' % kp
                        for kc in range(8):
                            S.op('pe', lambda e: e.matmul(pp[:], lhsT=Wh[:, kc, ct * 128:(ct + 1) * 128], rhs=hT[:, kc, 1 + tt * 512: 1 + (tt + 1) * 512], start=(kc == 0), stop=(kc == 7)),
                                 reads=['hT%s%d' % (hq, kc), 'Wh'], writes=[ppn], signal=(kc == 7))
                        S.op('act', lambda e: e.copy(out=pfb[:, 1 + tt * 512: 1 + (tt + 1) * 512], in_=pp[:]), reads=[ppn], writes=[pfn])
                    S.op('act', lambda e: e.activation(out=t1b[:], in_=pfb[:, 1:2049], func=AF.Identity, scale=cwc[:, ct, 1:2], bias=cwc[:, ct, 3:4]),
                         reads=[pfn, 'cwc'], writes=[t1n])
                    S.op('dve', lambda e: e.scalar_tensor_tensor(out=t1b[:], in0=pfb[:, 0:2048], scalar=cwc[:, ct, 0:1], in1=t1b[:], op0=ALU.mult, op1=ALU.add),
                         reads=[pfn, 'cwc', t1n], writes=[t1n])
                    S.op('dve', lambda e: e.scalar_tensor_tensor(out=ub[:], in0=pfb[:, 2:2050], scalar=cwc[:, ct, 2:3], in1=t1b[:], op0=ALU.mult, op1=ALU.add),
                         reads=[pfn, 'cwc', t1n], writes=[ubn])
                    for fn_s in slots[c4]:
                        fn_s()

            def group_tr(b, gi, n, irange):
                for i in irange:
                    kt_ = cnt['pt'] % 3
                    cnt['pt'] += 1
                    pt, ptn = pst[kt_], 'pst%d' % kt_
                    for c4 in range(4):
                        S.op('pe', lambda e: e.transpose(pt[:, c4, :], ucb[(n % 2) * 4 + c4][:, i * 128:(i + 1) * 128], ident[:]),
                             reads=['ucb%d' % ((n % 2) * 4 + c4), 'ident'], writes=[ptn], signal=(c4 == 3))
                    ku = cnt['us'] % 3
                    cnt['us'] += 1
                    us, un = ust[ku], 'ust%d' % ku
                    S.op('act', lambda e: e.copy(out=us[:], in_=pt[:]), reads=[ptn], writes=[un])
                    S.dma('pool', U_s.ap()[gi, b * L + i * 128: b * L + (i + 1) * 128, :], us[:], reads=[un])

            pending = None
            emit_hT(0, hTs[0], bufs, 'hTa')
            for b in range(NB):
                S.dma('pool', hT_s.ap()[b], hTs[b % 2][:].rearrange('p k n -> p (k n)'), reads=HTN[b % 2])
                for gi in range(3):
                    n = b * 3 + gi
                    slots = [[], [], [], []]
                    if b + 1 < NB:
                        nh = hTs[(b + 1) % 2]
                        nhn = 'hT' + 'ab'[(b + 1) % 2]
                        tts = [(0, 1), (2,), (3,)][gi]
                        for ti, tt in enumerate(tts):
                            for st_ in range(3):
                                slots[min(3, ti + st_)].append(lambda st_=st_, tt=tt: emit_hT_stage(st_, b + 1, tt, nh, bufs, nhn))
                    if pending is not None:
                        for q4 in range(4):
                            slots[q4].append(lambda q4=q4, pd=pending: group_tr(*pd, range(q4 * 4, q4 * 4 + 4)))
                    group_mm(b, gi, n, slots)
                    pending = (b, gi, n)
            group_tr(*pending, range(NT))
            S.barrier()

        with contextlib.ExitStack() as P:
            Wr = sb(P, 'Wr', [128, 8, 1792], BF16)
            hT = sb(P, 'hT2', [128, 8, 2050], BF16)
            ropeT = sb(P, 'ropeT', [128, 8, 16, 32], F32)
            esink = sb(P, 'esink', [128, 8], F32)
            atg = sb(P, 'atg', [128, 512], F32)
            masks = sb(P, 'masks', [128, 2, 128], BF16)
            mhalf = sb(P, 'mhalf', [128, 16], F32)
            S.op('pool', lambda e: e.memset(mhalf[:], -0.5), writes=['mhalf'])
            with contextlib.ExitStack() as P0:
                gcol = sb(P0, 'gcol2', [128, 8], F32)
                wst = [sb(P0, 'wsr%d' % i, [128, 1792], F32) for i in range(2)]
                rcs = sb(P0, 'rcs', [128, 2, 16, 32], F32)
                qkg = sb(P0, 'qkg', [128, 2, 64], F32)
                S.dma('sp', gcol[:], gcol_d.ap(), writes=['gcol'])
                S.dma('sp', rcs[:].rearrange('p a i j -> p (a i j)'), rope_d.ap(), writes=['rcs'])
                S.dma('sp', qkg[:, 0, :], bc(qg_d, 64), writes=['qkg'])
                S.dma('sp', qkg[:, 1, :], bc(kg_d, 64), writes=['qkg'])
                S.dma('sp', esink[:], bc(sink_d, 8), writes=['esink'])
                S.dma('sp', atg[:], bc(atg_d, 512), writes=['atg'])
                S.dma('sp', masks[:].rearrange('p a n -> p (a n)'), masks_d.ap(), writes=['masks'])
                S.op('act', lambda e: e.activation(out=esink[:], in_=esink[:], func=AF.Exp), reads=['esink'], writes=['esink'])
                for qk in range(2):
                    sc = 0.125 if qk == 0 else 1.0
                    for ti, (cs, half) in enumerate([(0, 0), (1, 1), (0, 1), (1, 0)]):
                        gsl = qkg[:, qk, half * 32:(half + 1) * 32].unsqueeze(1).broadcast_to([128, 16, 32])
                        S.op('dve', lambda e: e.scalar_tensor_tensor(out=ropeT[:, qk * 4 + ti, :, :], in0=rcs[:, cs, :, :], scalar=sc, in1=gsl, op0=ALU.mult, op1=ALU.mult),
                             reads=['rcs', 'qkg'], writes=['ropeT'])
                for kc in range(8):
                    wb = wst[kc % 2]
                    wn = 'wsr%d' % (kc % 2)
                    S.dma('sp', wb[:], win_d.ap()[kc * 128:(kc + 1) * 128, 1536:3328], writes=[wn])
                    S.op('act', lambda e: e.activation(out=Wr[:, kc, 0:512], in_=wb[:, 0:512], func=AF.Copy, scale=gcol[:, kc:kc + 1]),
                         reads=[wn, 'gcol'], writes=['Wr'])
                    S.op('act', lambda e: e.activation(out=Wr[:, kc, 512:1024].rearrange('p (g k d) -> p g k d', g=4, k=2),
                                                       in_=wb[:, 512:1024].rearrange('p (k g d) -> p g k d', k=2, g=4),
                                                       func=AF.Copy, scale=gcol[:, kc:kc + 1]),
                         reads=[wn, 'gcol'], writes=['Wr'])
                    S.op('act', lambda e: e.activation(out=Wr[:, kc, 1024:1792], in_=wb[:, 1024:1792], func=AF.Copy, scale=gcol[:, kc:kc + 1]),
                         reads=[wn, 'gcol'], writes=['Wr'])
                S.barrier()
            QT = sb(P, 'QT', [128, 2, 4, L], BF16)
            KT = sb(P, 'KT', [128, L], BF16)
            V = sb(P, 'V', [128, NT, 2, 65], BF16)
            sga = sb(P, 'sga', [128, NT, 512], BF16)
            two = lambda name, shape, dt: [sb(P, '%s%d' % (name, i), shape, dt) for i in range(2)]
            sgh = two('sgh', [128, 512], BF16)
            qraw = two('qraw', [128, 640], F32)
            qsq = two('qsq', [128, 640], F32)
            ssq = two('ssq', [128, 10], F32)
            qn = two('qn', [128, 640], F32)
            tq = [two('tq%d' % k, [128, 256], F32) for k in range(4)]
            tk = [two('tk%d' % k, [128, 64], F32) for k in range(4)]
            qb = two('qb', [128, 640], BF16)
            Pm = [[sb(P, 'Pm%d_%d' % (i, j), [128, 512], BF16) for j in range(3)] for i in range(2)]
            den = two('den', [128, 4], F32)
            ya = two('ya', [128, 512], F32)
            yq = two('yq', [128, 512], F32)
            ssa = two('ssa', [128, 1], F32)
            yab = two('yab', [128, 512], BF16)
            yts = two('yts', [128, 4, 128], BF16)
            Bk = [ps(P, 'B%d' % i, [128, 512]) for i in range(6)]
            PT = ps(P, 'PT', [128, 4, 128], BF16)
            PK = ps(P, 'PK', [128, 128], BF16)
            S.op('pool', lambda e: e.memset(V[:, :, :, 64:65], 1.0), writes=['V'])
            S.op('pool', lambda e: e.memset(QT[64:128, 0, :, :], 0.0), writes=['QT'])
            S.op('pool', lambda e: e.memset(QT[0:64, 1, :, :], 0.0), writes=['QT'])

            def grp(bank, bname, lt, wcols):
                for kc in range(8):
                    S.op('pe', lambda e: e.matmul(bank, lhsT=lt(kc), rhs=Wr[:, kc, wcols], start=(kc == 0), stop=(kc == 7)),
                         reads=['hT', 'Wr'], writes=[bname], signal=(kc == 7))

            def proj_pe(i):
                par = i % 2
                lt = lambda kc: hT[:, kc, 1 + i * 128: 1 + (i + 1) * 128]
                grp(Bk[4][:, 0:256], 'B4', lt, slice(1024, 1280))
                grp(Bk[2][:], 'B2', lt, slice(512, 1024))
                grp(Bk[0][:], 'B0', lt, slice(0, 512))
                grp(Bk[1][:], 'B1', lt, slice(1280, 1792))

            def proj_ew(b, i):
                par = i % 2
                kvb = Bk[4][:, 0:256]
                kvn = 'B4'
                qbk = Bk[2]
                qbn = 'B2'
                sfx = str(par)
                S.op('act', lambda e: e.copy(out=qraw[par][:, 512:640], in_=kvb[:, 0:128]), reads=[kvn], writes=['qrawk' + sfx])
                S.op('act', lambda e: e.copy(out=V[:, i, :, 0:64], in_=kvb[:, 128:256].rearrange('p (k d) -> p k d', k=2)), reads=[kvn], writes=['V'])
                S.op('act', lambda e: e.copy(out=qraw[par][:, 0:512], in_=qbk[:]), reads=[qbn], writes=['qrawq' + sfx])
                S.op('dve', lambda e: e.tensor_tensor(out=qsq[par][:], in0=qraw[par][:], in1=qraw[par][:], op=ALU.mult),
                     reads=['qrawq' + sfx, 'qrawk' + sfx], writes=['qsq' + sfx])
                S.op('dve', lambda e: e.tensor_reduce(out=ssq[par][:], in_=qsq[par][:].rearrange('p (h d) -> p h d', d=64), axis=AX.X, op=ALU.add),
                     reads=['qsq' + sfx], writes=['ssq' + sfx])
                S.op('act', lambda e: e.activation(out=ssq[par][:], in_=ssq[par][:], func=AF.Sqrt, scale=1.0 / 64, bias=eps_t[:, 0:1]),
                     reads=['ssq' + sfx, 'eps'], writes=['ssq' + sfx])
                S.op('dve', lambda e: e.reciprocal(out=ssq[par][:], in_=ssq[par][:]), reads=['ssq' + sfx], writes=['ssq' + sfx])
                S.op('dve', lambda e: e.tensor_tensor(out=qn[par][:].rearrange('p (h d) -> p h d', d=64), in0=qraw[par][:].rearrange('p (h d) -> p h d', d=64),
                                                      in1=ssq[par][:].unsqueeze(2).broadcast_to([128, 10, 64]), op=ALU.mult),
                     reads=['qrawq' + sfx, 'qrawk' + sfx, 'ssq' + sfx], writes=['qn' + sfx])
                sg = sgh[par]
                sgn = 'sgh' + sfx
                S.op('act', lambda e: e.activation(out=sg[:], in_=Bk[0][:], func=AF.Silu), reads=['B0'], writes=[sgn])
                S.dma('pool', SG_s.ap()[b * L + i * 128: b * L + (i + 1) * 128, :], sg[:], reads=[sgn])
                S.op('act', lambda e: e.activation(out=sga[:, i, :], in_=Bk[1][:], func=AF.Silu), reads=['B1'], writes=['sga'])
                for qk, (c0, nh, tt_) in enumerate([(0, 8, tq), (512, 2, tk)]):
                    v4 = qn[par][:, c0:c0 + nh * 64].rearrange('p (h t j) -> p h t j', t=2, j=32)
                    o4 = qb[par][:, c0:c0 + nh * 64].rearrange('p (h t j) -> p h t j', t=2, j=32)
                    q1 = v4[:, :, 0, :]
                    q2 = v4[:, :, 1, :]
                    tb = lambda ti: ropeT[:, qk * 4 + ti, i, :].unsqueeze(1).broadcast_to([128, nh, 32])
                    a = [tt_[k][par][:, 0:nh * 32].rearrange('p (h j) -> p h j', j=32) for k in range(4)]
                    an = ['t%d%d%s' % (qk, k, sfx) for k in range(4)]
                    qbn2 = 'qb%d%s' % (qk, sfx)
                    S.op('dve', lambda e: e.tensor_tensor(out=a[0], in0=q1, in1=tb(0), op=ALU.mult), reads=['qn' + sfx, 'ropeT'], writes=[an[0]])
                    S.op('pool', lambda e: e.tensor_tensor(out=a[1], in0=q2, in1=tb(1), op=ALU.mult), reads=['qn' + sfx, 'ropeT'], writes=[an[1]])
                    S.op('pool', lambda e: e.tensor_tensor(out=a[2], in0=q2, in1=tb(2), op=ALU.mult), reads=['qn' + sfx, 'ropeT'], writes=[an[2]])
                    S.op('dve', lambda e: e.tensor_tensor(out=a[3], in0=q1, in1=tb(3), op=ALU.mult), reads=['qn' + sfx, 'ropeT'], writes=[an[3]])
                    S.op('dve', lambda e: e.tensor_tensor(out=o4[:, :, 0, :], in0=a[0], in1=a[1], op=ALU.subtract), reads=[an[0], an[1]], writes=[qbn2 + 'a'])
                    S.op('pool', lambda e: e.tensor_tensor(out=o4[:, :, 1, :], in0=a[2], in1=a[3], op=ALU.add), reads=[an[2], an[3]], writes=[qbn2 + 'b'])

            def proj_tr(i):
                par = i % 2
                sfx = str(par)
                for g in range(4):
                    S.op('pe', lambda e: e.transpose(PT[:, g, :], qb[par][:, g * 128:(g + 1) * 128], ident[:]),
                         reads=['qb0%sa' % sfx, 'qb0%sb' % sfx, 'ident'], writes=['PT'], signal=(g == 3))
                S.op('pe', lambda e: e.transpose(PK[:], qb[par][:, 512:640], ident[:]),
                     reads=['qb1%sa' % sfx, 'qb1%sb' % sfx, 'ident'], writes=['PK'])
                S.op('act', lambda e: e.copy(out=QT[0:64, 0, :, i * 128:(i + 1) * 128], in_=PT[0:64, :, :]), reads=['PT'], writes=['QT'])
                S.op('act', lambda e: e.copy(out=QT[64:128, 1, :, i * 128:(i + 1) * 128], in_=PT[64:128, :, :]), reads=['PT'], writes=['QT'])
                S.op('act', lambda e: e.copy(out=KT[:, i * 128:(i + 1) * 128], in_=PK[:]), reads=['PK'], writes=['KT'])

            def s_jobs():
                jobs = []
                for u in range(2 * NT):
                    i, kv = divmod(u, 2)
                    ccs = [c for c in (i - 1, i, i + 1) if 0 <= c < NT]
                    for ci, c in enumerate(ccs):
                        jobs.append((u, ci, c, ci == len(ccs) - 1))
                return jobs

            def att_S1(k, job):
                u, ci, c, last = job
                i, kv = divmod(u, 2)
                pr = slice(kv * 64, (kv + 1) * 64)
                bi = k % 4
                S.op('pe', lambda e: e.matmul(Bk[bi][:].rearrange('p (g q) -> p g q', g=4), lhsT=KT[:, c * 128:(c + 1) * 128], rhs=QT[:, kv, :, i * 128:(i + 1) * 128], start=True, stop=True),
                     reads=['KT', 'QT'], writes=['B%d' % bi])

            def att_E1(k, job):
                u, ci, c, last = job
                i, kv = divmod(u, 2)
                bi = k % 4
                pm = Pm[u % 2][ci]
                pmn = 'Pm%d_%d' % (u % 2, ci)
                S.op('act', lambda e: e.activation(out=pm[:], in_=Bk[bi][:], func=AF.Exp), reads=['B%d' % bi], writes=[pmn])
                if c != i:
                    mk = masks[:, 0 if c < i else 1, :].unsqueeze(1).broadcast_to([128, 4, 128])
                    S.op('dve', lambda e: e.tensor_tensor(out=pm[:].rearrange('p (g q) -> p g q', g=4), in0=pm[:].rearrange('p (g q) -> p g q', g=4), in1=mk, op=ALU.mult),
                         reads=[pmn, 'masks'], writes=[pmn])

            def att_PV(u):
                i, kv = divmod(u, 2)
                po = Bk[4 + u % 2][:, 0:260].rearrange('p (g d) -> p g d', g=4)
                pon = ['B%d' % (4 + u % 2)]
                lst = [c for c in (i - 1, i, i + 1) if 0 <= c < NT]
                for g in range(4):
                    for ci, c in enumerate(lst):
                        S.op('pe', lambda e: e.matmul(po[:, g, :], lhsT=Pm[u % 2][ci][:, g * 128:(g + 1) * 128], rhs=V[:, c, kv, :], start=(ci == 0), stop=(ci == len(lst) - 1)),
                             reads=['Pm%d_%d' % (u % 2, ci), 'V'], writes=pon, signal=(g == 3 and ci == len(lst) - 1))
                return po, pon

            def att_D(u, po, pon):
                i, kv = divmod(u, 2)
                d_ = den[u % 2]
                dn = 'den%d' % (u % 2)
                yan = 'ya%d_%d' % (i % 2, kv)
                S.op('dve', lambda e: e.tensor_tensor(out=d_[:], in0=po[:, :, 64], in1=esink[:, kv * 4:(kv + 1) * 4], op=ALU.add),
                     reads=pon + ['esink'], writes=[dn])
                S.op('dve', lambda e: e.reciprocal(out=d_[:], in_=d_[:]), reads=[dn], writes=[dn])
                S.op('dve', lambda e: e.tensor_tensor(out=ya[i % 2][:, kv * 256:(kv + 1) * 256].rearrange('p (g d) -> p g d', g=4), in0=po[:, :, 0:64],
                                                      in1=d_[:].unsqueeze(2).broadcast_to([128, 4, 64]), op=ALU.mult),
                     reads=pon + [dn], writes=[yan])

            def att_N(i):
                par = i % 2
                sfx = str(par)
                yr = ['ya%d_0' % par, 'ya%d_1' % par]
                S.op('dve', lambda e: e.scalar_tensor_tensor(out=yq[par][:], in0=ya[par][:], scalar=1.0, in1=ya[par][:], op0=ALU.mult, op1=ALU.mult, accum_out=ssa[par][:, 0:1]),
                     reads=yr, writes=['yq' + sfx, 'ssa' + sfx])
                S.op('act', lambda e: e.activation(out=ssa[par][:], in_=ssa[par][:], func=AF.Ln, scale=1.0 / 512, bias=eps_t[:, 0:1]),
                     reads=['ssa' + sfx, 'eps'], writes=['ssa' + sfx])
                S.op('act', lambda e: e.activation(out=ssa[par][:], in_=ssa[par][:], func=AF.Exp, scale=-0.5),
                     reads=['ssa' + sfx], writes=['ssa' + sfx])
                S.op('dve', lambda e: e.scalar_tensor_tensor(out=yq[par][:], in0=ya[par][:], scalar=ssa[par][:, 0:1], in1=atg[:], op0=ALU.mult, op1=ALU.mult),
                     reads=yr + ['ssa' + sfx, 'atg', 'yq' + sfx], writes=['yq' + sfx])
                S.op('pool', lambda e: e.tensor_tensor(out=yab[par][:], in0=yq[par][:], in1=sga[:, i, :], op=ALU.mult), reads=['yq' + sfx, 'sga'], writes=['yab' + sfx])

            def att_T(b, i):
                par = i % 2
                sfx = str(par)
                for g in range(4):
                    S.op('pe', lambda e: e.transpose(PT[:, g, :], yab[par][:, g * 128:(g + 1) * 128], ident[:]),
                         reads=['yab' + sfx, 'ident'], writes=['PT'], signal=(g == 3))
                yt = yts[par]
                ytn = 'yts' + sfx
                S.op('act', lambda e: e.copy(out=yt[:], in_=PT[:]), reads=['PT'], writes=[ytn])
                S.dma('pool', yT_s.ap()[b * NT + i, :, 4:8, :], yt[:], reads=[ytn])

            for b in range(NB):
                S.dma('sp', hT[:].rearrange('p k n -> p (k n)'), hT_s.ap()[b], writes=['hT'])
                for i in range(NT + 1):
                    if i < NT:
                        proj_pe(i)
                        proj_ew(b, i)
                    if i >= 1:
                        proj_tr(i - 1)
                jobs = s_jobs()
                AHEAD = 4
                for k in range(min(AHEAD, len(jobs))):
                    att_S1(k, jobs[k])
                deferred = []
                for k, job in enumerate(jobs):
                    att_E1(k, job)
                    if k + AHEAD < len(jobs):
                        att_S1(k + AHEAD, jobs[k + AHEAD])
                    for fn_d in deferred:
                        fn_d()
                    deferred = []
                    u, ci, c, last = job
                    if last:
                        po, pon = att_PV(u)
                        att_D(u, po, pon)
                        if u % 2 == 1:
                            deferred.append(lambda i_=u // 2: att_N(i_))
                            if u // 2 >= 1:
                                deferred.append(lambda i_=u // 2 - 1: att_T(b, i_))
                for fn_d in deferred:
                    fn_d()
                att_T(b, NT - 1)
            S.barrier()

        with contextlib.ExitStack() as P:
            ke = sb(P, 'ke', [128, 16, 1024], BF16)
            ko = sb(P, 'ko', [128, 16, 1024], BF16)
            with contextlib.ExitStack() as P0:
                zT = sb(P0, 'zT', [33, L], F32)
                w1 = sb(P0, 'w1', [33, 64], F32)
                w2 = sb(P0, 'w2', [64, 64], F32)
                w3 = sb(P0, 'w3', [64, 64], F32)
                w4 = sb(P0, 'w4', [64, 2048], F32)
                fcol = sb(P0, 'fcol', [64, 4], F32)
                fsc = sb(P0, 'fsc', [64, 4], F32)
                hb = sb(P0, 'hb', [1, 1024], F32)
                hA = sb(P0, 'hA', [64, L], F32)
                hB = sb(P0, 'hB', [64, L], F32)
                s1 = sb(P0, 's1', [64, 512], F32)
                s2 = sb(P0, 's2', [64, 512], F32)
                dct = [sb(P0, 'dct%d' % i, [128, 512], F32) for i in range(2)]
                kf = [sb(P0, 'kf%d' % i, [128, 512], F32) for i in range(2)]
                kb = [sb(P0, 'kb%d' % i, [128, 512], F32) for i in range(2)]
                psf = [ps(P0, 'psf%d' % i, [64, 512]) for i in range(2)]
                psk = [ps(P0, 'psk%d' % i, [128, 512]) for i in range(4)]
                S.dma('sp', zT[:], zT_d.ap(), writes=['zT'])
                S.dma('sp', w1[:], fw1_d.ap(), writes=['w1'])
                S.dma('sp', w2[:], fw2_d.ap(), writes=['w2'])
                S.dma('sp', w3[:], fw3_d.ap(), writes=['w3'])
                S.dma('sp', w4[:], fw4_d.ap(), writes=['w4'])
                S.dma('sp', fcol[:], fcol_d.ap(), writes=['fcol'])
                S.dma('sp', hb[:], hb_d.ap(), writes=['hb'])
                S.op('dve', lambda e: e.tensor_scalar(out=fsc[:, 0:1], in0=fcol[:, 3:4], scalar1=1.0 / 3.0, scalar2=None, op0=ALU.mult),
                     reads=['fcol'], writes=['fsc'])
                S.op('dve', lambda e: e.tensor_scalar(out=fsc[:, 1:4], in0=fcol[:, 0:3], scalar1=fsc[:, 0:1], scalar2=None, op0=ALU.mult),
                     reads=['fcol', 'fsc'], writes=['fsc'])
                layers = [(w1, 'w1', zT, 'zT', 33, hA, 'hA'), (w2, 'w2', hA, 'hA', 64, hB, 'hB'), (w3, 'w3', hB, 'hB', 64, hA, 'hA')]
                for li, (wt, wn, src, sn, kk, dst, dn) in enumerate(layers):
                    for ct in range(4):
                        pf = psf[ct % 2]
                        pfn = 'psf%d' % (ct % 2)
                        cs = slice(ct * 512, (ct + 1) * 512)
                        S.op('pe', lambda e: e.matmul(pf[:], lhsT=wt[0:kk, :], rhs=src[0:kk, cs], start=True, stop=True), reads=[wn, sn], writes=[pfn])
                        S.op('act', lambda e: e.activation(out=s1[:], in_=pf[:], func=AF.Sin, scale=fsc[:, 0:1], bias=fsc[:, li + 1:li + 2]),
                             reads=[pfn, 'fsc'], writes=['s1'])
                        S.op('dve', lambda e: e.tensor_tensor(out=s2[:], in0=s1[:], in1=s1[:], op=ALU.mult), reads=['s1'], writes=['s2'])
                        S.op('dve', lambda e: e.tensor_scalar(out=s2[:], in0=s2[:], scalar1=-4.0, scalar2=3.0, op0=ALU.mult, op1=ALU.add), reads=['s2'], writes=['s2'])
                        S.op('dve', lambda e: e.tensor_tensor(out=dst[:, cs], in0=s2[:], in1=s1[:], op=ALU.mult), reads=['s1', 's2', sn], writes=[dn])
                h3 = hA
                w4v = w4[:].rearrange('p (o r c) -> p o r c', o=2, r=2)
                nk = 0
                for mc in range(16):
                    dc = dct[mc % 2]
                    dcn = 'dct%d' % (mc % 2)
                    S.dma('sp', dc[:], dec_d.ap()[mc * 128:(mc + 1) * 128, :], writes=[dcn])
                    for o in range(2):
                        pkf = psk[o * 2]
                        pkb = psk[o * 2 + 1]
                        f_ = kf[nk % 2]
                        b_ = kb[nk % 2]
                        fn_ = 'kf%d' % (nk % 2)
                        bn_ = 'kb%d' % (nk % 2)
                        nk += 1
                        S.op('pe', lambda e: e.matmul(pkf[:], lhsT=h3[:, mc * 128:(mc + 1) * 128], rhs=w4v[:, o, 0, :], start=True, stop=True), reads=['hA', 'w4'], writes=['psk%d' % (o * 2)])
                        S.op('pe', lambda e: e.matmul(pkb[:], lhsT=h3[:, mc * 128:(mc + 1) * 128], rhs=w4v[:, o, 1, :], start=True, stop=True), reads=['hA', 'w4'], writes=['psk%d' % (o * 2 + 1)])
                        S.op('dve', lambda e: e.tensor_tensor(out=f_[:], in0=pkf[:], in1=dc[:], op=ALU.mult), reads=['psk%d' % (o * 2), dcn], writes=[fn_])
                        S.op('dve', lambda e: e.tensor_tensor(out=b_[:], in0=pkb[:], in1=dc[:], op=ALU.mult), reads=['psk%d' % (o * 2 + 1), dcn], writes=[bn_])
                        if mc == 0:
                            S.op('dve', lambda e: e.memset(b_[0:1, :], 0.0), reads=[bn_], writes=[bn_])
                            S.op('dve', lambda e: e.tensor_tensor(out=f_[0:1, :], in0=f_[0:1, :], in1=hb[0:1, o * 512:(o + 1) * 512], op=ALU.add), reads=[fn_, 'hb'], writes=[fn_])
                        S.op('pool', lambda e: e.tensor_tensor(out=ke[:, mc, o * 512:(o + 1) * 512], in0=f_[:], in1=b_[:], op=ALU.add), reads=[fn_, bn_], writes=['ke'])
                        S.op('pool', lambda e: e.tensor_tensor(out=ko[:, mc, o * 512:(o + 1) * 512], in0=f_[:], in1=b_[:], op=ALU.subtract), reads=[fn_, bn_], writes=['ko'])
                S.barrier()
            fwt = [sb(P, 'fwk%d' % i, [128, 16, 128], BF16) for i in range(2)]
            kst = [sb(P, 'kst%d' % i, [128, 1024], BF16) for i in range(2)]
            psK = [ps(P, 'psK%d' % i, [128, 512]) for i in range(4)]
            psN = ps(P, 'psN', [1, 1024])
            for gt in range(32):
                fw = fwt[gt % 2]
                fn_ = 'fwk%d' % (gt % 2)
                ks = kst[gt % 2]
                ksn = 'kst%d' % (gt % 2)
                src, srcn = (ke, 'ke') if gt < 16 else (ko, 'ko')
                S.dma('sp', fw[:].rearrange('p m g -> p (m g)'), Fw_d.ap()[gt, :, 0:2048], writes=[fn_])
                for o in range(2):
                    pk = psK[(gt % 2) * 2 + o]
                    pkn = 'psK%d' % ((gt % 2) * 2 + o)
                    for mc in range(16):
                        S.op('pe', lambda e: e.matmul(pk[:], lhsT=fw[:, mc, :], rhs=src[:, mc, o * 512:(o + 1) * 512], start=(mc == 0), stop=(mc == 15)),
                             reads=[fn_, srcn], writes=[pkn], signal=(mc == 15))
                    if o == 0:
                        S.op('act', lambda e: e.copy(out=ks[:, 0:512], in_=pk[:]), reads=[pkn], writes=[ksn])
                    else:
                        S.op('dve', lambda e: e.tensor_copy(out=ks[:, 512:1024], in_=pk[:]), reads=[pkn], writes=[ksn])
                if gt == 16:
                    for o in range(2):
                        for mc in range(16):
                            S.op('pe', lambda e: e.matmul(psN[0:1, o * 512:(o + 1) * 512], lhsT=fw[:, mc, 0:1], rhs=ke[:, mc, o * 512:(o + 1) * 512], start=(mc == 0), stop=(mc == 15)),
                                 reads=[fn_, 'ke'], writes=['psN'], signal=(mc == 15))
                    S.op('act', lambda e: e.copy(out=ks[0:1, :], in_=psN[0:1, :]), reads=['psN', ksn], writes=[ksn])
                S.dma('pool', Kh_s.ap()[gt % 16, :, gt // 16, :], ks[:], reads=[ksn])
            S.barrier()

        with contextlib.ExitStack() as P:
            uv = sb(P, 'uv', [128, NT, 512], BF16)
            x1 = sb(P, 'x1', [128, NT, 512], BF16)
            x2 = sb(P, 'x2', [128, NT, 512], BF16)
            x1r = sb(P, 'x1r', [128, 8, 512], BF16)
            x2r = sb(P, 'x2r', [128, 8, 512], BF16)
            vp = sb(P, 'vp', [128, 8, 512], BF16)
            vm = sb(P, 'vm', [128, 8, 512], BF16)
            Yh = sb(P, 'Yh', [128, 32, 512], BF16)
            hyg = sb(P, 'hyg', [128, 512], F32)
            Jm = sb(P, 'Jm', [128, 128], BF16)
            fwt = [sb(P, 'fwt%d' % i, [128, 2, 8, 128], BF16) for i in range(3)]
            bwt = [sb(P, 'bwt%d' % i, [128, 32, 128], BF16) for i in range(3)]
            kt = [sb(P, 'kt%d' % i, [128, 2, 512], BF16) for i in range(3)]
            ta = sb(P, 'ta', [128, 512], F32)
            tb_ = sb(P, 'tb', [128, 512], F32)
            tc = sb(P, 'tc', [128, 512], F32)
            td = sb(P, 'td', [128, 512], F32)
            Ac = [sb(P, 'Ac%d' % i, [128, 512], F32) for i in range(2)]
            dd = [sb(P, 'dd%d' % i, [128, 512], F32) for i in range(2)]
            ss = [sb(P, 'ss%d' % i, [128, 512], F32) for i in range(2)]
            yq2 = [sb(P, 'yq2_%d' % i, [128, 512], F32) for i in range(2)]
            ssh2 = [sb(P, 'ssh2_%d' % i, [128, 1], F32) for i in range(2)]
            yhb2 = [sb(P, 'yhb2_%d' % i, [128, 512], BF16) for i in range(2)]
            sgt = [sb(P, 'sgt%d' % i, [128, 512], BF16) for i in range(2)]
            yts = [sb(P, 'yth%d' % i, [128, 4, 128], BF16) for i in range(2)]
            Q = [ps(P, 'Q%d' % i, [128, 512]) for i in range(4)]
            Rb = ps(P, 'Rb', [128, 512])
            PT4 = [ps(P, 'PT4_%d' % i, [128, 4, 128]) for i in range(2)]
            S.dma('sp', hyg[:], bc(hyg_d, 512), writes=['hyg'])
            S.dma('sp', Jm[:], jmat_d.ap(), writes=['Jm'])

            yhb_lo = [sb(P, 'yhblo%d' % i, [128, 512], BF16) for i in range(2)]

            def emit_T4(b, tok0, hl, src, srcn):
                sfx = str(hl)
                mv = ident if hl == 0 else Jm
                mvn = 'ident' if hl == 0 else 'Jm'
                for g in range(4):
                    S.op('pe', lambda e: e.matmul(PT4[hl][:, g, :], lhsT=src[:, g * 128:(g + 1) * 128], rhs=mv[:], start=True, stop=True),
                         reads=[srcn, mvn], writes=['PT4' + sfx], signal=(g == 3))
                yt = yts[hl]
                ytn = 'yth' + sfx
                S.op('act', lambda e: e.copy(out=yt[:], in_=PT4[hl][:]), reads=['PT4' + sfx], writes=[ytn])
                S.dma('pool', yT_s.ap()[tok0 // 128, :, 0:4, :], yt[:], reads=[ytn])

            nf = 0
            nb_ = 0
            nq = 0
            def load_in(b, which):
                t_, tn = [(uv, 'uv'), (x1, 'x1'), (x2, 'x2')][which]
                S.dma('sp', t_[:], U_s.ap()[which, b * L:(b + 1) * L, :].rearrange('(i p) c -> p i c', p=128), writes=[tn])

            def prep(which):
                t_, tn = [(uv, 'uv'), (x1, 'x1'), (x2, 'x2')][which]
                for a_ in range(8):
                    c = 7 - a_
                    qb_ = Q[nqc[0] % 4]
                    qn_ = 'Q%d' % (nqc[0] % 4)
                    nqc[0] += 1
                    S.op('pe', lambda e: e.matmul(qb_[:], lhsT=Jm[:], rhs=t_[:, c, :], start=True, stop=True), reads=['Jm', tn], writes=[qn_])
                    if which == 0:
                        S.op('dve', lambda e: e.tensor_tensor(out=vp[:, a_, :], in0=qb_[:], in1=uv[:, 8 + a_, :], op=ALU.add), reads=[qn_, 'uv'], writes=['vp%d' % a_])
                        S.op('dve', lambda e: e.tensor_tensor(out=vm[:, a_, :], in0=uv[:, 8 + a_, :], in1=qb_[:], op=ALU.subtract), reads=[qn_, 'uv'], writes=['vm%d' % a_])
                    elif which == 1:
                        S.op('act', lambda e: e.copy(out=x1r[:, a_, :], in_=qb_[:]), reads=[qn_], writes=['x1r'])
                    else:
                        S.op('act', lambda e: e.copy(out=x2r[:, a_, :], in_=qb_[:]), reads=[qn_], writes=['x2r'])

            nqc = [0]
            for w_ in range(3):
                load_in(0, w_)
            for b in range(NB):
                prep(0)
                prep(1)
                for o in range(2):
                    for ft in range(16):
                        fw = fwt[nf % 3]
                        fn_ = 'fwt%d' % (nf % 3)
                        k_ = kt[nf % 3]
                        kn = 'kt%d' % (nf % 3)
                        pR = Q[(nf % 2) * 2]
                        pRn = 'Q%d' % ((nf % 2) * 2)
                        pI = Q[(nf % 2) * 2 + 1]
                        pIn = 'Q%d' % ((nf % 2) * 2 + 1)
                        nf += 1
                        S.dma('sp', fw[:].rearrange('p r m g -> p (r m g)'), Fh_d.ap()[ft], writes=[fn_])
                        if ft == 6 and o == 1 and b + 1 < NB:
                            load_in(b + 1, 1)
                        if ft == 6 and o == 0 and b >= 1:
                            load_in(b, 2)
                        S.dma('sp', k_[:], Kh_s.ap()[ft, :, :, o * 512:(o + 1) * 512], writes=[kn])
                        for mc in range(8):
                            S.op('pe', lambda e: e.matmul(pR[:], lhsT=fw[:, 0, mc, :], rhs=vp[:, mc, :], start=(mc == 0), stop=(mc == 7)),
                                 reads=[fn_, 'vp%d' % mc], writes=[pRn], signal=(mc == 7))
                        for mc in range(8):
                            S.op('pe', lambda e: e.matmul(pI[:], lhsT=fw[:, 1, mc, :], rhs=vm[:, mc, :], start=(mc == 0), stop=(mc == 7)),
                                 reads=[fn_, 'vm%d' % mc], writes=[pIn], signal=(mc == 7))
                        S.op('dve', lambda e: e.tensor_tensor(out=ta[:], in0=pR[:], in1=k_[:, 0, :], op=ALU.mult), reads=[pRn, kn], writes=['ta'])
                        S.op('dve', lambda e: e.tensor_tensor(out=tb_[:], in0=pI[:], in1=k_[:, 1, :], op=ALU.mult), reads=[pIn, kn], writes=['tb'])
                        S.op('pool', lambda e: e.tensor_tensor(out=Yh[:, ft, :], in0=ta[:], in1=tb_[:], op=ALU.subtract), reads=['ta', 'tb'], writes=['Yh%d' % ft])
                        S.op('dve', lambda e: e.tensor_tensor(out=tc[:], in0=pR[:], in1=k_[:, 1, :], op=ALU.mult), reads=[pRn, kn], writes=['tc'])
                        S.op('dve', lambda e: e.tensor_tensor(out=td[:], in0=pI[:], in1=k_[:, 0, :], op=ALU.mult), reads=[pIn, kn], writes=['td'])
                        S.op('pool', lambda e: e.tensor_tensor(out=Yh[:, 16 + ft, :], in0=tc[:], in1=td[:], op=ALU.add), reads=['tc', 'td'], writes=['Yh%d' % (16 + ft)])
                        if ft == 0:
                            S.op('pool', lambda e: e.tensor_copy(out=Yh[0:1, 0, :], in_=ta[0:1, :]), reads=['ta', 'Yh0'], writes=['Yh0'])
                            S.op('pool', lambda e: e.tensor_copy(out=Yh[0:1, 16, :], in_=tb_[0:1, :]), reads=['tb', 'Yh16'], writes=['Yh16'])
                    if o == 0:
                        prep(2)
                    pend = None
                    for jt in range(8):
                        bw = bwt[nb_ % 3]
                        bn = 'bwt%d' % (nb_ % 3)
                        par = nb_ % 2
                        sfx = str(par)
                        pA = Q[par * 2]
                        pAn = 'Q%d' % (par * 2)
                        pB = Q[par * 2 + 1]
                        pBn = 'Q%d' % (par * 2 + 1)
                        nb_ += 1
                        S.dma('sp', bw[:].rearrange('p g t -> p (g t)'), Bh_d.ap()[jt], writes=[bn])
                        if jt == 3 and o == 0 and b + 1 < NB:
                            load_in(b + 1, 0)
                        for gt in range(16):
                            S.op('pe', lambda e: e.matmul(pA[:], lhsT=bw[:, gt, :], rhs=Yh[:, gt, :], start=(gt == 0), stop=(gt == 15)),
                                 reads=[bn, 'Yh%d' % gt], writes=[pAn], signal=(gt == 15))
                        for gt in range(16, 32):
                            S.op('pe', lambda e: e.matmul(pB[:], lhsT=bw[:, gt, :], rhs=Yh[:, gt, :], start=(gt == 16), stop=(gt == 31)),
                                 reads=[bn, 'Yh%d' % gt], writes=[pBn], signal=(gt == 31))
                        S.op('act', lambda e: e.copy(out=Ac[par][:], in_=pA[:]), reads=[pAn], writes=['Ac' + sfx])
                        S.op('dve', lambda e: e.tensor_tensor(out=dd[par][:], in0=Ac[par][:], in1=pB[:], op=ALU.subtract), reads=['Ac' + sfx, pBn], writes=['dd' + sfx])
                        S.op('dve', lambda e: e.tensor_tensor(out=ss[par][:], in0=Ac[par][:], in1=pB[:], op=ALU.add), reads=['Ac' + sfx, pBn], writes=['ss' + sfx])
                        if o == 0:
                            S.op('pool', lambda e: e.tensor_tensor(out=dd[par][:], in0=dd[par][:], in1=x1[:, 8 + jt, :], op=ALU.mult), reads=['dd' + sfx, 'x1'], writes=['dd' + sfx])
                            S.op('pool', lambda e: e.tensor_tensor(out=ss[par][:], in0=ss[par][:], in1=x1r[:, jt, :], op=ALU.mult), reads=['ss' + sfx, 'x1r'], writes=['ss' + sfx])
                            S.op('dve', lambda e: e.tensor_tensor(out=vp[:, jt, :], in0=dd[par][:], in1=ss[par][:], op=ALU.add), reads=['dd' + sfx, 'ss' + sfx], writes=['vp%d' % jt])
                            S.op('dve', lambda e: e.tensor_tensor(out=vm[:, jt, :], in0=dd[par][:], in1=ss[par][:], op=ALU.subtract), reads=['dd' + sfx, 'ss' + sfx], writes=['vm%d' % jt])
                        else:
                            if pend is None:
                                pend = []
                            npend = []
                            for lag_, args in pend:
                                if lag_ <= 1:
                                    emit_T4(*args)
                                else:
                                    npend.append((lag_ - 1, args))
                            pend = npend
                            for hl in range(2):
                                hs = str(hl)
                                ysrc, ysn = (dd[par], 'dd' + sfx) if hl == 0 else (ss[par], 'ss' + sfx)
                                ck = 8 + jt if hl == 0 else 7 - jt
                                tok0 = b * L + ck * 128
                                xg, xgn = (x2[:, 8 + jt, :], 'x2') if hl == 0 else (x2r[:, jt, :], 'x2r')
                                sg = sgt[hl]
                                sgn = 'sgt' + hs
                                S.dma('sp', sg[:], SG_s.ap()[tok0: tok0 + 128, :], writes=[sgn])
                                S.op('pool', lambda e: e.tensor_tensor(out=ysrc[:], in0=ysrc[:], in1=xg, op=ALU.mult), reads=[ysn, xgn], writes=[ysn])
                                S.op('dve', lambda e: e.scalar_tensor_tensor(out=yq2[hl][:], in0=ysrc[:], scalar=1.0, in1=ysrc[:], op0=ALU.mult, op1=ALU.mult, accum_out=ssh2[hl][:, 0:1]),
                                     reads=[ysn], writes=['yq4' + hs, 'ssh' + hs])
                                S.op('act', lambda e: e.activation(out=ssh2[hl][:], in_=ssh2[hl][:], func=AF.Sqrt, scale=1.0 / 512, bias=eps_t[:, 0:1]),
                                     reads=['ssh' + hs, 'eps'], writes=['ssh' + hs])
                                S.op('dve', lambda e: e.reciprocal(out=ssh2[hl][:], in_=ssh2[hl][:]), reads=['ssh' + hs], writes=['ssh' + hs])
                                S.op('dve', lambda e: e.scalar_tensor_tensor(out=yq2[hl][:], in0=ysrc[:], scalar=ssh2[hl][:, 0:1], in1=hyg[:], op0=ALU.mult, op1=ALU.mult),
                                     reads=[ysn, 'ssh' + hs, 'hyg', 'yq4' + hs], writes=['yq4' + hs])
                                if hl == 0:
                                    S.op('pool', lambda e: e.tensor_tensor(out=yhb2[hl][:], in0=yq2[hl][:], in1=sg[:], op=ALU.mult), reads=['yq4' + hs, sgn], writes=['yhb' + hs])
                                else:
                                    S.op('pe', lambda e: e.matmul(Rb[:], lhsT=Jm[:], rhs=sg[:], start=True, stop=True), reads=['Jm', sgn], writes=['Rb'])
                                    ylo = yhb_lo[jt % 2]
                                    ylon = 'yhblo%d' % (jt % 2)
                                    S.op('dve', lambda e: e.tensor_tensor(out=ylo[:], in0=yq2[hl][:], in1=Rb[:], op=ALU.mult), reads=['yq4' + hs, 'Rb'], writes=[ylon])
                                if hl == 0:
                                    pend.append((1, (b, tok0, 0, yhb2[0], 'yhb0')))
                                else:
                                    pend.append((2, (b, tok0, 1, ylo, ylon)))
                    if pend is not None:
                        for lag_, args in pend:
                            emit_T4(*args)
            S.barrier()

        with contextlib.ExitStack() as P:
            Wo = sb(P, 'Wo', [128, 8, D], BF16)
            wst = [sb(P, 'wso%d' % i, [128, D], F32) for i in range(2)]
            yt = [sb(P, 'yt%d' % i, [128, 8, 128], BF16) for i in range(3)]
            xr = [sb(P, 'xr%d' % i, [128, D], F32) for i in range(3)]
            ot = [sb(P, 'ot%d' % i, [128, D], F32) for i in range(2)]
            psO = [ps(P, 'psW%d' % i, [128, 512]) for i in range(4)]
            for kc in range(8):
                wb = wst[kc % 2]
                wn = 'wso%d' % (kc % 2)
                S.dma('sp', wb[:], wout_d.ap()[kc * 128:(kc + 1) * 128, :], writes=[wn])
                if kc % 2 == 0:
                    S.op('act', lambda e: e.copy(out=Wo[:, kc, :], in_=wb[:]), reads=[wn], writes=['Wo'])
                else:
                    S.op('dve', lambda e: e.tensor_copy(out=Wo[:, kc, :], in_=wb[:]), reads=[wn], writes=['Wo'])
            for c in range(NB * NT):
                y_ = yt[c % 3]
                yn = 'yt%d' % (c % 3)
                x_ = xr[c % 3]
                xn = 'xr%d' % (c % 3)
                o_ = ot[c % 2]
                on = 'ot%d' % (c % 2)
                S.dma('sp', y_[:], yT_s.ap()[c], writes=[yn])
                S.dma('sp', x_[:], x_d.ap()[c * 128:(c + 1) * 128, :], writes=[xn])
                for hf in range(2):
                    po = psO[(c % 2) * 2 + hf]
                    pon = 'psW%d' % ((c % 2) * 2 + hf)
                    for fc in range(8):
                        S.op('pe', lambda e: e.matmul(po[:], lhsT=y_[:, fc, :], rhs=Wo[:, fc, hf * 512:(hf + 1) * 512], start=(fc == 0), stop=(fc == 7)),
                             reads=[yn, 'Wo'], writes=[pon], signal=(fc == 7))
                    S.op('dve', lambda e: e.tensor_tensor(out=o_[:, hf * 512:(hf + 1) * 512], in0=po[:], in1=x_[:, hf * 512:(hf + 1) * 512], op=ALU.add),
                         reads=[pon, xn], writes=[on])
                S.dma('pool', out_d.ap()[c * 128:(c + 1) * 128, :], o_[:], reads=[on])
            S.finish('sp')
            S.finish('pool')
    return nc


_NC = None


def kernel(x, norm_g, w_in, conv_w, conv_b, filt_w1, filt_b1, filt_w2, filt_b2, filt_w3, filt_b3,
           filt_w4, filt_sin_freq, hyena_bias, q_norm_g, k_norm_g, attn_sink, hy_out_norm_g,
           attn_out_norm_g, w_out):
    global _NC
    f32 = lambda a: np.ascontiguousarray(np.asarray(a, dtype=np.float32))
    x = f32(x)
    C = _consts()
    shared = dict(
        w_in=f32(w_in)[0], w_out=f32(w_out)[0],
        gcol=np.ascontiguousarray(f32(norm_g)[0].reshape(8, 128).T),
        cwc=np.ascontiguousarray(np.concatenate([f32(conv_w)[0], f32(conv_b)], axis=0).reshape(4, 12, 128).transpose(2, 1, 0)).reshape(128, 48),
        filt_w1=f32(filt_w1)[0], filt_w2=f32(filt_w2)[0], filt_w3=f32(filt_w3)[0], filt_w4=f32(filt_w4)[0],
        fcols=np.ascontiguousarray(np.stack([f32(filt_b1)[0], f32(filt_b2)[0], f32(filt_b3)[0], f32(filt_sin_freq)[0]], axis=1)),
        hyena_bias=f32(hyena_bias)[0].reshape(1, 1024),
        q_norm_g=f32(q_norm_g)[0].reshape(1, 64), k_norm_g=f32(k_norm_g)[0].reshape(1, 64),
        attn_sink=f32(attn_sink)[0].reshape(1, 8),
        hy_out_norm_g=f32(hy_out_norm_g)[0].reshape(1, 512), attn_out_norm_g=f32(attn_out_norm_g)[0].reshape(1, 512),
        zT=C['zT'], decay=C['decay'],
        Fw_t=C['Fw_t'].reshape(32, 128, 32 * 128), Fh_t=C['Fh_t'].reshape(16, 128, 2 * 8 * 128), Bh_t=C['Bh_t'].reshape(8, 128, 32 * 128), jmat=C['jmat'],
        rope_cs=C['rope_cs'].reshape(128, 2 * 16 * 32), ident=C['ident'], masks=C['masks'].reshape(128, 256),
    )
    in_maps = []
    for c in range(NCORES):
        xc = x[c * NB:(c + 1) * NB].reshape(NB * L, D)
        m = dict(shared)
        m['x'] = np.ascontiguousarray(xc)
        m['xT'] = np.ascontiguousarray(xc.T)
        in_maps.append(m)
    if _NC is None:
        _NC = build_nc()
    res = run_bass_kernel_spmd(_NC, in_maps, core_ids=list(range(NCORES)))
    kernel.last_results = res
    out = np.concatenate([r['out'].reshape(NB, L, D) for r in res.results], axis=0)
    return out.astype(np.float32)
```

```python
import contextlib
import math
import numpy as np
import ml_dtypes
import concourse.bass as bass
import concourse.mybir as mybir
from concourse.bass_utils import run_bass_kernel_spmd

F32 = mybir.dt.float32
BF16 = mybir.dt.bfloat16
AF = mybir.ActivationFunctionType
ALU = mybir.AluOpType
AX = mybir.AxisListType

NCORES = 8
NB = 4
L = 2048
NT = 16
D = 1024
NFFT = 4096
EPS = 1e-6
DEBUG = False


class Sched:
    ENG = ('pe', 'act', 'dve', 'pool', 'sp')
    NRING = 8

    def __init__(self, nc, stack):
        self.nc = nc
        self.e = {'pe': nc.tensor, 'act': nc.scalar, 'dve': nc.vector,
                  'pool': nc.gpsimd, 'sp': nc.sync}
        self.sem = {}
        for k in self.ENG:
            self.sem[k] = stack.enter_context(nc.semaphore('s_' + k))
        self.cnt = {k: 0 for k in self.ENG}
        self.dq = {}
        for q in ('sp', 'pool', 'act'):
            for i in range(self.NRING):
                self.sem[('d', q, i)] = stack.enter_context(nc.semaphore('d_%s_%d' % (q, i)))
            self.dq[q] = 0
        self.seen = {k: {} for k in self.ENG}
        self.last_w = {}
        self.readers = {}
        self.all_dma = []

    def _wait(self, eng, key, val):
        if self.seen[eng].get(key, 0) >= val:
            return
        self.e[eng].wait_ge(self.sem[key], val)
        self.seen[eng][key] = val

    def _deps(self, eng, reads, writes):
        deps = {}

        def add(t, same_ok):
            if t is None:
                return
            key, val = t
            if key == eng and eng == 'pe':
                return
            if deps.get(key, 0) < val:
                deps[key] = val
        for b in reads:
            add(self.last_w.get(b), True)
        for b in writes:
            add(self.last_w.get(b), True)
            for t in self.readers.get(b, ()):
                add(t, False)
        for key, val in deps.items():
            self._wait(eng, key, val)

    def _record(self, ticket, reads, writes):
        for b in reads:
            self.readers.setdefault(b, []).append(ticket)
        for b in writes:
            self.last_w[b] = ticket
            self.readers[b] = []

    def op(self, eng, fn, reads=(), writes=(), signal=True):
        self._deps(eng, reads, writes)
        inst = fn(self.e[eng])
        if signal:
            self.cnt[eng] += 1
            inst.then_inc(self.sem[eng], 1)
            ticket = (eng, self.cnt[eng])
        else:
            ticket = (eng, self.cnt[eng] + 1)
        self._record(ticket, reads, writes)
        return ticket

    def dma(self, q, out, in_, reads=(), writes=(), **kw):
        i = self.dq[q]
        slot = i % self.NRING
        key = ('d', q, slot)
        if i >= self.NRING:
            self._wait(q, key, 16 * (i // self.NRING))
        self._deps(q, reads, writes)
        self.e[q].dma_start(out=out, in_=in_, **kw).then_inc(self.sem[key], 16)
        self.dq[q] = i + 1
        ticket = (key, 16 * (i // self.NRING + 1))
        self._record(ticket, reads, writes)
        self.all_dma.append(ticket)
        return ticket

    def _last_dma(self):
        last = {}
        for key, val in self.all_dma:
            if last.get(key, 0) < val:
                last[key] = val
        return last

    def barrier(self):
        last = self._last_dma()
        for eng in self.ENG:
            for other in self.ENG:
                if other != eng and self.cnt[other] > 0:
                    self._wait(eng, other, self.cnt[other])
            for key, val in last.items():
                self._wait(eng, key, val)
        self.all_dma = []
        self.last_w = {}
        self.readers = {}

    def finish(self, eng='sp'):
        for key, val in self._last_dma().items():
            self._wait(eng, key, val)


_CONST = None


def _consts():
    global _CONST
    if _CONST is not None:
        return _CONST
    bf = ml_dtypes.bfloat16
    m = np.arange(NFFT)
    pos = np.where(m < L, m, NFFT - m)
    pos_c = np.minimum(pos, L - 1)
    t = np.linspace(0.0, 1.0, L, dtype=np.float32)[:, None]
    bands = 16
    f = np.linspace(1e-4, bands - 1, bands, dtype=np.float32)[None, :]
    w = (2.0 * math.pi * np.arange(L, dtype=np.float32)[:, None] / L).astype(np.float32)
    z = np.concatenate([t, np.cos(f * w), -np.sin(f * w)], axis=-1).astype(np.float32)
    max_decay = math.log(1e-2) / 0.3
    min_decay = math.log(1e-2) / 1.5
    deltas = np.linspace(min_decay, max_decay, 512, dtype=np.float32)
    decay = np.exp(-t * np.abs(deltas)[None, :]).astype(np.float32)
    tt = np.arange(NFFT, dtype=np.int64)[:, None]
    g = np.arange(NFFT, dtype=np.int64)[None, :]
    gg = g % L
    ang = 2.0 * np.pi * ((gg * tt) % NFFT).astype(np.float64) / NFFT
    Fw = np.where(g < L, np.cos(ang), -np.sin(ang))
    Fw[:, L] = np.cos(np.pi * (np.arange(NFFT) % 2))
    Fw_t = Fw.reshape(32, 128, 32, 128).transpose(2, 1, 0, 3)
    Fw_t = np.ascontiguousarray(Fw_t).astype(bf)
    jj = np.arange(1024, dtype=np.int64)[:, None]
    ff = np.arange(L, dtype=np.int64)[None, :]
    ah = 2.0 * np.pi * ((ff * (2 * jj + 1)) % (2 * NFFT)).astype(np.float64) / (2 * NFFT)
    sgn = np.where(np.arange(1024) % 2 == 0, 1.0, -1.0)
    FhRe = np.cos(ah)
    FhIm = -np.sin(ah)
    FhIm[:, 0] = -sgn
    Fh = np.concatenate([FhRe, FhIm], axis=1)
    Fh_t = np.ascontiguousarray(Fh.reshape(8, 128, 2, 16, 128).transpose(3, 1, 2, 0, 4)).astype(bf)
    BhRe = np.cos(ah).T * (2.0 / NFFT)
    BhRe[0, :] = 1.0 / NFFT
    BhIm = np.sin(ah).T * (2.0 / NFFT)
    BhIm[0, :] = sgn / NFFT
    Bh = np.concatenate([BhRe, BhIm], axis=0)
    Bh_t = np.ascontiguousarray(Bh.reshape(32, 128, 8, 128).transpose(2, 1, 0, 3)).astype(bf)
    jmat = np.ascontiguousarray(np.eye(128, dtype=np.float32)[::-1]).astype(bf)
    half = 32
    inv = (10000.0 ** (-np.arange(half, dtype=np.float32) / half)).astype(np.float32)
    ang = np.arange(L, dtype=np.float32)[:, None] * inv[None, :]
    cos = np.cos(ang).astype(np.float32).reshape(NT, 128, half).transpose(1, 0, 2)
    sin = np.sin(ang).astype(np.float32).reshape(NT, 128, half).transpose(1, 0, 2)
    rope_cs = np.ascontiguousarray(np.stack([cos, sin], axis=1))
    ident = np.eye(128, dtype=np.float32).astype(bf)
    s_ = np.arange(128)[:, None]
    q_ = np.arange(128)[None, :]
    maskL = (q_ <= s_).astype(np.float32).astype(bf)
    maskU = (s_ <= q_).astype(np.float32).astype(bf)
    masks = np.ascontiguousarray(np.stack([maskL, maskU], axis=1))
    _CONST = dict(zT=np.ascontiguousarray(z.T), decay=decay, Fw_t=Fw_t, Fh_t=Fh_t, Bh_t=Bh_t, jmat=jmat,
                  rope_cs=rope_cs, ident=ident, masks=masks)
    return _CONST


def build_nc():
    nc = bass.Bass('TRN2', target_bir_lowering=False)

    def din(name, shape, dt=F32):
        return nc.dram_tensor(name, list(shape), dt, kind='ExternalInput')

    def dscr(name, shape, dt):
        return nc.dram_tensor(name, list(shape), dt, kind='ExternalOutput' if DEBUG else 'Internal')

    xT_d = din('xT', [D, NB * L])
    x_d = din('x', [NB * L, D])
    win_d = din('w_in', [D, 3328])
    wout_d = din('w_out', [D, D])
    gcol_d = din('gcol', [128, 8])
    cwc_d = din('cwc', [128, 48])
    fw1_d = din('filt_w1', [33, 64])
    fw2_d = din('filt_w2', [64, 64])
    fw3_d = din('filt_w3', [64, 64])
    fw4_d = din('filt_w4', [64, 2048])
    fcol_d = din('fcols', [64, 4])
    hb_d = din('hyena_bias', [1, 1024])
    qg_d = din('q_norm_g', [1, 64])
    kg_d = din('k_norm_g', [1, 64])
    sink_d = din('attn_sink', [1, 8])
    hyg_d = din('hy_out_norm_g', [1, 512])
    atg_d = din('attn_out_norm_g', [1, 512])
    zT_d = din('zT', [33, L])
    dec_d = din('decay', [L, 512])
    Fw_d = din('Fw_t', [32, 128, 32 * 128], BF16)
    Fh_d = din('Fh_t', [16, 128, 2 * 8 * 128], BF16)
    Bh_d = din('Bh_t', [8, 128, 32 * 128], BF16)
    jmat_d = din('jmat', [128, 128], BF16)
    rope_d = din('rope_cs', [128, 2 * 16 * 32])
    ident_d = din('ident', [128, 128], BF16)
    masks_d = din('masks', [128, 256], BF16)
    out_d = nc.dram_tensor('out', [NB * L, D], F32, kind='ExternalOutput')

    hT_s = dscr('hT_s', [NB, 128, 8 * 2050], BF16)
    U_s = dscr('U_s', [3, NB * L, 512], BF16)
    SG_s = dscr('SG_s', [NB * L, 512], BF16)
    yT_s = dscr('yT_s', [NB * NT, 128, 8, 128], BF16)
    Kh_s = dscr('Kh_s', [16, 128, 2, 1024], BF16)

    def bc(handle, ncols, parts=128, off=0):
        return bass.AP(handle, off, [[0, parts], [1, ncols]])

    with contextlib.ExitStack() as G:
        S = Sched(nc, G)

        def sb(st, name, shape, dt):
            return st.enter_context(nc.sbuf_tensor('sb_' + name, list(shape), dt))

        def ps(st, name, shape, dt=F32):
            return st.enter_context(nc.psum_tensor('ps_' + name, list(shape), dt))

        ident = sb(G, 'ident', [128, 128], BF16)
        ones = sb(G, 'ones', [128, 128], BF16)
        S.dma('sp', ident[:], ident_d.ap(), writes=['ident'])
        S.op('dve', lambda e: e.memset(ones[:], 1.0), writes=['ones'])

        def compute_hT(st, b, hT, tagp):
            xs = [sb(st, tagp + 'xs%d' % i, [128, 8, 512], F32) for i in range(2)]
            sq = sb(st, tagp + 'sq', [128, 8, 512], BF16)
            rs = sb(st, tagp + 'rs', [128, 512], F32)
            psr = ps(st, tagp + 'psr', [128, 512])
            return xs, sq, rs, psr

        def emit_hT_stage(st_, b, tt, hT, bufs, hname):
            xs, sq, rs, psr = bufs
            xb = xs[tt % 2]
            xn = 'xs%d' % (tt % 2)
            if st_ == 0:
                src = xT_d.ap()[:, b * L + tt * 512: b * L + (tt + 1) * 512].rearrange('(kc p) n -> p kc n', p=128)
                S.dma('sp', xb[:], src, writes=[xn])
                S.op('act', lambda e: e.activation(out=sq[:], in_=xb[:], func=AF.Square), reads=[xn], writes=['sq'])
            elif st_ == 1:
                for kc in range(8):
                    S.op('pe', lambda e: e.matmul(psr[:], lhsT=ones[:], rhs=sq[:, kc, :], start=(kc == 0), stop=(kc == 7)),
                         reads=['sq', 'ones'], writes=['psr'], signal=(kc == 7))
                S.op('act', lambda e: e.activation(out=rs[:], in_=psr[:], func=AF.Sqrt, scale=1.0 / D, bias=eps_t[:, 0:1]),
                     reads=['psr', 'eps'], writes=['rs'])
                S.op('dve', lambda e: e.reciprocal(out=rs[:], in_=rs[:]), reads=['rs'], writes=['rs'])
            else:
                for kc in range(8):
                    eng = 'dve' if kc % 2 == 0 else 'pool'
                    S.op(eng, lambda e: e.tensor_tensor(out=hT[:, kc, 1 + tt * 512: 1 + (tt + 1) * 512], in0=xb[:, kc, :], in1=rs[:], op=ALU.mult),
                         reads=[xn, 'rs'], writes=['%s%d' % (hname, kc)])

        def emit_hT_tile(b, tt, hT, bufs, hname):
            for st_ in range(3):
                emit_hT_stage(st_, b, tt, hT, bufs, hname)

        def emit_hT(b, hT, bufs, hname):
            for tt in range(4):
                emit_hT_tile(b, tt, hT, bufs, hname)

        eps_t = sb(G, 'eps_t', [128, 1], F32)
        S.op('dve', lambda e: e.memset(eps_t[:], EPS), writes=['eps'])

        with contextlib.ExitStack() as P:
            Wh = sb(P, 'Wh', [128, 8, 1536], BF16)
            hTs = [sb(P, 'hT_%d' % i, [128, 8, 2050], BF16) for i in range(2)]
            cwc = sb(P, 'cwc', [128, 12, 4], F32)
            with contextlib.ExitStack() as P0:
                gcol = sb(P0, 'gcol', [128, 8], F32)
                wst = [sb(P0, 'wst%d' % i, [128, 1536], F32) for i in range(2)]
                S.dma('sp', gcol[:], gcol_d.ap(), writes=['gcol'])
                S.dma('sp', cwc[:].rearrange('p c j -> p (c j)'), cwc_d.ap(), writes=['cwc'])
                for kc in range(8):
                    wb = wst[kc % 2]
                    wn = 'wst%d' % (kc % 2)
                    S.dma('sp', wb[:], win_d.ap()[kc * 128:(kc + 1) * 128, 0:1536], writes=[wn])
                    if kc % 2 == 0:
                        S.op('act', lambda e: e.activation(out=Wh[:, kc, :], in_=wb[:], func=AF.Copy, scale=gcol[:, kc:kc + 1]), reads=[wn, 'gcol'], writes=['Wh'])
                    else:
                        S.op('dve', lambda e: e.tensor_scalar(out=Wh[:, kc, :], in0=wb[:], scalar1=gcol[:, kc:kc + 1], scalar2=None, op0=ALU.mult), reads=[wn, 'gcol'], writes=['Wh'])
                S.barrier()
            bufs = compute_hT(P, 0, None, 'p1')
            pf = [sb(P, 'pf%d' % i, [128, 2050], F32) for i in range(2)]
            t1 = [sb(P, 't1_0', [128, 2048], F32)] * 2
            ucb = [sb(P, 'ucb%d' % i, [128, 2048], BF16) for i in range(8)]
            ust = [sb(P, 'ust%d' % i, [128, 512], BF16) for i in range(3)]
            psp = [ps(P, 'psp%d' % i, [128, 512]) for i in range(4)]
            pst = [ps(P, 'pst%d' % i, [128, 4, 128], BF16) for i in range(3)]
            HTN = [['hT%s%d' % ('ab'[q], k) for k in range(8)] for q in range(2)]
            for q in range(2):
                S.op('pool', lambda e: e.memset(hTs[q][:, :, 0:1], 0.0), writes=HTN[q])
                S.op('pool', lambda e: e.memset(hTs[q][:, :, 2049:2050], 0.0), writes=HTN[q])
            for i in range(2):
                S.op('pool', lambda e: e.memset(pf[i][:, 0:1], 0.0), writes=['pf%d' % i])
                S.op('pool', lambda e: e.memset(pf[i][:, 2049:2050], 0.0), writes=['pf%d' % i])
            cnt = {'pf': 0, 'pp': 0, 'pt': 0, 'us': 0}

            def group_mm(b, gi, n, slots):
                hT = hTs[b % 2]
                hq = 'ab'[b % 2]
                for c4 in range(4):
                    ct = gi * 4 + c4
                    k = cnt['pf'] % 2
                    cnt['pf'] += 1
                    pfb, pfn = pf[k], 'pf%d' % k
                    t1b, t1n = t1[0], 't1_0'
                    ub = ucb[(n % 2) * 4 + c4]
                    ubn = 'ucb%d' % ((n % 2) * 4 + c4)
                    for tt in range(4):
                        kp = cnt['pp'] % 4
                        cnt['pp'] += 1
                        pp, ppn = psp[kp], 'psp%d' % kp
                        for kc in range(8):
                            S.op('pe', lambda e: e.matmul(pp[:], lhsT=Wh[:, kc, ct * 128:(ct + 1) * 128], rhs=hT[:, kc, 1 + tt * 512: 1 + (tt + 1) * 512], start=(kc == 0), stop=(kc == 7)),
                                 reads=['hT%s%d' % (hq, kc), 'Wh'], writes=[ppn], signal=(kc == 7))
                        S.op('act', lambda e: e.copy(out=pfb[:, 1 + tt * 512: 1 + (tt + 1) * 512], in_=pp[:]), reads=[ppn], writes=[pfn])
                    S.op('act', lambda e: e.activation(out=t1b[:], in_=pfb[:, 1:2049], func=AF.Identity, scale=cwc[:, ct, 1:2], bias=cwc[:, ct, 3:4]),
                         reads=[pfn, 'cwc'], writes=[t1n])
                    S.op('dve', lambda e: e.scalar_tensor_tensor(out=t1b[:], in0=pfb[:, 0:2048], scalar=cwc[:, ct, 0:1], in1=t1b[:], op0=ALU.mult, op1=ALU.add),
                         reads=[pfn, 'cwc', t1n], writes=[t1n])
                    S.op('dve', lambda e: e.scalar_tensor_tensor(out=ub[:], in0=pfb[:, 2:2050], scalar=cwc[:, ct, 2:3], in1=t1b[:], op0=ALU.mult, op1=ALU.add),
                         reads=[pfn, 'cwc', t1n], writes=[ubn])
                    for fn_s in slots[c4]:
                        fn_s()

            def group_tr(b, gi, n, irange):
                for i in irange:
                    kt_ = cnt['pt'] % 3
                    cnt['pt'] += 1
                    pt, ptn = pst[kt_], 'pst%d' % kt_
                    for c4 in range(4):
                        S.op('pe', lambda e: e.transpose(pt[:, c4, :], ucb[(n % 2) * 4 + c4][:, i * 128:(i + 1) * 128], ident[:]),
                             reads=['ucb%d' % ((n % 2) * 4 + c4), 'ident'], writes=[ptn], signal=(c4 == 3))
                    ku = cnt['us'] % 3
                    cnt['us'] += 1
                    us, un = ust[ku], 'ust%d' % ku
                    S.op('act', lambda e: e.copy(out=us[:], in_=pt[:]), reads=[ptn], writes=[un])
                    S.dma('pool', U_s.ap()[gi, b * L + i * 128: b * L + (i + 1) * 128, :], us[:], reads=[un])

            pending = None
            emit_hT(0, hTs[0], bufs, 'hTa')
            for b in range(NB):
                S.dma('pool', hT_s.ap()[b], hTs[b % 2][:].rearrange('p k n -> p (k n)'), reads=HTN[b % 2])
                for gi in range(3):
                    n = b * 3 + gi
                    slots = [[], [], [], []]
                    if b + 1 < NB:
                        nh = hTs[(b + 1) % 2]
                        nhn = 'hT' + 'ab'[(b + 1) % 2]
                        tts = [(0, 1), (2,), (3,)][gi]
                        for ti, tt in enumerate(tts):
                            for st_ in range(3):
                                slots[min(3, ti + st_)].append(lambda st_=st_, tt=tt: emit_hT_stage(st_, b + 1, tt, nh, bufs, nhn))
                    if pending is not None:
                        for q4 in range(4):
                            slots[q4].append(lambda q4=q4, pd=pending: group_tr(*pd, range(q4 * 4, q4 * 4 + 4)))
                    group_mm(b, gi, n, slots)
                    pending = (b, gi, n)
            group_tr(*pending, range(NT))
            S.barrier()

        with contextlib.ExitStack() as P:
            Wr = sb(P, 'Wr', [128, 8, 1792], BF16)
            hT = sb(P, 'hT2', [128, 8, 2050], BF16)
            ropeT = sb(P, 'ropeT', [128, 8, 16, 32], F32)
            esink = sb(P, 'esink', [128, 8], F32)
            atg = sb(P, 'atg', [128, 512], F32)
            masks = sb(P, 'masks', [128, 2, 128], BF16)
            mhalf = sb(P, 'mhalf', [128, 16], F32)
            S.op('pool', lambda e: e.memset(mhalf[:], -0.5), writes=['mhalf'])
            with contextlib.ExitStack() as P0:
                gcol = sb(P0, 'gcol2', [128, 8], F32)
                wst = [sb(P0, 'wsr%d' % i, [128, 1792], F32) for i in range(2)]
                rcs = sb(P0, 'rcs', [128, 2, 16, 32], F32)
                qkg = sb(P0, 'qkg', [128, 2, 64], F32)
                S.dma('sp', gcol[:], gcol_d.ap(), writes=['gcol'])
                S.dma('sp', rcs[:].rearrange('p a i j -> p (a i j)'), rope_d.ap(), writes=['rcs'])
                S.dma('sp', qkg[:, 0, :], bc(qg_d, 64), writes=['qkg'])
                S.dma('sp', qkg[:, 1, :], bc(kg_d, 64), writes=['qkg'])
                S.dma('sp', esink[:], bc(sink_d, 8), writes=['esink'])
                S.dma('sp', atg[:], bc(atg_d, 512), writes=['atg'])
                S.dma('sp', masks[:].rearrange('p a n -> p (a n)'), masks_d.ap(), writes=['masks'])
                S.op('act', lambda e: e.activation(out=esink[:], in_=esink[:], func=AF.Exp), reads=['esink'], writes=['esink'])
                for qk in range(2):
                    sc = 0.125 if qk == 0 else 1.0
                    for ti, (cs, half) in enumerate([(0, 0), (1, 1), (0, 1), (1, 0)]):
                        gsl = qkg[:, qk, half * 32:(half + 1) * 32].unsqueeze(1).broadcast_to([128, 16, 32])
                        S.op('dve', lambda e: e.scalar_tensor_tensor(out=ropeT[:, qk * 4 + ti, :, :], in0=rcs[:, cs, :, :], scalar=sc, in1=gsl, op0=ALU.mult, op1=ALU.mult),
                             reads=['rcs', 'qkg'], writes=['ropeT'])
                for kc in range(8):
                    wb = wst[kc % 2]
                    wn = 'wsr%d' % (kc % 2)
                    S.dma('sp', wb[:], win_d.ap()[kc * 128:(kc + 1) * 128, 1536:3328], writes=[wn])
                    S.op('act', lambda e: e.activation(out=Wr[:, kc, 0:512], in_=wb[:, 0:512], func=AF.Copy, scale=gcol[:, kc:kc + 1]),
                         reads=[wn, 'gcol'], writes=['Wr'])
                    S.op('act', lambda e: e.activation(out=Wr[:, kc, 512:1024].rearrange('p (g k d) -> p g k d', g=4, k=2),
                                                       in_=wb[:, 512:1024].rearrange('p (k g d) -> p g k d', k=2, g=4),
                                                       func=AF.Copy, scale=gcol[:, kc:kc + 1]),
                         reads=[wn, 'gcol'], writes=['Wr'])
                    S.op('act', lambda e: e.activation(out=Wr[:, kc, 1024:1792], in_=wb[:, 1024:1792], func=AF.Copy, scale=gcol[:, kc:kc + 1]),
                         reads=[wn, 'gcol'], writes=['Wr'])
                S.barrier()
            QT = sb(P, 'QT', [128, 2, 4, L], BF16)
            KT = sb(P, 'KT', [128, L], BF16)
            V = sb(P, 'V', [128, NT, 2, 65], BF16)
            sga = sb(P, 'sga', [128, NT, 512], BF16)
            two = lambda name, shape, dt: [sb(P, '%s%d' % (name, i), shape, dt) for i in range(2)]
            sgh = two('sgh', [128, 512], BF16)
            qraw = two('qraw', [128, 640], F32)
            qsq = two('qsq', [128, 640], F32)
            ssq = two('ssq', [128, 10], F32)
            qn = two('qn', [128, 640], F32)
            tq = [two('tq%d' % k, [128, 256], F32) for k in range(4)]
            tk = [two('tk%d' % k, [128, 64], F32) for k in range(4)]
            qb = two('qb', [128, 640], BF16)
            Pm = [[sb(P, 'Pm%d_%d' % (i, j), [128, 512], BF16) for j in range(3)] for i in range(2)]
            den = two('den', [128, 4], F32)
            ya = two('ya', [128, 512], F32)
            yq = two('yq', [128, 512], F32)
            ssa = two('ssa', [128, 1], F32)
            yab = two('yab', [128, 512], BF16)
            yts = two('yts', [128, 4, 128], BF16)
            Bk = [ps(P, 'B%d' % i, [128, 512]) for i in range(6)]
            PT = ps(P, 'PT', [128, 4, 128], BF16)
            PK = ps(P, 'PK', [128, 128], BF16)
            S.op('pool', lambda e: e.memset(V[:, :, :, 64:65], 1.0), writes=['V'])
            S.op('pool', lambda e: e.memset(QT[64:128, 0, :, :], 0.0), writes=['QT'])
            S.op('pool', lambda e: e.memset(QT[0:64, 1, :, :], 0.0), writes=['QT'])

            def grp(bank, bname, lt, wcols):
                for kc in range(8):
                    S.op('pe', lambda e: e.matmul(bank, lhsT=lt(kc), rhs=Wr[:, kc, wcols], start=(kc == 0), stop=(kc == 7)),
                         reads=['hT', 'Wr'], writes=[bname], signal=(kc == 7))

            def proj_pe(i):
                par = i % 2
                lt = lambda kc: hT[:, kc, 1 + i * 128: 1 + (i + 1) * 128]
                grp(Bk[4][:, 0:256], 'B4', lt, slice(1024, 1280))
                grp(Bk[2][:], 'B2', lt, slice(512, 1024))
                grp(Bk[0][:], 'B0', lt, slice(0, 512))
                grp(Bk[1][:], 'B1', lt, slice(1280, 1792))

            def proj_ew(b, i):
                par = i % 2
                kvb = Bk[4][:, 0:256]
                kvn = 'B4'
                qbk = Bk[2]
                qbn = 'B2'
                sfx = str(par)
                S.op('act', lambda e: e.copy(out=qraw[par][:, 512:640], in_=kvb[:, 0:128]), reads=[kvn], writes=['qrawk' + sfx])
                S.op('act', lambda e: e.copy(out=V[:, i, :, 0:64], in_=kvb[:, 128:256].rearrange('p (k d) -> p k d', k=2)), reads=[kvn], writes=['V'])
                S.op('act', lambda e: e.copy(out=qraw[par][:, 0:512], in_=qbk[:]), reads=[qbn], writes=['qrawq' + sfx])
                S.op('dve', lambda e: e.tensor_tensor(out=qsq[par][:], in0=qraw[par][:], in1=qraw[par][:], op=ALU.mult),
                     reads=['qrawq' + sfx, 'qrawk' + sfx], writes=['qsq' + sfx])
                S.op('dve', lambda e: e.tensor_reduce(out=ssq[par][:], in_=qsq[par][:].rearrange('p (h d) -> p h d', d=64), axis=AX.X, op=ALU.add),
                     reads=['qsq' + sfx], writes=['ssq' + sfx])
                S.op('act', lambda e: e.activation(out=ssq[par][:], in_=ssq[par][:], func=AF.Sqrt, scale=1.0 / 64, bias=eps_t[:, 0:1]),
                     reads=['ssq' + sfx, 'eps'], writes=['ssq' + sfx])
                S.op('dve', lambda e: e.reciprocal(out=ssq[par][:], in_=ssq[par][:]), reads=['ssq' + sfx], writes=['ssq' + sfx])
                S.op('dve', lambda e: e.tensor_tensor(out=qn[par][:].rearrange('p (h d) -> p h d', d=64), in0=qraw[par][:].rearrange('p (h d) -> p h d', d=64),
                                                      in1=ssq[par][:].unsqueeze(2).broadcast_to([128, 10, 64]), op=ALU.mult),
                     reads=['qrawq' + sfx, 'qrawk' + sfx, 'ssq' + sfx], writes=['qn' + sfx])
                sg = sgh[par]
                sgn = 'sgh' + sfx
                S.op('act', lambda e: e.activation(out=sg[:], in_=Bk[0][:], func=AF.Silu), reads=['B0'], writes=[sgn])
                S.dma('pool', SG_s.ap()[b * L + i * 128: b * L + (i + 1) * 128, :], sg[:], reads=[sgn])
                S.op('act', lambda e: e.activation(out=sga[:, i, :], in_=Bk[1][:], func=AF.Silu), reads=['B1'], writes=['sga'])
                for qk, (c0, nh, tt_) in enumerate([(0, 8, tq), (512, 2, tk)]):
                    v4 = qn[par][:, c0:c0 + nh * 64].rearrange('p (h t j) -> p h t j', t=2, j=32)
                    o4 = qb[par][:, c0:c0 + nh * 64].rearrange('p (h t j) -> p h t j', t=2, j=32)
                    q1 = v4[:, :, 0, :]
                    q2 = v4[:, :, 1, :]
                    tb = lambda ti: ropeT[:, qk * 4 + ti, i, :].unsqueeze(1).broadcast_to([128, nh, 32])
                    a = [tt_[k][par][:, 0:nh * 32].rearrange('p (h j) -> p h j', j=32) for k in range(4)]
                    an = ['t%d%d%s' % (qk, k, sfx) for k in range(4)]
                    qbn2 = 'qb%d%s' % (qk, sfx)
                    S.op('dve', lambda e: e.tensor_tensor(out=a[0], in0=q1, in1=tb(0), op=ALU.mult), reads=['qn' + sfx, 'ropeT'], writes=[an[0]])
                    S.op('pool', lambda e: e.tensor_tensor(out=a[1], in0=q2, in1=tb(1), op=ALU.mult), reads=['qn' + sfx, 'ropeT'], writes=[an[1]])
                    S.op('pool', lambda e: e.tensor_tensor(out=a[2], in0=q2, in1=tb(2), op=ALU.mult), reads=['qn' + sfx, 'ropeT'], writes=[an[2]])
                    S.op('dve', lambda e: e.tensor_tensor(out=a[3], in0=q1, in1=tb(3), op=ALU.mult), reads=['qn' + sfx, 'ropeT'], writes=[an[3]])
                    S.op('dve', lambda e: e.tensor_tensor(out=o4[:, :, 0, :], in0=a[0], in1=a[1], op=ALU.subtract), reads=[an[0], an[1]], writes=[qbn2 + 'a'])
                    S.op('pool', lambda e: e.tensor_tensor(out=o4[:, :, 1, :], in0=a[2], in1=a[3], op=ALU.add), reads=[an[2], an[3]], writes=[qbn2 + 'b'])

            def proj_tr(i):
                par = i % 2
                sfx = str(par)
                for g in range(4):
                    S.op('pe', lambda e: e.transpose(PT[:, g, :], qb[par][:, g * 128:(g + 1) * 128], ident[:]),
                         reads=['qb0%sa' % sfx, 'qb0%sb' % sfx, 'ident'], writes=['PT'], signal=(g == 3))
                S.op('pe', lambda e: e.transpose(PK[:], qb[par][:, 512:640], ident[:]),
                     reads=['qb1%sa' % sfx, 'qb1%sb' % sfx, 'ident'], writes=['PK'])
                S.op('act', lambda e: e.copy(out=QT[0:64, 0, :, i * 128:(i + 1) * 128], in_=PT[0:64, :, :]), reads=['PT'], writes=['QT'])
                S.op('act', lambda e: e.copy(out=QT[64:128, 1, :, i * 128:(i + 1) * 128], in_=PT[64:128, :, :]), reads=['PT'], writes=['QT'])
                S.op('act', lambda e: e.copy(out=KT[:, i * 128:(i + 1) * 128], in_=PK[:]), reads=['PK'], writes=['KT'])

            def s_jobs():
                jobs = []
                for u in range(2 * NT):
                    i, kv = divmod(u, 2)
                    ccs = [c for c in (i - 1, i, i + 1) if 0 <= c < NT]
                    for ci, c in enumerate(ccs):
                        jobs.append((u, ci, c, ci == len(ccs) - 1))
                return jobs

            def att_S1(k, job):
                u, ci, c, last = job
                i, kv = divmod(u, 2)
                pr = slice(kv * 64, (kv + 1) * 64)
                bi = k % 4
                S.op('pe', lambda e: e.matmul(Bk[bi][:].rearrange('p (g q) -> p g q', g=4), lhsT=KT[:, c * 128:(c + 1) * 128], rhs=QT[:, kv, :, i * 128:(i + 1) * 128], start=True, stop=True),
                     reads=['KT', 'QT'], writes=['B%d' % bi])

            def att_E1(k, job):
                u, ci, c, last = job
                i, kv = divmod(u, 2)
                bi = k % 4
                pm = Pm[u % 2][ci]
                pmn = 'Pm%d_%d' % (u % 2, ci)
                S.op('act', lambda e: e.activation(out=pm[:], in_=Bk[bi][:], func=AF.Exp), reads=['B%d' % bi], writes=[pmn])
                if c != i:
                    mk = masks[:, 0 if c < i else 1, :].unsqueeze(1).broadcast_to([128, 4, 128])
                    S.op('dve', lambda e: e.tensor_tensor(out=pm[:].rearrange('p (g q) -> p g q', g=4), in0=pm[:].rearrange('p (g q) -> p g q', g=4), in1=mk, op=ALU.mult),
                         reads=[pmn, 'masks'], writes=[pmn])

            def att_PV(u):
                i, kv = divmod(u, 2)
                po = Bk[4 + u % 2][:, 0:260].rearrange('p (g d) -> p g d', g=4)
                pon = ['B%d' % (4 + u % 2)]
                lst = [c for c in (i - 1, i, i + 1) if 0 <= c < NT]
                for g in range(4):
                    for ci, c in enumerate(lst):
                        S.op('pe', lambda e: e.matmul(po[:, g, :], lhsT=Pm[u % 2][ci][:, g * 128:(g + 1) * 128], rhs=V[:, c, kv, :], start=(ci == 0), stop=(ci == len(lst) - 1)),
                             reads=['Pm%d_%d' % (u % 2, ci), 'V'], writes=pon, signal=(g == 3 and ci == len(lst) - 1))
                return po, pon

            def att_D(u, po, pon):
                i, kv = divmod(u, 2)
                d_ = den[u % 2]
                dn = 'den%d' % (u % 2)
                yan = 'ya%d_%d' % (i % 2, kv)
                S.op('dve', lambda e: e.tensor_tensor(out=d_[:], in0=po[:, :, 64], in1=esink[:, kv * 4:(kv + 1) * 4], op=ALU.add),
                     reads=pon + ['esink'], writes=[dn])
                S.op('dve', lambda e: e.reciprocal(out=d_[:], in_=d_[:]), reads=[dn], writes=[dn])
                S.op('dve', lambda e: e.tensor_tensor(out=ya[i % 2][:, kv * 256:(kv + 1) * 256].rearrange('p (g d) -> p g d', g=4), in0=po[:, :, 0:64],
                                                      in1=d_[:].unsqueeze(2).broadcast_to([128, 4, 64]), op=ALU.mult),
                     reads=pon + [dn], writes=[yan])

            def att_N(i):
                par = i % 2
                sfx = str(par)
                yr = ['ya%d_0' % par, 'ya%d_1' % par]
                S.op('dve', lambda e: e.scalar_tensor_tensor(out=yq[par][:], in0=ya[par][:], scalar=1.0, in1=ya[par][:], op0=ALU.mult, op1=ALU.mult, accum_out=ssa[par][:, 0:1]),
                     reads=yr, writes=['yq' + sfx, 'ssa' + sfx])
                S.op('act', lambda e: e.activation(out=ssa[par][:], in_=ssa[par][:], func=AF.Ln, scale=1.0 / 512, bias=eps_t[:, 0:1]),
                     reads=['ssa' + sfx, 'eps'], writes=['ssa' + sfx])
                S.op('act', lambda e: e.activation(out=ssa[par][:], in_=ssa[par][:], func=AF.Exp, scale=-0.5),
                     reads=['ssa' + sfx], writes=['ssa' + sfx])
                S.op('dve', lambda e: e.scalar_tensor_tensor(out=yq[par][:], in0=ya[par][:], scalar=ssa[par][:, 0:1], in1=atg[:], op0=ALU.mult, op1=ALU.mult),
                     reads=yr + ['ssa' + sfx, 'atg', 'yq' + sfx], writes=['yq' + sfx])
                S.op('pool', lambda e: e.tensor_tensor(out=yab[par][:], in0=yq[par][:], in1=sga[:, i, :], op=ALU.mult), reads=['yq' + sfx, 'sga'], writes=['yab' + sfx])

            def att_T(b, i):
                par = i % 2
                sfx = str(par)
                for g in range(4):
                    S.op('pe', lambda e: e.transpose(PT[:, g, :], yab[par][:, g * 128:(g + 1) * 128], ident[:]),
                         reads=['yab' + sfx, 'ident'], writes=['PT'], signal=(g == 3))
                yt = yts[par]
                ytn = 'yts' + sfx
                S.op('act', lambda e: e.copy(out=yt[:], in_=PT[:]), reads=['PT'], writes=[ytn])
                S.dma('pool', yT_s.ap()[b * NT + i, :, 4:8, :], yt[:], reads=[ytn])

            for b in range(NB):
                S.dma('sp', hT[:].rearrange('p k n -> p (k n)'), hT_s.ap()[b], writes=['hT'])
                for i in range(NT + 1):
                    if i < NT:
                        proj_pe(i)
                        proj_ew(b, i)
                    if i >= 1:
                        proj_tr(i - 1)
                jobs = s_jobs()
                AHEAD = 4
                for k in range(min(AHEAD, len(jobs))):
                    att_S1(k, jobs[k])
                deferred = []
                for k, job in enumerate(jobs):
                    att_E1(k, job)
                    if k + AHEAD < len(jobs):
                        att_S1(k + AHEAD, jobs[k + AHEAD])
                    for fn_d in deferred:
                        fn_d()
                    deferred = []
                    u, ci, c, last = job
                    if last:
                        po, pon = att_PV(u)
                        att_D(u, po, pon)
                        if u % 2 == 1:
                            deferred.append(lambda i_=u // 2: att_N(i_))
                            if u // 2 >= 1:
                                deferred.append(lambda i_=u // 2 - 1: att_T(b, i_))
                for fn_d in deferred:
                    fn_d()
                att_T(b, NT - 1)
            S.barrier()

        with contextlib.ExitStack() as P:
            ke = sb(P, 'ke', [128, 16, 1024], BF16)
            ko = sb(P, 'ko', [128, 16, 1024], BF16)
            with contextlib.ExitStack() as P0:
                zT = sb(P0, 'zT', [33, L], F32)
                w1 = sb(P0, 'w1', [33, 64], F32)
                w2 = sb(P0, 'w2', [64, 64], F32)
                w3 = sb(P0, 'w3', [64, 64], F32)
                w4 = sb(P0, 'w4', [64, 2048], F32)
                fcol = sb(P0, 'fcol', [64, 4], F32)
                fsc = sb(P0, 'fsc', [64, 4], F32)
                hb = sb(P0, 'hb', [1, 1024], F32)
                hA = sb(P0, 'hA', [64, L], F32)
                hB = sb(P0, 'hB', [64, L], F32)
                s1 = sb(P0, 's1', [64, 512], F32)
                s2 = sb(P0, 's2', [64, 512], F32)
                dct = [sb(P0, 'dct%d' % i, [128, 512], F32) for i in range(2)]
                kf = [sb(P0, 'kf%d' % i, [128, 512], F32) for i in range(2)]
                kb = [sb(P0, 'kb%d' % i, [128, 512], F32) for i in range(2)]
                psf = [ps(P0, 'psf%d' % i, [64, 512]) for i in range(2)]
                psk = [ps(P0, 'psk%d' % i, [128, 512]) for i in range(4)]
                S.dma('sp', zT[:], zT_d.ap(), writes=['zT'])
                S.dma('sp', w1[:], fw1_d.ap(), writes=['w1'])
                S.dma('sp', w2[:], fw2_d.ap(), writes=['w2'])
                S.dma('sp', w3[:], fw3_d.ap(), writes=['w3'])
                S.dma('sp', w4[:], fw4_d.ap(), writes=['w4'])
                S.dma('sp', fcol[:], fcol_d.ap(), writes=['fcol'])
                S.dma('sp', hb[:], hb_d.ap(), writes=['hb'])
                S.op('dve', lambda e: e.tensor_scalar(out=fsc[:, 0:1], in0=fcol[:, 3:4], scalar1=1.0 / 3.0, scalar2=None, op0=ALU.mult),
                     reads=['fcol'], writes=['fsc'])
                S.op('dve', lambda e: e.tensor_scalar(out=fsc[:, 1:4], in0=fcol[:, 0:3], scalar1=fsc[:, 0:1], scalar2=None, op0=ALU.mult),
                     reads=['fcol', 'fsc'], writes=['fsc'])
                layers = [(w1, 'w1', zT, 'zT', 33, hA, 'hA'), (w2, 'w2', hA, 'hA', 64, hB, 'hB'), (w3, 'w3', hB, 'hB', 64, hA, 'hA')]
                for li, (wt, wn, src, sn, kk, dst, dn) in enumerate(layers):
                    for ct in range(4):
                        pf = psf[ct % 2]
                        pfn = 'psf%d' % (ct % 2)
                        cs = slice(ct * 512, (ct + 1) * 512)
                        S.op('pe', lambda e: e.matmul(pf[:], lhsT=wt[0:kk, :], rhs=src[0:kk, cs], start=True, stop=True), reads=[wn, sn], writes=[pfn])
                        S.op('act', lambda e: e.activation(out=s1[:], in_=pf[:], func=AF.Sin, scale=fsc[:, 0:1], bias=fsc[:, li + 1:li + 2]),
                             reads=[pfn, 'fsc'], writes=['s1'])
                        S.op('dve', lambda e: e.tensor_tensor(out=s2[:], in0=s1[:], in1=s1[:], op=ALU.mult), reads=['s1'], writes=['s2'])
                        S.op('dve', lambda e: e.tensor_scalar(out=s2[:], in0=s2[:], scalar1=-4.0, scalar2=3.0, op0=ALU.mult, op1=ALU.add), reads=['s2'], writes=['s2'])
                        S.op('dve', lambda e: e.tensor_tensor(out=dst[:, cs], in0=s2[:], in1=s1[:], op=ALU.mult), reads=['s1', 's2', sn], writes=[dn])
                h3 = hA
                w4v = w4[:].rearrange('p (o r c) -> p o r c', o=2, r=2)
                nk = 0
                for mc in range(16):
                    dc = dct[mc % 2]
                    dcn = 'dct%d' % (mc % 2)
                    S.dma('sp', dc[:], dec_d.ap()[mc * 128:(mc + 1) * 128, :], writes=[dcn])
                    for o in range(2):
                        pkf = psk[o * 2]
                        pkb = psk[o * 2 + 1]
                        f_ = kf[nk % 2]
                        b_ = kb[nk % 2]
                        fn_ = 'kf%d' % (nk % 2)
                        bn_ = 'kb%d' % (nk % 2)
                        nk += 1
                        S.op('pe', lambda e: e.matmul(pkf[:], lhsT=h3[:, mc * 128:(mc + 1) * 128], rhs=w4v[:, o, 0, :], start=True, stop=True), reads=['hA', 'w4'], writes=['psk%d' % (o * 2)])
                        S.op('pe', lambda e: e.matmul(pkb[:], lhsT=h3[:, mc * 128:(mc + 1) * 128], rhs=w4v[:, o, 1, :], start=True, stop=True), reads=['hA', 'w4'], writes=['psk%d' % (o * 2 + 1)])
                        S.op('dve', lambda e: e.tensor_tensor(out=f_[:], in0=pkf[:], in1=dc[:], op=ALU.mult), reads=['psk%d' % (o * 2), dcn], writes=[fn_])
                        S.op('dve', lambda e: e.tensor_tensor(out=b_[:], in0=pkb[:], in1=dc[:], op=ALU.mult), reads=['psk%d' % (o * 2 + 1), dcn], writes=[bn_])
                        if mc == 0:
                            S.op('dve', lambda e: e.memset(b_[0:1, :], 0.0), reads=[bn_], writes=[bn_])
                            S.op('dve', lambda e: e.tensor_tensor(out=f_[0:1, :], in0=f_[0:1, :], in1=hb[0:1, o * 512:(o + 1) * 512], op=ALU.add), reads=[fn_, 'hb'], writes=[fn_])
                        S.op('pool', lambda e: e.tensor_tensor(out=ke[:, mc, o * 512:(o + 1) * 512], in0=f_[:], in1=b_[:], op=ALU.add), reads=[fn_, bn_], writes=['ke'])
                        S.op('pool', lambda e: e.tensor_tensor(out=ko[:, mc, o * 512:(o + 1) * 512], in0=f_[:], in1=b_[:], op=ALU.subtract), reads=[fn_, bn_], writes=['ko'])
                S.barrier()
            fwt = [sb(P, 'fwk%d' % i, [128, 16, 128], BF16) for i in range(2)]
            kst = [sb(P, 'kst%d' % i, [128, 1024], BF16) for i in range(2)]
            psK = [ps(P, 'psK%d' % i, [128, 512]) for i in range(4)]
            psN = ps(P, 'psN', [1, 1024])
            for gt in range(32):
                fw = fwt[gt % 2]
                fn_ = 'fwk%d' % (gt % 2)
                ks = kst[gt % 2]
                ksn = 'kst%d' % (gt % 2)
                src, srcn = (ke, 'ke') if gt < 16 else (ko, 'ko')
                S.dma('sp', fw[:].rearrange('p m g -> p (m g)'), Fw_d.ap()[gt, :, 0:2048], writes=[fn_])
                for o in range(2):
                    pk = psK[(gt % 2) * 2 + o]
                    pkn = 'psK%d' % ((gt % 2) * 2 + o)
                    for mc in range(16):
                        S.op('pe', lambda e: e.matmul(pk[:], lhsT=fw[:, mc, :], rhs=src[:, mc, o * 512:(o + 1) * 512], start=(mc == 0), stop=(mc == 15)),
                             reads=[fn_, srcn], writes=[pkn], signal=(mc == 15))
                    if o == 0:
                        S.op('act', lambda e: e.copy(out=ks[:, 0:512], in_=pk[:]), reads=[pkn], writes=[ksn])
                    else:
                        S.op('dve', lambda e: e.tensor_copy(out=ks[:, 512:1024], in_=pk[:]), reads=[pkn], writes=[ksn])
                if gt == 16:
                    for o in range(2):
                        for mc in range(16):
                            S.op('pe', lambda e: e.matmul(psN[0:1, o * 512:(o + 1) * 512], lhsT=fw[:, mc, 0:1], rhs=ke[:, mc, o * 512:(o + 1) * 512], start=(mc == 0), stop=(mc == 15)),
                                 reads=[fn_, 'ke'], writes=['psN'], signal=(mc == 15))
                    S.op('act', lambda e: e.copy(out=ks[0:1, :], in_=psN[0:1, :]), reads=['psN', ksn], writes=[ksn])
                S.dma('pool', Kh_s.ap()[gt % 16, :, gt // 16, :], ks[:], reads=[ksn])
            S.barrier()

        with contextlib.ExitStack() as P:
            uv = sb(P, 'uv', [128, NT, 512], BF16)
            x1 = sb(P, 'x1', [128, NT, 512], BF16)
            x2 = sb(P, 'x2', [128, NT, 512], BF16)
            x1r = sb(P, 'x1r', [128, 8, 512], BF16)
            x2r = sb(P, 'x2r', [128, 8, 512], BF16)
            vp = sb(P, 'vp', [128, 8, 512], BF16)
            vm = sb(P, 'vm', [128, 8, 512], BF16)
            Yh = sb(P, 'Yh', [128, 32, 512], BF16)
            hyg = sb(P, 'hyg', [128, 512], F32)
            Jm = sb(P, 'Jm', [128, 128], BF16)
            fwt = [sb(P, 'fwt%d' % i, [128, 2, 8, 128], BF16) for i in range(3)]
            bwt = [sb(P, 'bwt%d' % i, [128, 32, 128], BF16) for i in range(3)]
            kt = [sb(P, 'kt%d' % i, [128, 2, 512], BF16) for i in range(3)]
            ta = sb(P, 'ta', [128, 512], F32)
            tb_ = sb(P, 'tb', [128, 512], F32)
            tc = sb(P, 'tc', [128, 512], F32)
            td = sb(P, 'td', [128, 512], F32)
            Ac = [sb(P, 'Ac%d' % i, [128, 512], F32) for i in range(2)]
            dd = [sb(P, 'dd%d' % i, [128, 512], F32) for i in range(2)]
            ss = [sb(P, 'ss%d' % i, [128, 512], F32) for i in range(2)]
            yq2 = [sb(P, 'yq2_%d' % i, [128, 512], F32) for i in range(2)]
            ssh2 = [sb(P, 'ssh2_%d' % i, [128, 1], F32) for i in range(2)]
            yhb2 = [sb(P, 'yhb2_%d' % i, [128, 512], BF16) for i in range(2)]
            sgt = [sb(P, 'sgt%d' % i, [128, 512], BF16) for i in range(2)]
            yts = [sb(P, 'yth%d' % i, [128, 4, 128], BF16) for i in range(2)]
            Q = [ps(P, 'Q%d' % i, [128, 512]) for i in range(4)]
            Rb = ps(P, 'Rb', [128, 512])
            PT4 = [ps(P, 'PT4_%d' % i, [128, 4, 128]) for i in range(2)]
            S.dma('sp', hyg[:], bc(hyg_d, 512), writes=['hyg'])
            S.dma('sp', Jm[:], jmat_d.ap(), writes=['Jm'])

            yhb_lo = [sb(P, 'yhblo%d' % i, [128, 512], BF16) for i in range(2)]

            def emit_T4(b, tok0, hl, src, srcn):
                sfx = str(hl)
                mv = ident if hl == 0 else Jm
                mvn = 'ident' if hl == 0 else 'Jm'
                for g in range(4):
                    S.op('pe', lambda e: e.matmul(PT4[hl][:, g, :], lhsT=src[:, g * 128:(g + 1) * 128], rhs=mv[:], start=True, stop=True),
                         reads=[srcn, mvn], writes=['PT4' + sfx], signal=(g == 3))
                yt = yts[hl]
                ytn = 'yth' + sfx
                S.op('act', lambda e: e.copy(out=yt[:], in_=PT4[hl][:]), reads=['PT4' + sfx], writes=[ytn])
                S.dma('pool', yT_s.ap()[tok0 // 128, :, 0:4, :], yt[:], reads=[ytn])

            nf = 0
            nb_ = 0
            nq = 0
            def load_in(b, which):
                t_, tn = [(uv, 'uv'), (x1, 'x1'), (x2, 'x2')][which]
                S.dma('sp', t_[:], U_s.ap()[which, b * L:(b + 1) * L, :].rearrange('(i p) c -> p i c', p=128), writes=[tn])

            def prep(which):
                t_, tn = [(uv, 'uv'), (x1, 'x1'), (x2, 'x2')][which]
                for a_ in range(8):
                    c = 7 - a_
                    qb_ = Q[nqc[0] % 4]
                    qn_ = 'Q%d' % (nqc[0] % 4)
                    nqc[0] += 1
                    S.op('pe', lambda e: e.matmul(qb_[:], lhsT=Jm[:], rhs=t_[:, c, :], start=True, stop=True), reads=['Jm', tn], writes=[qn_])
                    if which == 0:
                        S.op('dve', lambda e: e.tensor_tensor(out=vp[:, a_, :], in0=qb_[:], in1=uv[:, 8 + a_, :], op=ALU.add), reads=[qn_, 'uv'], writes=['vp%d' % a_])
                        S.op('dve', lambda e: e.tensor_tensor(out=vm[:, a_, :], in0=uv[:, 8 + a_, :], in1=qb_[:], op=ALU.subtract), reads=[qn_, 'uv'], writes=['vm%d' % a_])
                    elif which == 1:
                        S.op('act', lambda e: e.copy(out=x1r[:, a_, :], in_=qb_[:]), reads=[qn_], writes=['x1r'])
                    else:
                        S.op('act', lambda e: e.copy(out=x2r[:, a_, :], in_=qb_[:]), reads=[qn_], writes=['x2r'])

            nqc = [0]
            for w_ in range(3):
                load_in(0, w_)
            for b in range(NB):
                prep(0)
                prep(1)
                for o in range(2):
                    for ft in range(16):
                        fw = fwt[nf % 3]
                        fn_ = 'fwt%d' % (nf % 3)
                        k_ = kt[nf % 3]
                        kn = 'kt%d' % (nf % 3)
                        pR = Q[(nf % 2) * 2]
                        pRn = 'Q%d' % ((nf % 2) * 2)
                        pI = Q[(nf % 2) * 2 + 1]
                        pIn = 'Q%d' % ((nf % 2) * 2 + 1)
                        nf += 1
                        S.dma('sp', fw[:].rearrange('p r m g -> p (r m g)'), Fh_d.ap()[ft], writes=[fn_])
                        if ft == 6 and o == 1 and b + 1 < NB:
                            load_in(b + 1, 1)
                        if ft == 6 and o == 0 and b >= 1:
                            load_in(b, 2)
                        S.dma('sp', k_[:], Kh_s.ap()[ft, :, :, o * 512:(o + 1) * 512], writes=[kn])
                        for mc in range(8):
                            S.op('pe', lambda e: e.matmul(pR[:], lhsT=fw[:, 0, mc, :], rhs=vp[:, mc, :], start=(mc == 0), stop=(mc == 7)),
                                 reads=[fn_, 'vp%d' % mc], writes=[pRn], signal=(mc == 7))
                        for mc in range(8):
                            S.op('pe', lambda e: e.matmul(pI[:], lhsT=fw[:, 1, mc, :], rhs=vm[:, mc, :], start=(mc == 0), stop=(mc == 7)),
                                 reads=[fn_, 'vm%d' % mc], writes=[pIn], signal=(mc == 7))
                        S.op('dve', lambda e: e.tensor_tensor(out=ta[:], in0=pR[:], in1=k_[:, 0, :], op=ALU.mult), reads=[pRn, kn], writes=['ta'])
                        S.op('dve', lambda e: e.tensor_tensor(out=tb_[:], in0=pI[:], in1=k_[:, 1, :], op=ALU.mult), reads=[pIn, kn], writes=['tb'])
                        S.op('pool', lambda e: e.tensor_tensor(out=Yh[:, ft, :], in0=ta[:], in1=tb_[:], op=ALU.subtract), reads=['ta', 'tb'], writes=['Yh%d' % ft])
                        S.op('dve', lambda e: e.tensor_tensor(out=tc[:], in0=pR[:], in1=k_[:, 1, :], op=ALU.mult), reads=[pRn, kn], writes=['tc'])
                        S.op('dve', lambda e: e.tensor_tensor(out=td[:], in0=pI[:], in1=k_[:, 0, :], op=ALU.mult), reads=[pIn, kn], writes=['td'])
                        S.op('pool', lambda e: e.tensor_tensor(out=Yh[:, 16 + ft, :], in0=tc[:], in1=td[:], op=ALU.add), reads=['tc', 'td'], writes=['Yh%d' % (16 + ft)])
                        if ft == 0:
                            S.op('pool', lambda e: e.tensor_copy(out=Yh[0:1, 0, :], in_=ta[0:1, :]), reads=['ta', 'Yh0'], writes=['Yh0'])
                            S.op('pool', lambda e: e.tensor_copy(out=Yh[0:1, 16, :], in_=tb_[0:1, :]), reads=['tb', 'Yh16'], writes=['Yh16'])
                    if o == 0:
                        prep(2)
                    pend = None
                    for jt in range(8):
                        bw = bwt[nb_ % 3]
                        bn = 'bwt%d' % (nb_ % 3)
                        par = nb_ % 2
                        sfx = str(par)
                        pA = Q[par * 2]
                        pAn = 'Q%d' % (par * 2)
                        pB = Q[par * 2 + 1]
                        pBn = 'Q%d' % (par * 2 + 1)
                        nb_ += 1
                        S.dma('sp', bw[:].rearrange('p g t -> p (g t)'), Bh_d.ap()[jt], writes=[bn])
                        if jt == 3 and o == 0 and b + 1 < NB:
                            load_in(b + 1, 0)
                        for gt in range(16):
                            S.op('pe', lambda e: e.matmul(pA[:], lhsT=bw[:, gt, :], rhs=Yh[:, gt, :], start=(gt == 0), stop=(gt == 15)),
                                 reads=[bn, 'Yh%d' % gt], writes=[pAn], signal=(gt == 15))
                        for gt in range(16, 32):
                            S.op('pe', lambda e: e.matmul(pB[:], lhsT=bw[:, gt, :], rhs=Yh[:, gt, :], start=(gt == 16), stop=(gt == 31)),
                                 reads=[bn, 'Yh%d' % gt], writes=[pBn], signal=(gt == 31))
                        S.op('act', lambda e: e.copy(out=Ac[par][:], in_=pA[:]), reads=[pAn], writes=['Ac' + sfx])
                        S.op('dve', lambda e: e.tensor_tensor(out=dd[par][:], in0=Ac[par][:], in1=pB[:], op=ALU.subtract), reads=['Ac' + sfx, pBn], writes=['dd' + sfx])
                        S.op('dve', lambda e: e.tensor_tensor(out=ss[par][:], in0=Ac[par][:], in1=pB[:], op=ALU.add), reads=['Ac' + sfx, pBn], writes=['ss' + sfx])
                        if o == 0:
                            S.op('pool', lambda e: e.tensor_tensor(out=dd[par][:], in0=dd[par][:], in1=x1[:, 8 + jt, :], op=ALU.mult), reads=['dd' + sfx, 'x1'], writes=['dd' + sfx])
                            S.op('pool', lambda e: e.tensor_tensor(out=ss[par][:], in0=ss[par][:], in1=x1r[:, jt, :], op=ALU.mult), reads=['ss' + sfx, 'x1r'], writes=['ss' + sfx])
                            S.op('dve', lambda e: e.tensor_tensor(out=vp[:, jt, :], in0=dd[par][:], in1=ss[par][:], op=ALU.add), reads=['dd' + sfx, 'ss' + sfx], writes=['vp%d' % jt])
                            S.op('dve', lambda e: e.tensor_tensor(out=vm[:, jt, :], in0=dd[par][:], in1=ss[par][:], op=ALU.subtract), reads=['dd' + sfx, 'ss' + sfx], writes=['vm%d' % jt])
                        else:
                            if pend is None:
                                pend = []
                            npend = []
                            for lag_, args in pend:
                                if lag_ <= 1:
                                    emit_T4(*args)
                                else:
                                    npend.append((lag_ - 1, args))
                            pend = npend
                            for hl in range(2):
                                hs = str(hl)
                                ysrc, ysn = (dd[par], 'dd' + sfx) if hl == 0 else (ss[par], 'ss' + sfx)
                                ck = 8 + jt if hl == 0 else 7 - jt
                                tok0 = b * L + ck * 128
                                xg, xgn = (x2[:, 8 + jt, :], 'x2') if hl == 0 else (x2r[:, jt, :], 'x2r')
                                sg = sgt[hl]
                                sgn = 'sgt' + hs
                                S.dma('sp', sg[:], SG_s.ap()[tok0: tok0 + 128, :], writes=[sgn])
                                S.op('pool', lambda e: e.tensor_tensor(out=ysrc[:], in0=ysrc[:], in1=xg, op=ALU.mult), reads=[ysn, xgn], writes=[ysn])
                                S.op('dve', lambda e: e.scalar_tensor_tensor(out=yq2[hl][:], in0=ysrc[:], scalar=1.0, in1=ysrc[:], op0=ALU.mult, op1=ALU.mult, accum_out=ssh2[hl][:, 0:1]),
                                     reads=[ysn], writes=['yq4' + hs, 'ssh' + hs])
                                S.op('act', lambda e: e.activation(out=ssh2[hl][:], in_=ssh2[hl][:], func=AF.Sqrt, scale=1.0 / 512, bias=eps_t[:, 0:1]),
                                     reads=['ssh' + hs, 'eps'], writes=['ssh' + hs])
                                S.op('dve', lambda e: e.reciprocal(out=ssh2[hl][:], in_=ssh2[hl][:]), reads=['ssh' + hs], writes=['ssh' + hs])
                                S.op('dve', lambda e: e.scalar_tensor_tensor(out=yq2[hl][:], in0=ysrc[:], scalar=ssh2[hl][:, 0:1], in1=hyg[:], op0=ALU.mult, op1=ALU.mult),
                                     reads=[ysn, 'ssh' + hs, 'hyg', 'yq4' + hs], writes=['yq4' + hs])
                                if hl == 0:
                                    S.op('pool', lambda e: e.tensor_tensor(out=yhb2[hl][:], in0=yq2[hl][:], in1=sg[:], op=ALU.mult), reads=['yq4' + hs, sgn], writes=['yhb' + hs])
                                else:
                                    S.op('pe', lambda e: e.matmul(Rb[:], lhsT=Jm[:], rhs=sg[:], start=True, stop=True), reads=['Jm', sgn], writes=['Rb'])
                                    ylo = yhb_lo[jt % 2]
                                    ylon = 'yhblo%d' % (jt % 2)
                                    S.op('dve', lambda e: e.tensor_tensor(out=ylo[:], in0=yq2[hl][:], in1=Rb[:], op=ALU.mult), reads=['yq4' + hs, 'Rb'], writes=[ylon])
                                if hl == 0:
                                    pend.append((1, (b, tok0, 0, yhb2[0], 'yhb0')))
                                else:
                                    pend.append((2, (b, tok0, 1, ylo, ylon)))
                    if pend is not None:
                        for lag_, args in pend:
                            emit_T4(*args)
            S.barrier()

        with contextlib.ExitStack() as P:
            Wo = sb(P, 'Wo', [128, 8, D], BF16)
            wst = [sb(P, 'wso%d' % i, [128, D], F32) for i in range(2)]
            yt = [sb(P, 'yt%d' % i, [128, 8, 128], BF16) for i in range(3)]
            xr = [sb(P, 'xr%d' % i, [128, D], F32) for i in range(3)]
            ot = [sb(P, 'ot%d' % i, [128, D], F32) for i in range(2)]
            psO = [ps(P, 'psW%d' % i, [128, 512]) for i in range(4)]
            for kc in range(8):
                wb = wst[kc % 2]
                wn = 'wso%d' % (kc % 2)
                S.dma('sp', wb[:], wout_d.ap()[kc * 128:(kc + 1) * 128, :], writes=[wn])
                if kc % 2 == 0:
                    S.op('act', lambda e: e.copy(out=Wo[:, kc, :], in_=wb[:]), reads=[wn], writes=['Wo'])
                else:
                    S.op('dve', lambda e: e.tensor_copy(out=Wo[:, kc, :], in_=wb[:]), reads=[wn], writes=['Wo'])
            for c in range(NB * NT):
                y_ = yt[c % 3]
                yn = 'yt%d' % (c % 3)
                x_ = xr[c % 3]
                xn = 'xr%d' % (c % 3)
                o_ = ot[c % 2]
                on = 'ot%d' % (c % 2)
                S.dma('sp', y_[:], yT_s.ap()[c], writes=[yn])
                S.dma('sp', x_[:], x_d.ap()[c * 128:(c + 1) * 128, :], writes=[xn])
                for hf in range(2):
                    po = psO[(c % 2) * 2 + hf]
                    pon = 'psW%d' % ((c % 2) * 2 + hf)
                    for fc in range(8):
                        S.op('pe', lambda e: e.matmul(po[:], lhsT=y_[:, fc, :], rhs=Wo[:, fc, hf * 512:(hf + 1) * 512], start=(fc == 0), stop=(fc == 7)),
                             reads=[yn, 'Wo'], writes=[pon], signal=(fc == 7))
                    S.op('dve', lambda e: e.tensor_tensor(out=o_[:, hf * 512:(hf + 1) * 512], in0=po[:], in1=x_[:, hf * 512:(hf + 1) * 512], op=ALU.add),
                         reads=[pon, xn], writes=[on])
                S.dma('act', out_d.ap()[c * 128:(c + 1) * 128, :], o_[:], reads=[on])
            S.finish('sp')
            S.finish('pool')
            S.finish('act')
    return nc


_NC = None


def kernel(x, norm_g, w_in, conv_w, conv_b, filt_w1, filt_b1, filt_w2, filt_b2, filt_w3, filt_b3,
           filt_w4, filt_sin_freq, hyena_bias, q_norm_g, k_norm_g, attn_sink, hy_out_norm_g,
           attn_out_norm_g, w_out):
    global _NC
    f32 = lambda a: np.ascontiguousarray(np.asarray(a, dtype=np.float32))
    x = f32(x)
    C = _consts()
    shared = dict(
        w_in=f32(w_in)[0], w_out=f32(w_out)[0],
        gcol=np.ascontiguousarray(f32(norm_g)[0].reshape(8, 128).T),
        cwc=np.ascontiguousarray(np.concatenate([f32(conv_w)[0], f32(conv_b)], axis=0).reshape(4, 12, 128).transpose(2, 1, 0)).reshape(128, 48),
        filt_w1=f32(filt_w1)[0], filt_w2=f32(filt_w2)[0], filt_w3=f32(filt_w3)[0], filt_w4=f32(filt_w4)[0],
        fcols=np.ascontiguousarray(np.stack([f32(filt_b1)[0], f32(filt_b2)[0], f32(filt_b3)[0], f32(filt_sin_freq)[0]], axis=1)),
        hyena_bias=f32(hyena_bias)[0].reshape(1, 1024),
        q_norm_g=f32(q_norm_g)[0].reshape(1, 64), k_norm_g=f32(k_norm_g)[0].reshape(1, 64),
        attn_sink=f32(attn_sink)[0].reshape(1, 8),
        hy_out_norm_g=f32(hy_out_norm_g)[0].reshape(1, 512), attn_out_norm_g=f32(attn_out_norm_g)[0].reshape(1, 512),
        zT=C['zT'], decay=C['decay'],
        Fw_t=C['Fw_t'].reshape(32, 128, 32 * 128), Fh_t=C['Fh_t'].reshape(16, 128, 2 * 8 * 128), Bh_t=C['Bh_t'].reshape(8, 128, 32 * 128), jmat=C['jmat'],
        rope_cs=C['rope_cs'].reshape(128, 2 * 16 * 32), ident=C['ident'], masks=C['masks'].reshape(128, 256),
    )
    in_maps = []
    for c in range(NCORES):
        xc = x[c * NB:(c + 1) * NB].reshape(NB * L, D)
        m = dict(shared)
        m['x'] = np.ascontiguousarray(xc)
        m['xT'] = np.ascontiguousarray(xc.T)
        in_maps.append(m)
    if _NC is None:
        _NC = build_nc()
    res = run_bass_kernel_spmd(_NC, in_maps, core_ids=list(range(NCORES)))
    kernel.last_results = res
    out = np.concatenate([r['out'].reshape(NB, L, D) for r in res.results], axis=0)
    return out.astype(np.float32)
```

```python
import contextlib
import math
import numpy as np
import ml_dtypes
import concourse.bass as bass
import concourse.mybir as mybir
from concourse.bass_utils import run_bass_kernel_spmd

F32 = mybir.dt.float32
BF16 = mybir.dt.bfloat16
AF = mybir.ActivationFunctionType
ALU = mybir.AluOpType
AX = mybir.AxisListType

NCORES = 8
NB = 4
L = 2048
NT = 16
D = 1024
NFFT = 4096
EPS = 1e-6
DEBUG = False


class Sched:
    ENG = ('pe', 'act', 'dve', 'pool', 'sp')
    NRING = 8

    def __init__(self, nc, stack):
        self.nc = nc
        self.e = {'pe': nc.tensor, 'act': nc.scalar, 'dve': nc.vector,
                  'pool': nc.gpsimd, 'sp': nc.sync}
        self.sem = {}
        for k in self.ENG:
            self.sem[k] = stack.enter_context(nc.semaphore('s_' + k))
        self.cnt = {k: 0 for k in self.ENG}
        self.dq = {}
        for q in ('sp', 'pool', 'act'):
            for i in range(self.NRING):
                self.sem[('d', q, i)] = stack.enter_context(nc.semaphore('d_%s_%d' % (q, i)))
            self.dq[q] = 0
        self.seen = {k: {} for k in self.ENG}
        self.last_w = {}
        self.readers = {}
        self.all_dma = []

    def _wait(self, eng, key, val):
        if self.seen[eng].get(key, 0) >= val:
            return
        self.e[eng].wait_ge(self.sem[key], val)
        self.seen[eng][key] = val

    def _deps(self, eng, reads, writes):
        deps = {}

        def add(t, same_ok):
            if t is None:
                return
            key, val = t
            if key == eng and eng == 'pe':
                return
            if deps.get(key, 0) < val:
                deps[key] = val
        for b in reads:
            add(self.last_w.get(b), True)
        for b in writes:
            add(self.last_w.get(b), True)
            for t in self.readers.get(b, ()):
                add(t, False)
        for key, val in deps.items():
            self._wait(eng, key, val)

    def _record(self, ticket, reads, writes):
        for b in reads:
            self.readers.setdefault(b, []).append(ticket)
        for b in writes:
            self.last_w[b] = ticket
            self.readers[b] = []

    def op(self, eng, fn, reads=(), writes=(), signal=True):
        self._deps(eng, reads, writes)
        inst = fn(self.e[eng])
        if signal:
            self.cnt[eng] += 1
            inst.then_inc(self.sem[eng], 1)
            ticket = (eng, self.cnt[eng])
        else:
            ticket = (eng, self.cnt[eng] + 1)
        self._record(ticket, reads, writes)
        return ticket

    def dma(self, q, out, in_, reads=(), writes=(), **kw):
        i = self.dq[q]
        slot = i % self.NRING
        key = ('d', q, slot)
        if i >= self.NRING:
            self._wait(q, key, 16 * (i // self.NRING))
        self._deps(q, reads, writes)
        self.e[q].dma_start(out=out, in_=in_, **kw).then_inc(self.sem[key], 16)
        self.dq[q] = i + 1
        ticket = (key, 16 * (i // self.NRING + 1))
        self._record(ticket, reads, writes)
        self.all_dma.append(ticket)
        return ticket

    def _last_dma(self):
        last = {}
        for key, val in self.all_dma:
            if last.get(key, 0) < val:
                last[key] = val
        return last

    def barrier(self):
        last = self._last_dma()
        for eng in self.ENG:
            for other in self.ENG:
                if other != eng and self.cnt[other] > 0:
                    self._wait(eng, other, self.cnt[other])
            for key, val in last.items():
                self._wait(eng, key, val)
        self.all_dma = []
        self.last_w = {}
        self.readers = {}

    def finish(self, eng='sp'):
        for key, val in self._last_dma().items():
            self._wait(eng, key, val)


_CONST = None


def _consts():
    global _CONST
    if _CONST is not None:
        return _CONST
    bf = ml_dtypes.bfloat16
    m = np.arange(NFFT)
    pos = np.where(m < L, m, NFFT - m)
    pos_c = np.minimum(pos, L - 1)
    t = np.linspace(0.0, 1.0, L, dtype=np.float32)[:, None]
    bands = 16
    f = np.linspace(1e-4, bands - 1, bands, dtype=np.float32)[None, :]
    w = (2.0 * math.pi * np.arange(L, dtype=np.float32)[:, None] / L).astype(np.float32)
    z = np.concatenate([t, np.cos(f * w), -np.sin(f * w)], axis=-1).astype(np.float32)
    max_decay = math.log(1e-2) / 0.3
    min_decay = math.log(1e-2) / 1.5
    deltas = np.linspace(min_decay, max_decay, 512, dtype=np.float32)
    decay = np.exp(-t * np.abs(deltas)[None, :]).astype(np.float32)
    tt = np.arange(NFFT, dtype=np.int64)[:, None]
    g = np.arange(NFFT, dtype=np.int64)[None, :]
    gg = g % L
    ang = 2.0 * np.pi * ((gg * tt) % NFFT).astype(np.float64) / NFFT
    Fw = np.where(g < L, np.cos(ang), -np.sin(ang))
    Fw[:, L] = np.cos(np.pi * (np.arange(NFFT) % 2))
    Fw_t = Fw.reshape(32, 128, 32, 128).transpose(2, 1, 0, 3)
    Fw_t = np.ascontiguousarray(Fw_t).astype(bf)
    jj = np.arange(1024, dtype=np.int64)[:, None]
    ff = np.arange(L, dtype=np.int64)[None, :]
    ah = 2.0 * np.pi * ((ff * (2 * jj + 1)) % (2 * NFFT)).astype(np.float64) / (2 * NFFT)
    sgn = np.where(np.arange(1024) % 2 == 0, 1.0, -1.0)
    FhRe = np.cos(ah)
    FhIm = -np.sin(ah)
    FhIm[:, 0] = -sgn
    Fh = np.concatenate([FhRe, FhIm], axis=1)
    Fh_t = np.ascontiguousarray(Fh.reshape(8, 128, 2, 16, 128).transpose(3, 1, 2, 0, 4)).astype(bf)
    BhRe = np.cos(ah).T * (2.0 / NFFT)
    BhRe[0, :] = 1.0 / NFFT
    BhIm = np.sin(ah).T * (2.0 / NFFT)
    BhIm[0, :] = sgn / NFFT
    Bh = np.concatenate([BhRe, BhIm], axis=0)
    Bh_t = np.ascontiguousarray(Bh.reshape(32, 128, 8, 128).transpose(2, 1, 0, 3)).astype(bf)
    jmat = np.ascontiguousarray(np.eye(128, dtype=np.float32)[::-1]).astype(bf)
    half = 32
    inv = (10000.0 ** (-np.arange(half, dtype=np.float32) / half)).astype(np.float32)
    ang = np.arange(L, dtype=np.float32)[:, None] * inv[None, :]
    cos = np.cos(ang).astype(np.float32).reshape(NT, 128, half).transpose(1, 0, 2)
    sin = np.sin(ang).astype(np.float32).reshape(NT, 128, half).transpose(1, 0, 2)
    rope_cs = np.ascontiguousarray(np.stack([cos, sin], axis=1))
    ident = np.eye(128, dtype=np.float32).astype(bf)
    s_ = np.arange(128)[:, None]
    q_ = np.arange(128)[None, :]
    maskL = (q_ <= s_).astype(np.float32).astype(bf)
    maskU = (s_ <= q_).astype(np.float32).astype(bf)
    masks = np.ascontiguousarray(np.stack([maskL, maskU], axis=1))
    _CONST = dict(zT=np.ascontiguousarray(z.T), decay=decay, Fw_t=Fw_t, Fh_t=Fh_t, Bh_t=Bh_t, jmat=jmat,
                  rope_cs=rope_cs, ident=ident, masks=masks)
    return _CONST


def build_nc():
    nc = bass.Bass('TRN2', target_bir_lowering=False)

    def din(name, shape, dt=F32):
        return nc.dram_tensor(name, list(shape), dt, kind='ExternalInput')

    def dscr(name, shape, dt):
        return nc.dram_tensor(name, list(shape), dt, kind='ExternalOutput' if DEBUG else 'Internal')

    xT_d = din('xT', [D, NB * L])
    x_d = din('x', [NB * L, D])
    win_d = din('w_in', [D, 3328])
    wout_d = din('w_out', [D, D])
    gcol_d = din('gcol', [128, 8])
    cwc_d = din('cwc', [128, 48])
    fw1_d = din('filt_w1', [33, 64])
    fw2_d = din('filt_w2', [64, 64])
    fw3_d = din('filt_w3', [64, 64])
    fw4_d = din('filt_w4', [64, 2048])
    fcol_d = din('fcols', [64, 4])
    hb_d = din('hyena_bias', [1, 1024])
    qg_d = din('q_norm_g', [1, 64])
    kg_d = din('k_norm_g', [1, 64])
    sink_d = din('attn_sink', [1, 8])
    hyg_d = din('hy_out_norm_g', [1, 512])
    atg_d = din('attn_out_norm_g', [1, 512])
    zT_d = din('zT', [33, L])
    dec_d = din('decay', [L, 512])
    Fw_d = din('Fw_t', [32, 128, 32 * 128], BF16)
    Fh_d = din('Fh_t', [16, 128, 2 * 8 * 128], BF16)
    Bh_d = din('Bh_t', [8, 128, 32 * 128], BF16)
    jmat_d = din('jmat', [128, 128], BF16)
    rope_d = din('rope_cs', [128, 2 * 16 * 32])
    ident_d = din('ident', [128, 128], BF16)
    masks_d = din('masks', [128, 256], BF16)
    out_d = nc.dram_tensor('out', [NB * L, D], F32, kind='ExternalOutput')

    hT_s = dscr('hT_s', [NB, 128, 8 * 2050], BF16)
    U_s = dscr('U_s', [3, NB * L, 512], BF16)
    SG_s = dscr('SG_s', [NB * L, 512], BF16)
    yT_s = dscr('yT_s', [NB * NT, 128, 8, 128], BF16)
    Kh_s = dscr('Kh_s', [16, 128, 2, 1024], BF16)

    def bc(handle, ncols, parts=128, off=0):
        return bass.AP(handle, off, [[0, parts], [1, ncols]])

    with contextlib.ExitStack() as G:
        S = Sched(nc, G)

        def sb(st, name, shape, dt):
            return st.enter_context(nc.sbuf_tensor('sb_' + name, list(shape), dt))

        def ps(st, name, shape, dt=F32):
            return st.enter_context(nc.psum_tensor('ps_' + name, list(shape), dt))

        ident = sb(G, 'ident', [128, 128], BF16)
        ones = sb(G, 'ones', [128, 128], BF16)
        S.dma('sp', ident[:], ident_d.ap(), writes=['ident'])
        S.op('dve', lambda e: e.memset(ones[:], 1.0), writes=['ones'])

        def compute_hT(st, b, hT, tagp):
            xs = [sb(st, tagp + 'xs%d' % i, [128, 8, 512], F32) for i in range(2)]
            sq = sb(st, tagp + 'sq', [128, 8, 512], BF16)
            rs = sb(st, tagp + 'rs', [128, 512], F32)
            psr = ps(st, tagp + 'psr', [128, 512])
            return xs, sq, rs, psr

        def emit_hT_stage(st_, b, tt, hT, bufs, hname):
            xs, sq, rs, psr = bufs
            xb = xs[tt % 2]
            xn = 'xs%d' % (tt % 2)
            if st_ == 0:
                src = xT_d.ap()[:, b * L + tt * 512: b * L + (tt + 1) * 512].rearrange('(kc p) n -> p kc n', p=128)
                S.dma('sp', xb[:], src, writes=[xn])
                S.op('act', lambda e: e.activation(out=sq[:], in_=xb[:], func=AF.Square), reads=[xn], writes=['sq'])
            elif st_ == 1:
                for kc in range(8):
                    S.op('pe', lambda e: e.matmul(psr[:], lhsT=ones[:], rhs=sq[:, kc, :], start=(kc == 0), stop=(kc == 7)),
                         reads=['sq', 'ones'], writes=['psr'], signal=(kc == 7))
                S.op('act', lambda e: e.activation(out=rs[:], in_=psr[:], func=AF.Sqrt, scale=1.0 / D, bias=eps_t[:, 0:1]),
                     reads=['psr', 'eps'], writes=['rs'])
                S.op('dve', lambda e: e.reciprocal(out=rs[:], in_=rs[:]), reads=['rs'], writes=['rs'])
            else:
                for kc in range(8):
                    eng = 'dve' if kc % 2 == 0 else 'pool'
                    S.op(eng, lambda e: e.tensor_tensor(out=hT[:, kc, 1 + tt * 512: 1 + (tt + 1) * 512], in0=xb[:, kc, :], in1=rs[:], op=ALU.mult),
                         reads=[xn, 'rs'], writes=['%s%d' % (hname, kc)])

        def emit_hT_tile(b, tt, hT, bufs, hname):
            for st_ in range(3):
                emit_hT_stage(st_, b, tt, hT, bufs, hname)

        def emit_hT(b, hT, bufs, hname):
            for tt in range(4):
                emit_hT_tile(b, tt, hT, bufs, hname)

        eps_t = sb(G, 'eps_t', [128, 1], F32)
        S.op('dve', lambda e: e.memset(eps_t[:], EPS), writes=['eps'])

        with contextlib.ExitStack() as P:
            Wh = sb(P, 'Wh', [128, 8, 1536], BF16)
            hTs = [sb(P, 'hT_%d' % i, [128, 8, 2050], BF16) for i in range(2)]
            cwc = sb(P, 'cwc', [128, 12, 4], F32)
            with contextlib.ExitStack() as P0:
                gcol = sb(P0, 'gcol', [128, 8], F32)
                wst = [sb(P0, 'wst%d' % i, [128, 1536], F32) for i in range(2)]
                S.dma('sp', gcol[:], gcol_d.ap(), writes=['gcol'])
                S.dma('sp', cwc[:].rearrange('p c j -> p (c j)'), cwc_d.ap(), writes=['cwc'])
                for kc in range(8):
                    wb = wst[kc % 2]
                    wn = 'wst%d' % (kc % 2)
                    S.dma('sp', wb[:], win_d.ap()[kc * 128:(kc + 1) * 128, 0:1536], writes=[wn])
                    if kc % 2 == 0:
                        S.op('act', lambda e: e.activation(out=Wh[:, kc, :], in_=wb[:], func=AF.Copy, scale=gcol[:, kc:kc + 1]), reads=[wn, 'gcol'], writes=['Wh'])
                    else:
                        S.op('dve', lambda e: e.tensor_scalar(out=Wh[:, kc, :], in0=wb[:], scalar1=gcol[:, kc:kc + 1], scalar2=None, op0=ALU.mult), reads=[wn, 'gcol'], writes=['Wh'])
                S.barrier()
            bufs = compute_hT(P, 0, None, 'p1')
            pf = [sb(P, 'pf%d' % i, [128, 2050], F32) for i in range(2)]
            t1 = [sb(P, 't1_0', [128, 2048], F32)] * 2
            ucb = [sb(P, 'ucb%d' % i, [128, 2048], BF16) for i in range(8)]
            ust = [sb(P, 'ust%d' % i, [128, 512], BF16) for i in range(3)]
            psp = [ps(P, 'psp%d' % i, [128, 512]) for i in range(4)]
            pst = [ps(P, 'pst%d' % i, [128, 4, 128], BF16) for i in range(3)]
            HTN = [['hT%s%d' % ('ab'[q], k) for k in range(8)] for q in range(2)]
            for q in range(2):
                S.op('pool', lambda e: e.memset(hTs[q][:, :, 0:1], 0.0), writes=HTN[q])
                S.op('pool', lambda e: e.memset(hTs[q][:, :, 2049:2050], 0.0), writes=HTN[q])
            for i in range(2):
                S.op('pool', lambda e: e.memset(pf[i][:, 0:1], 0.0), writes=['pf%d' % i])
                S.op('pool', lambda e: e.memset(pf[i][:, 2049:2050], 0.0), writes=['pf%d' % i])
            cnt = {'pf': 0, 'pp': 0, 'pt': 0, 'us': 0}

            def group_mm(b, gi, n, slots):
                hT = hTs[b % 2]
                hq = 'ab'[b % 2]
                for c4 in range(4):
                    ct = gi * 4 + c4
                    k = cnt['pf'] % 2
                    cnt['pf'] += 1
                    pfb, pfn = pf[k], 'pf%d' % k
                    t1b, t1n = t1[0], 't1_0'
                    ub = ucb[(n % 2) * 4 + c4]
                    ubn = 'ucb%d' % ((n % 2) * 4 + c4)
                    for tt in range(4):
                        kp = cnt['pp'] % 4
                        cnt['pp'] += 1
                        pp, ppn = psp[kp], 'psp%d' % kp
                        for kc in range(8):
                            S.op('pe', lambda e: e.matmul(pp[:], lhsT=Wh[:, kc, ct * 128:(ct + 1) * 128], rhs=hT[:, kc, 1 + tt * 512: 1 + (tt + 1) * 512], start=(kc == 0), stop=(kc == 7)),
                                 reads=['hT%s%d' % (hq, kc), 'Wh'], writes=[ppn], signal=(kc == 7))
                        S.op('act', lambda e: e.copy(out=pfb[:, 1 + tt * 512: 1 + (tt + 1) * 512], in_=pp[:]), reads=[ppn], writes=[pfn])
                    S.op('act', lambda e: e.activation(out=t1b[:], in_=pfb[:, 1:2049], func=AF.Identity, scale=cwc[:, ct, 1:2], bias=cwc[:, ct, 3:4]),
                         reads=[pfn, 'cwc'], writes=[t1n])
                    S.op('dve', lambda e: e.scalar_tensor_tensor(out=t1b[:], in0=pfb[:, 0:2048], scalar=cwc[:, ct, 0:1], in1=t1b[:], op0=ALU.mult, op1=ALU.add),
                         reads=[pfn, 'cwc', t1n], writes=[t1n])
                    S.op('dve', lambda e: e.scalar_tensor_tensor(out=ub[:], in0=pfb[:, 2:2050], scalar=cwc[:, ct, 2:3], in1=t1b[:], op0=ALU.mult, op1=ALU.add),
                         reads=[pfn, 'cwc', t1n], writes=[ubn])
                    for fn_s in slots[c4]:
                        fn_s()

            def group_tr(b, gi, n, irange):
                for i in irange:
                    kt_ = cnt['pt'] % 3
                    cnt['pt'] += 1
                    pt, ptn = pst[kt_], 'pst%d' % kt_
                    for c4 in range(4):
                        S.op('pe', lambda e: e.transpose(pt[:, c4, :], ucb[(n % 2) * 4 + c4][:, i * 128:(i + 1) * 128], ident[:]),
                             reads=['ucb%d' % ((n % 2) * 4 + c4), 'ident'], writes=[ptn], signal=(c4 == 3))
                    ku = cnt['us'] % 3
                    cnt['us'] += 1
                    us, un = ust[ku], 'ust%d' % ku
                    S.op('act', lambda e: e.copy(out=us[:], in_=pt[:]), reads=[ptn], writes=[un])
                    S.dma('pool', U_s.ap()[gi, b * L + i * 128: b * L + (i + 1) * 128, :], us[:], reads=[un])

            pending = None
            emit_hT(0, hTs[0], bufs, 'hTa')
            for b in range(NB):
                S.dma('pool', hT_s.ap()[b], hTs[b % 2][:].rearrange('p k n -> p (k n)'), reads=HTN[b % 2])
                for gi in range(3):
                    n = b * 3 + gi
                    slots = [[], [], [], []]
                    if b + 1 < NB:
                        nh = hTs[(b + 1) % 2]
                        nhn = 'hT' + 'ab'[(b + 1) % 2]
                        tts = [(0, 1), (2,), (3,)][gi]
                        for ti, tt in enumerate(tts):
                            for st_ in range(3):
                                slots[min(3, ti + st_)].append(lambda st_=st_, tt=tt: emit_hT_stage(st_, b + 1, tt, nh, bufs, nhn))
                    if pending is not None:
                        for q4 in range(4):
                            slots[q4].append(lambda q4=q4, pd=pending: group_tr(*pd, range(q4 * 4, q4 * 4 + 4)))
                    group_mm(b, gi, n, slots)
                    pending = (b, gi, n)
            group_tr(*pending, range(NT))
            S.barrier()

        with contextlib.ExitStack() as P:
            Wr = sb(P, 'Wr', [128, 8, 1792], BF16)
            hT = sb(P, 'hT2', [128, 8, 2050], BF16)
            ropeT = sb(P, 'ropeT', [128, 8, 16, 32], F32)
            esink = sb(P, 'esink', [128, 8], F32)
            atg = sb(P, 'atg', [128, 512], F32)
            masks = sb(P, 'masks', [128, 2, 128], BF16)
            mhalf = sb(P, 'mhalf', [128, 16], F32)
            S.op('pool', lambda e: e.memset(mhalf[:], -0.5), writes=['mhalf'])
            with contextlib.ExitStack() as P0:
                gcol = sb(P0, 'gcol2', [128, 8], F32)
                wst = [sb(P0, 'wsr%d' % i, [128, 1792], F32) for i in range(2)]
                rcs = sb(P0, 'rcs', [128, 2, 16, 32], F32)
                qkg = sb(P0, 'qkg', [128, 2, 64], F32)
                S.dma('sp', gcol[:], gcol_d.ap(), writes=['gcol'])
                S.dma('sp', rcs[:].rearrange('p a i j -> p (a i j)'), rope_d.ap(), writes=['rcs'])
                S.dma('sp', qkg[:, 0, :], bc(qg_d, 64), writes=['qkg'])
                S.dma('sp', qkg[:, 1, :], bc(kg_d, 64), writes=['qkg'])
                S.dma('sp', esink[:], bc(sink_d, 8), writes=['esink'])
                S.dma('sp', atg[:], bc(atg_d, 512), writes=['atg'])
                S.dma('sp', masks[:].rearrange('p a n -> p (a n)'), masks_d.ap(), writes=['masks'])
                S.op('act', lambda e: e.activation(out=esink[:], in_=esink[:], func=AF.Exp), reads=['esink'], writes=['esink'])
                for qk in range(2):
                    sc = 0.125 if qk == 0 else 1.0
                    for ti, (cs, half) in enumerate([(0, 0), (1, 1), (0, 1), (1, 0)]):
                        gsl = qkg[:, qk, half * 32:(half + 1) * 32].unsqueeze(1).broadcast_to([128, 16, 32])
                        S.op('dve', lambda e: e.scalar_tensor_tensor(out=ropeT[:, qk * 4 + ti, :, :], in0=rcs[:, cs, :, :], scalar=sc, in1=gsl, op0=ALU.mult, op1=ALU.mult),
                             reads=['rcs', 'qkg'], writes=['ropeT'])
                for kc in range(8):
                    wb = wst[kc % 2]
                    wn = 'wsr%d' % (kc % 2)
                    S.dma('sp', wb[:], win_d.ap()[kc * 128:(kc + 1) * 128, 1536:3328], writes=[wn])
                    S.op('act', lambda e: e.activation(out=Wr[:, kc, 0:512], in_=wb[:, 0:512], func=AF.Copy, scale=gcol[:, kc:kc + 1]),
                         reads=[wn, 'gcol'], writes=['Wr'])
                    S.op('act', lambda e: e.activation(out=Wr[:, kc, 512:1024].rearrange('p (g k d) -> p g k d', g=4, k=2),
                                                       in_=wb[:, 512:1024].rearrange('p (k g d) -> p g k d', k=2, g=4),
                                                       func=AF.Copy, scale=gcol[:, kc:kc + 1]),
                         reads=[wn, 'gcol'], writes=['Wr'])
                    S.op('act', lambda e: e.activation(out=Wr[:, kc, 1024:1792], in_=wb[:, 1024:1792], func=AF.Copy, scale=gcol[:, kc:kc + 1]),
                         reads=[wn, 'gcol'], writes=['Wr'])
                S.barrier()
            QT = sb(P, 'QT', [128, 2, 4, L], BF16)
            KT = sb(P, 'KT', [128, L], BF16)
            V = sb(P, 'V', [128, NT, 2, 65], BF16)
            sga = sb(P, 'sga', [128, NT, 512], BF16)
            two = lambda name, shape, dt: [sb(P, '%s%d' % (name, i), shape, dt) for i in range(2)]
            sgh = two('sgh', [128, 512], BF16)
            qraw = two('qraw', [128, 640], F32)
            qsq = two('qsq', [128, 640], F32)
            ssq = two('ssq', [128, 10], F32)
            qn = two('qn', [128, 640], F32)
            tq = [two('tq%d' % k, [128, 256], F32) for k in range(4)]
            tk = [two('tk%d' % k, [128, 64], F32) for k in range(4)]
            qb = two('qb', [128, 640], BF16)
            Pm = [[sb(P, 'Pm%d_%d' % (i, j), [128, 512], BF16) for j in range(3)] for i in range(2)]
            den = two('den', [128, 4], F32)
            ya = two('ya', [128, 512], F32)
            yq = two('yq', [128, 512], F32)
            ssa = two('ssa', [128, 1], F32)
            yab = two('yab', [128, 512], BF16)
            yts = two('yts', [128, 4, 128], BF16)
            Bk = [ps(P, 'B%d' % i, [128, 512]) for i in range(6)]
            PT = ps(P, 'PT', [128, 4, 128], BF16)
            PK = ps(P, 'PK', [128, 128], BF16)
            S.op('pool', lambda e: e.memset(V[:, :, :, 64:65], 1.0), writes=['V'])
            S.op('pool', lambda e: e.memset(QT[64:128, 0, :, :], 0.0), writes=['QT'])
            S.op('pool', lambda e: e.memset(QT[0:64, 1, :, :], 0.0), writes=['QT'])

            def grp(bank, bname, lt, wcols):
                for kc in range(8):
                    S.op('pe', lambda e: e.matmul(bank, lhsT=lt(kc), rhs=Wr[:, kc, wcols], start=(kc == 0), stop=(kc == 7)),
                         reads=['hT', 'Wr'], writes=[bname], signal=(kc == 7))

            def proj_pe(i):
                par = i % 2
                lt = lambda kc: hT[:, kc, 1 + i * 128: 1 + (i + 1) * 128]
                grp(Bk[4][:, 0:256], 'B4', lt, slice(1024, 1280))
                grp(Bk[2][:], 'B2', lt, slice(512, 1024))
                grp(Bk[0][:], 'B0', lt, slice(0, 512))
                grp(Bk[1][:], 'B1', lt, slice(1280, 1792))

            def proj_ew(b, i):
                par = i % 2
                kvb = Bk[4][:, 0:256]
                kvn = 'B4'
                qbk = Bk[2]
                qbn = 'B2'
                sfx = str(par)
                S.op('act', lambda e: e.copy(out=qraw[par][:, 512:640], in_=kvb[:, 0:128]), reads=[kvn], writes=['qrawk' + sfx])
                S.op('act', lambda e: e.copy(out=V[:, i, :, 0:64], in_=kvb[:, 128:256].rearrange('p (k d) -> p k d', k=2)), reads=[kvn], writes=['V'])
                S.op('act', lambda e: e.copy(out=qraw[par][:, 0:512], in_=qbk[:]), reads=[qbn], writes=['qrawq' + sfx])
                S.op('dve', lambda e: e.tensor_tensor(out=qsq[par][:], in0=qraw[par][:], in1=qraw[par][:], op=ALU.mult),
                     reads=['qrawq' + sfx, 'qrawk' + sfx], writes=['qsq' + sfx])
                S.op('dve', lambda e: e.tensor_reduce(out=ssq[par][:], in_=qsq[par][:].rearrange('p (h d) -> p h d', d=64), axis=AX.X, op=ALU.add),
                     reads=['qsq' + sfx], writes=['ssq' + sfx])
                S.op('act', lambda e: e.activation(out=ssq[par][:], in_=ssq[par][:], func=AF.Sqrt, scale=1.0 / 64, bias=eps_t[:, 0:1]),
                     reads=['ssq' + sfx, 'eps'], writes=['ssq' + sfx])
                S.op('dve', lambda e: e.reciprocal(out=ssq[par][:], in_=ssq[par][:]), reads=['ssq' + sfx], writes=['ssq' + sfx])
                S.op('dve', lambda e: e.tensor_tensor(out=qn[par][:].rearrange('p (h d) -> p h d', d=64), in0=qraw[par][:].rearrange('p (h d) -> p h d', d=64),
                                                      in1=ssq[par][:].unsqueeze(2).broadcast_to([128, 10, 64]), op=ALU.mult),
                     reads=['qrawq' + sfx, 'qrawk' + sfx, 'ssq' + sfx], writes=['qn' + sfx])
                sg = sgh[par]
                sgn = 'sgh' + sfx
                S.op('act', lambda e: e.activation(out=sg[:], in_=Bk[0][:], func=AF.Silu), reads=['B0'], writes=[sgn])
                S.dma('pool', SG_s.ap()[b * L + i * 128: b * L + (i + 1) * 128, :], sg[:], reads=[sgn])
                S.op('act', lambda e: e.activation(out=sga[:, i, :], in_=Bk[1][:], func=AF.Silu), reads=['B1'], writes=['sga'])
                for qk, (c0, nh, tt_) in enumerate([(0, 8, tq), (512, 2, tk)]):
                    v4 = qn[par][:, c0:c0 + nh * 64].rearrange('p (h t j) -> p h t j', t=2, j=32)
                    o4 = qb[par][:, c0:c0 + nh * 64].rearrange('p (h t j) -> p h t j', t=2, j=32)
                    q1 = v4[:, :, 0, :]
                    q2 = v4[:, :, 1, :]
                    tb = lambda ti: ropeT[:, qk * 4 + ti, i, :].unsqueeze(1).broadcast_to([128, nh, 32])
                    a = [tt_[k][par][:, 0:nh * 32].rearrange('p (h j) -> p h j', j=32) for k in range(4)]
                    an = ['t%d%d%s' % (qk, k, sfx) for k in range(4)]
                    qbn2 = 'qb%d%s' % (qk, sfx)
                    S.op('dve', lambda e: e.tensor_tensor(out=a[0], in0=q1, in1=tb(0), op=ALU.mult), reads=['qn' + sfx, 'ropeT'], writes=[an[0]])
                    S.op('pool', lambda e: e.tensor_tensor(out=a[1], in0=q2, in1=tb(1), op=ALU.mult), reads=['qn' + sfx, 'ropeT'], writes=[an[1]])
                    S.op('pool', lambda e: e.tensor_tensor(out=a[2], in0=q2, in1=tb(2), op=ALU.mult), reads=['qn' + sfx, 'ropeT'], writes=[an[2]])
                    S.op('dve', lambda e: e.tensor_tensor(out=a[3], in0=q1, in1=tb(3), op=ALU.mult), reads=['qn' + sfx, 'ropeT'], writes=[an[3]])
                    S.op('dve', lambda e: e.tensor_tensor(out=o4[:, :, 0, :], in0=a[0], in1=a[1], op=ALU.subtract), reads=[an[0], an[1]], writes=[qbn2 + 'a'])
                    S.op('pool', lambda e: e.tensor_tensor(out=o4[:, :, 1, :], in0=a[2], in1=a[3], op=ALU.add), reads=[an[2], an[3]], writes=[qbn2 + 'b'])

            def proj_tr(i):
                par = i % 2
                sfx = str(par)
                for g in range(4):
                    S.op('pe', lambda e: e.transpose(PT[:, g, :], qb[par][:, g * 128:(g + 1) * 128], ident[:]),
                         reads=['qb0%sa' % sfx, 'qb0%sb' % sfx, 'ident'], writes=['PT'], signal=(g == 3))
                S.op('pe', lambda e: e.transpose(PK[:], qb[par][:, 512:640], ident[:]),
                     reads=['qb1%sa' % sfx, 'qb1%sb' % sfx, 'ident'], writes=['PK'])
                S.op('act', lambda e: e.copy(out=QT[0:64, 0, :, i * 128:(i + 1) * 128], in_=PT[0:64, :, :]), reads=['PT'], writes=['QT'])
                S.op('act', lambda e: e.copy(out=QT[64:128, 1, :, i * 128:(i + 1) * 128], in_=PT[64:128, :, :]), reads=['PT'], writes=['QT'])
                S.op('act', lambda e: e.copy(out=KT[:, i * 128:(i + 1) * 128], in_=PK[:]), reads=['PK'], writes=['KT'])

            def s_jobs():
                jobs = []
                for u in range(2 * NT):
                    i, kv = divmod(u, 2)
                    ccs = [c for c in (i - 1, i, i + 1) if 0 <= c < NT]
                    for ci, c in enumerate(ccs):
                        jobs.append((u, ci, c, ci == len(ccs) - 1))
                return jobs

            def att_S1(k, job):
                u, ci, c, last = job
                i, kv = divmod(u, 2)
                pr = slice(kv * 64, (kv + 1) * 64)
                bi = k % 4
                S.op('pe', lambda e: e.matmul(Bk[bi][:].rearrange('p (g q) -> p g q', g=4), lhsT=KT[:, c * 128:(c + 1) * 128], rhs=QT[:, kv, :, i * 128:(i + 1) * 128], start=True, stop=True),
                     reads=['KT', 'QT'], writes=['B%d' % bi])

            def att_E1(k, job):
                u, ci, c, last = job
                i, kv = divmod(u, 2)
                bi = k % 4
                pm = Pm[u % 2][ci]
                pmn = 'Pm%d_%d' % (u % 2, ci)
                S.op('act', lambda e: e.activation(out=pm[:], in_=Bk[bi][:], func=AF.Exp), reads=['B%d' % bi], writes=[pmn])
                if c != i:
                    mk = masks[:, 0 if c < i else 1, :].unsqueeze(1).broadcast_to([128, 4, 128])
                    S.op('dve', lambda e: e.tensor_tensor(out=pm[:].rearrange('p (g q) -> p g q', g=4), in0=pm[:].rearrange('p (g q) -> p g q', g=4), in1=mk, op=ALU.mult),
                         reads=[pmn, 'masks'], writes=[pmn])

            def att_PV(u):
                i, kv = divmod(u, 2)
                po = Bk[4 + u % 2][:, 0:260].rearrange('p (g d) -> p g d', g=4)
                pon = ['B%d' % (4 + u % 2)]
                lst = [c for c in (i - 1, i, i + 1) if 0 <= c < NT]
                for g in range(4):
                    for ci, c in enumerate(lst):
                        S.op('pe', lambda e: e.matmul(po[:, g, :], lhsT=Pm[u % 2][ci][:, g * 128:(g + 1) * 128], rhs=V[:, c, kv, :], start=(ci == 0), stop=(ci == len(lst) - 1)),
                             reads=['Pm%d_%d' % (u % 2, ci), 'V'], writes=pon, signal=(g == 3 and ci == len(lst) - 1))
                return po, pon

            def att_D(u, po, pon):
                i, kv = divmod(u, 2)
                d_ = den[u % 2]
                dn = 'den%d' % (u % 2)
                yan = 'ya%d_%d' % (i % 2, kv)
                S.op('dve', lambda e: e.tensor_tensor(out=d_[:], in0=po[:, :, 64], in1=esink[:, kv * 4:(kv + 1) * 4], op=ALU.add),
                     reads=pon + ['esink'], writes=[dn])
                S.op('dve', lambda e: e.reciprocal(out=d_[:], in_=d_[:]), reads=[dn], writes=[dn])
                S.op('dve', lambda e: e.tensor_tensor(out=ya[i % 2][:, kv * 256:(kv + 1) * 256].rearrange('p (g d) -> p g d', g=4), in0=po[:, :, 0:64],
                                                      in1=d_[:].unsqueeze(2).broadcast_to([128, 4, 64]), op=ALU.mult),
                     reads=pon + [dn], writes=[yan])

            def att_N(i):
                par = i % 2
                sfx = str(par)
                yr = ['ya%d_0' % par, 'ya%d_1' % par]
                S.op('dve', lambda e: e.scalar_tensor_tensor(out=yq[par][:], in0=ya[par][:], scalar=1.0, in1=ya[par][:], op0=ALU.mult, op1=ALU.mult, accum_out=ssa[par][:, 0:1]),
                     reads=yr, writes=['yq' + sfx, 'ssa' + sfx])
                S.op('act', lambda e: e.activation(out=ssa[par][:], in_=ssa[par][:], func=AF.Ln, scale=1.0 / 512, bias=eps_t[:, 0:1]),
                     reads=['ssa' + sfx, 'eps'], writes=['ssa' + sfx])
                S.op('act', lambda e: e.activation(out=ssa[par][:], in_=ssa[par][:], func=AF.Exp, scale=-0.5),
                     reads=['ssa' + sfx], writes=['ssa' + sfx])
                S.op('dve', lambda e: e.scalar_tensor_tensor(out=yq[par][:], in0=ya[par][:], scalar=ssa[par][:, 0:1], in1=atg[:], op0=ALU.mult, op1=ALU.mult),
                     reads=yr + ['ssa' + sfx, 'atg', 'yq' + sfx], writes=['yq' + sfx])
                S.op('pool', lambda e: e.tensor_tensor(out=yab[par][:], in0=yq[par][:], in1=sga[:, i, :], op=ALU.mult), reads=['yq' + sfx, 'sga'], writes=['yab' + sfx])

            def att_T(b, i):
                par = i % 2
                sfx = str(par)
                for g in range(4):
                    S.op('pe', lambda e: e.transpose(PT[:, g, :], yab[par][:, g * 128:(g + 1) * 128], ident[:]),
                         reads=['yab' + sfx, 'ident'], writes=['PT'], signal=(g == 3))
                yt = yts[par]
                ytn = 'yts' + sfx
                S.op('act', lambda e: e.copy(out=yt[:], in_=PT[:]), reads=['PT'], writes=[ytn])
                S.dma('pool', yT_s.ap()[b * NT + i, :, 4:8, :], yt[:], reads=[ytn])

            for b in range(NB):
                S.dma('sp', hT[:].rearrange('p k n -> p (k n)'), hT_s.ap()[b], writes=['hT'])
                for i in range(NT + 1):
                    if i < NT:
                        proj_pe(i)
                        proj_ew(b, i)
                    if i >= 1:
                        proj_tr(i - 1)
                jobs = s_jobs()
                AHEAD = 4
                for k in range(min(AHEAD, len(jobs))):
                    att_S1(k, jobs[k])
                deferred = []
                for k, job in enumerate(jobs):
                    att_E1(k, job)
                    if k + AHEAD < len(jobs):
                        att_S1(k + AHEAD, jobs[k + AHEAD])
                    for fn_d in deferred:
                        fn_d()
                    deferred = []
                    u, ci, c, last = job
                    if last:
                        po, pon = att_PV(u)
                        att_D(u, po, pon)
                        if u % 2 == 1:
                            deferred.append(lambda i_=u // 2: att_N(i_))
                            if u // 2 >= 1:
                                deferred.append(lambda i_=u // 2 - 1: att_T(b, i_))
                for fn_d in deferred:
                    fn_d()
                att_T(b, NT - 1)
            S.barrier()

        with contextlib.ExitStack() as P:
            ke = sb(P, 'ke', [128, 16, 1024], BF16)
            ko = sb(P, 'ko', [128, 16, 1024], BF16)
            with contextlib.ExitStack() as P0:
                zT = sb(P0, 'zT', [33, L], F32)
                w1 = sb(P0, 'w1', [33, 64], F32)
                w2 = sb(P0, 'w2', [64, 64], F32)
                w3 = sb(P0, 'w3', [64, 64], F32)
                w4 = sb(P0, 'w4', [64, 2048], F32)
                fcol = sb(P0, 'fcol', [64, 4], F32)
                fsc = sb(P0, 'fsc', [64, 4], F32)
                hb = sb(P0, 'hb', [1, 1024], F32)
                hA = sb(P0, 'hA', [64, L], F32)
                hB = sb(P0, 'hB', [64, L], F32)
                s1 = sb(P0, 's1', [64, 512], F32)
                s2 = sb(P0, 's2', [64, 512], F32)
                dct = [sb(P0, 'dct%d' % i, [128, 512], F32) for i in range(2)]
                kf = [sb(P0, 'kf%d' % i, [128, 512], F32) for i in range(2)]
                kb = [sb(P0, 'kb%d' % i, [128, 512], F32) for i in range(2)]
                psf = [ps(P0, 'psf%d' % i, [64, 512]) for i in range(2)]
                psk = [ps(P0, 'psk%d' % i, [128, 512]) for i in range(4)]
                S.dma('sp', zT[:], zT_d.ap(), writes=['zT'])
                S.dma('sp', w1[:], fw1_d.ap(), writes=['w1'])
                S.dma('sp', w2[:], fw2_d.ap(), writes=['w2'])
                S.dma('sp', w3[:], fw3_d.ap(), writes=['w3'])
                S.dma('sp', w4[:], fw4_d.ap(), writes=['w4'])
                S.dma('sp', fcol[:], fcol_d.ap(), writes=['fcol'])
                S.dma('sp', hb[:], hb_d.ap(), writes=['hb'])
                S.op('dve', lambda e: e.tensor_scalar(out=fsc[:, 0:1], in0=fcol[:, 3:4], scalar1=1.0 / 3.0, scalar2=None, op0=ALU.mult),
                     reads=['fcol'], writes=['fsc'])
                S.op('dve', lambda e: e.tensor_scalar(out=fsc[:, 1:4], in0=fcol[:, 0:3], scalar1=fsc[:, 0:1], scalar2=None, op0=ALU.mult),
                     reads=['fcol', 'fsc'], writes=['fsc'])
                layers = [(w1, 'w1', zT, 'zT', 33, hA, 'hA'), (w2, 'w2', hA, 'hA', 64, hB, 'hB'), (w3, 'w3', hB, 'hB', 64, hA, 'hA')]
                for li, (wt, wn, src, sn, kk, dst, dn) in enumerate(layers):
                    for ct in range(4):
                        pf = psf[ct % 2]
                        pfn = 'psf%d' % (ct % 2)
                        cs = slice(ct * 512, (ct + 1) * 512)
                        S.op('pe', lambda e: e.matmul(pf[:], lhsT=wt[0:kk, :], rhs=src[0:kk, cs], start=True, stop=True), reads=[wn, sn], writes=[pfn])
                        S.op('act', lambda e: e.activation(out=s1[:], in_=pf[:], func=AF.Sin, scale=fsc[:, 0:1], bias=fsc[:, li + 1:li + 2]),
                             reads=[pfn, 'fsc'], writes=['s1'])
                        S.op('dve', lambda e: e.tensor_tensor(out=s2[:], in0=s1[:], in1=s1[:], op=ALU.mult), reads=['s1'], writes=['s2'])
                        S.op('dve', lambda e: e.tensor_scalar(out=s2[:], in0=s2[:], scalar1=-4.0, scalar2=3.0, op0=ALU.mult, op1=ALU.add), reads=['s2'], writes=['s2'])
                        S.op('dve', lambda e: e.tensor_tensor(out=dst[:, cs], in0=s2[:], in1=s1[:], op=ALU.mult), reads=['s1', 's2', sn], writes=[dn])
                h3 = hA
                w4v = w4[:].rearrange('p (o r c) -> p o r c', o=2, r=2)
                nk = 0
                for mc in range(16):
                    dc = dct[mc % 2]
                    dcn = 'dct%d' % (mc % 2)
                    S.dma('sp', dc[:], dec_d.ap()[mc * 128:(mc + 1) * 128, :], writes=[dcn])
                    for o in range(2):
                        pkf = psk[o * 2]
                        pkb = psk[o * 2 + 1]
                        f_ = kf[nk % 2]
                        b_ = kb[nk % 2]
                        fn_ = 'kf%d' % (nk % 2)
                        bn_ = 'kb%d' % (nk % 2)
                        nk += 1
                        S.op('pe', lambda e: e.matmul(pkf[:], lhsT=h3[:, mc * 128:(mc + 1) * 128], rhs=w4v[:, o, 0, :], start=True, stop=True), reads=['hA', 'w4'], writes=['psk%d' % (o * 2)])
                        S.op('pe', lambda e: e.matmul(pkb[:], lhsT=h3[:, mc * 128:(mc + 1) * 128], rhs=w4v[:, o, 1, :], start=True, stop=True), reads=['hA', 'w4'], writes=['psk%d' % (o * 2 + 1)])
                        S.op('dve', lambda e: e.tensor_tensor(out=f_[:], in0=pkf[:], in1=dc[:], op=ALU.mult), reads=['psk%d' % (o * 2), dcn], writes=[fn_])
                        S.op('dve', lambda e: e.tensor_tensor(out=b_[:], in0=pkb[:], in1=dc[:], op=ALU.mult), reads=['psk%d' % (o * 2 + 1), dcn], writes=[bn_])
                        if mc == 0:
                            S.op('dve', lambda e: e.memset(b_[0:1, :], 0.0), reads=[bn_], writes=[bn_])
                            S.op('dve', lambda e: e.tensor_tensor(out=f_[0:1, :], in0=f_[0:1, :], in1=hb[0:1, o * 512:(o + 1) * 512], op=ALU.add), reads=[fn_, 'hb'], writes=[fn_])
                        S.op('pool', lambda e: e.tensor_tensor(out=ke[:, mc, o * 512:(o + 1) * 512], in0=f_[:], in1=b_[:], op=ALU.add), reads=[fn_, bn_], writes=['ke'])
                        S.op('pool', lambda e: e.tensor_tensor(out=ko[:, mc, o * 512:(o + 1) * 512], in0=f_[:], in1=b_[:], op=ALU.subtract), reads=[fn_, bn_], writes=['ko'])
                S.barrier()
            fwt = [sb(P, 'fwk%d' % i, [128, 16, 128], BF16) for i in range(2)]
            kst = [sb(P, 'kst%d' % i, [128, 1024], BF16) for i in range(2)]
            psK = [ps(P, 'psK%d' % i, [128, 512]) for i in range(4)]
            psN = ps(P, 'psN', [1, 1024])
            for gt in range(32):
                fw = fwt[gt % 2]
                fn_ = 'fwk%d' % (gt % 2)
                ks = kst[gt % 2]
                ksn = 'kst%d' % (gt % 2)
                src, srcn = (ke, 'ke') if gt < 16 else (ko, 'ko')
                S.dma('sp', fw[:].rearrange('p m g -> p (m g)'), Fw_d.ap()[gt, :, 0:2048], writes=[fn_])
                for o in range(2):
                    pk = psK[(gt % 2) * 2 + o]
                    pkn = 'psK%d' % ((gt % 2) * 2 + o)
                    for mc in range(16):
                        S.op('pe', lambda e: e.matmul(pk[:], lhsT=fw[:, mc, :], rhs=src[:, mc, o * 512:(o + 1) * 512], start=(mc == 0), stop=(mc == 15)),
                             reads=[fn_, srcn], writes=[pkn], signal=(mc == 15))
                    if o == 0:
                        S.op('act', lambda e: e.copy(out=ks[:, 0:512], in_=pk[:]), reads=[pkn], writes=[ksn])
                    else:
                        S.op('dve', lambda e: e.tensor_copy(out=ks[:, 512:1024], in_=pk[:]), reads=[pkn], writes=[ksn])
                if gt == 16:
                    for o in range(2):
                        for mc in range(16):
                            S.op('pe', lambda e: e.matmul(psN[0:1, o * 512:(o + 1) * 512], lhsT=fw[:, mc, 0:1], rhs=ke[:, mc, o * 512:(o + 1) * 512], start=(mc == 0), stop=(mc == 15)),
                                 reads=[fn_, 'ke'], writes=['psN'], signal=(mc == 15))
                    S.op('act', lambda e: e.copy(out=ks[0:1, :], in_=psN[0:1, :]), reads=['psN', ksn], writes=[ksn])
                S.dma('pool', Kh_s.ap()[gt % 16, :, gt // 16, :], ks[:], reads=[ksn])
            S.barrier()

        with contextlib.ExitStack() as P:
            uv = sb(P, 'uv', [128, NT, 512], BF16)
            x1 = sb(P, 'x1', [128, NT, 512], BF16)
            x2 = sb(P, 'x2', [128, NT, 512], BF16)
            x1r = sb(P, 'x1r', [128, 8, 512], BF16)
            x2r = sb(P, 'x2r', [128, 8, 512], BF16)
            vp = sb(P, 'vp', [128, 8, 512], BF16)
            vm = sb(P, 'vm', [128, 8, 512], BF16)
            Yh = sb(P, 'Yh', [128, 32, 512], BF16)
            hyg = sb(P, 'hyg', [128, 512], F32)
            Jm = sb(P, 'Jm', [128, 128], BF16)
            fwt = [sb(P, 'fwt%d' % i, [128, 2, 8, 128], BF16) for i in range(3)]
            bwt = [sb(P, 'bwt%d' % i, [128, 32, 128], BF16) for i in range(3)]
            kt = [sb(P, 'kt%d' % i, [128, 2, 512], BF16) for i in range(3)]
            ta = sb(P, 'ta', [128, 512], F32)
            tb_ = sb(P, 'tb', [128, 512], F32)
            tc = sb(P, 'tc', [128, 512], F32)
            td = sb(P, 'td', [128, 512], F32)
            Ac = [sb(P, 'Ac%d' % i, [128, 512], F32) for i in range(2)]
            dd = [sb(P, 'dd%d' % i, [128, 512], F32) for i in range(2)]
            ss = [sb(P, 'ss%d' % i, [128, 512], F32) for i in range(2)]
            yq2 = [sb(P, 'yq2_%d' % i, [128, 512], F32) for i in range(2)]
            ssh2 = [sb(P, 'ssh2_%d' % i, [128, 1], F32) for i in range(2)]
            yhb2 = [sb(P, 'yhb2_%d' % i, [128, 512], BF16) for i in range(2)]
            sgt = [sb(P, 'sgt%d' % i, [128, 512], BF16) for i in range(2)]
            yts = [sb(P, 'yth%d' % i, [128, 4, 128], BF16) for i in range(2)]
            Q = [ps(P, 'Q%d' % i, [128, 512]) for i in range(4)]
            Rb = ps(P, 'Rb', [128, 512])
            PT4 = [ps(P, 'PT4_%d' % i, [128, 4, 128]) for i in range(2)]
            S.dma('sp', hyg[:], bc(hyg_d, 512), writes=['hyg'])
            S.dma('sp', Jm[:], jmat_d.ap(), writes=['Jm'])

            yhb_lo = [sb(P, 'yhblo%d' % i, [128, 512], BF16) for i in range(2)]

            def emit_T4(b, tok0, hl, src, srcn):
                sfx = str(hl)
                mv = ident if hl == 0 else Jm
                mvn = 'ident' if hl == 0 else 'Jm'
                for g in range(4):
                    S.op('pe', lambda e: e.matmul(PT4[hl][:, g, :], lhsT=src[:, g * 128:(g + 1) * 128], rhs=mv[:], start=True, stop=True),
                         reads=[srcn, mvn], writes=['PT4' + sfx], signal=(g == 3))
                yt = yts[hl]
                ytn = 'yth' + sfx
                S.op('act', lambda e: e.copy(out=yt[:], in_=PT4[hl][:]), reads=['PT4' + sfx], writes=[ytn])
                S.dma('act', yT_s.ap()[tok0 // 128, :, 0:4, :], yt[:], reads=[ytn])

            nf = 0
            nb_ = 0
            nq = 0
            def load_in(b, which):
                t_, tn = [(uv, 'uv'), (x1, 'x1'), (x2, 'x2')][which]
                S.dma('sp', t_[:], U_s.ap()[which, b * L:(b + 1) * L, :].rearrange('(i p) c -> p i c', p=128), writes=[tn])

            def prep(which):
                t_, tn = [(uv, 'uv'), (x1, 'x1'), (x2, 'x2')][which]
                for a_ in range(8):
                    c = 7 - a_
                    qb_ = Q[nqc[0] % 4]
                    qn_ = 'Q%d' % (nqc[0] % 4)
                    nqc[0] += 1
                    S.op('pe', lambda e: e.matmul(qb_[:], lhsT=Jm[:], rhs=t_[:, c, :], start=True, stop=True), reads=['Jm', tn], writes=[qn_])
                    if which == 0:
                        S.op('dve', lambda e: e.tensor_tensor(out=vp[:, a_, :], in0=qb_[:], in1=uv[:, 8 + a_, :], op=ALU.add), reads=[qn_, 'uv'], writes=['vp%d' % a_])
                        S.op('dve', lambda e: e.tensor_tensor(out=vm[:, a_, :], in0=uv[:, 8 + a_, :], in1=qb_[:], op=ALU.subtract), reads=[qn_, 'uv'], writes=['vm%d' % a_])
                    elif which == 1:
                        S.op('act', lambda e: e.copy(out=x1r[:, a_, :], in_=qb_[:]), reads=[qn_], writes=['x1r'])
                    else:
                        S.op('act', lambda e: e.copy(out=x2r[:, a_, :], in_=qb_[:]), reads=[qn_], writes=['x2r'])

            nqc = [0]
            for w_ in range(3):
                load_in(0, w_)
            for b in range(NB):
                prep(0)
                prep(1)
                for o in range(2):
                    for ft in range(16):
                        fw = fwt[nf % 3]
                        fn_ = 'fwt%d' % (nf % 3)
                        k_ = kt[nf % 3]
                        kn = 'kt%d' % (nf % 3)
                        pR = Q[(nf % 2) * 2]
                        pRn = 'Q%d' % ((nf % 2) * 2)
                        pI = Q[(nf % 2) * 2 + 1]
                        pIn = 'Q%d' % ((nf % 2) * 2 + 1)
                        nf += 1
                        S.dma('sp', fw[:].rearrange('p r m g -> p (r m g)'), Fh_d.ap()[ft], writes=[fn_])
                        if ft == 6 and o == 1 and b + 1 < NB:
                            load_in(b + 1, 1)
                        if ft == 6 and o == 0 and b >= 1:
                            load_in(b, 2)
                        S.dma('sp', k_[:], Kh_s.ap()[ft, :, :, o * 512:(o + 1) * 512], writes=[kn])
                        for mc in range(8):
                            S.op('pe', lambda e: e.matmul(pR[:], lhsT=fw[:, 0, mc, :], rhs=vp[:, mc, :], start=(mc == 0), stop=(mc == 7)),
                                 reads=[fn_, 'vp%d' % mc], writes=[pRn], signal=(mc == 7))
                        for mc in range(8):
                            S.op('pe', lambda e: e.matmul(pI[:], lhsT=fw[:, 1, mc, :], rhs=vm[:, mc, :], start=(mc == 0), stop=(mc == 7)),
                                 reads=[fn_, 'vm%d' % mc], writes=[pIn], signal=(mc == 7))
                        S.op('dve', lambda e: e.tensor_tensor(out=ta[:], in0=pR[:], in1=k_[:, 0, :], op=ALU.mult), reads=[pRn, kn], writes=['ta'])
                        S.op('dve', lambda e: e.tensor_tensor(out=tb_[:], in0=pI[:], in1=k_[:, 1, :], op=ALU.mult), reads=[pIn, kn], writes=['tb'])
                        S.op('pool', lambda e: e.tensor_tensor(out=Yh[:, ft, :], in0=ta[:], in1=tb_[:], op=ALU.subtract), reads=['ta', 'tb'], writes=['Yh%d' % ft])
                        S.op('dve', lambda e: e.tensor_tensor(out=tc[:], in0=pR[:], in1=k_[:, 1, :], op=ALU.mult), reads=[pRn, kn], writes=['tc'])
                        S.op('dve', lambda e: e.tensor_tensor(out=td[:], in0=pI[:], in1=k_[:, 0, :], op=ALU.mult), reads=[pIn, kn], writes=['td'])
                        S.op('pool', lambda e: e.tensor_tensor(out=Yh[:, 16 + ft, :], in0=tc[:], in1=td[:], op=ALU.add), reads=['tc', 'td'], writes=['Yh%d' % (16 + ft)])
                        if ft == 0:
                            S.op('pool', lambda e: e.tensor_copy(out=Yh[0:1, 0, :], in_=ta[0:1, :]), reads=['ta', 'Yh0'], writes=['Yh0'])
                            S.op('pool', lambda e: e.tensor_copy(out=Yh[0:1, 16, :], in_=tb_[0:1, :]), reads=['tb', 'Yh16'], writes=['Yh16'])
                    if o == 0:
                        prep(2)
                    pend = None
                    for jt in range(8):
                        bw = bwt[nb_ % 3]
                        bn = 'bwt%d' % (nb_ % 3)
                        par = nb_ % 2
                        sfx = str(par)
                        pA = Q[par * 2]
                        pAn = 'Q%d' % (par * 2)
                        pB = Q[par * 2 + 1]
                        pBn = 'Q%d' % (par * 2 + 1)
                        nb_ += 1
                        S.dma('sp', bw[:].rearrange('p g t -> p (g t)'), Bh_d.ap()[jt], writes=[bn])
                        if jt == 3 and o == 0 and b + 1 < NB:
                            load_in(b + 1, 0)
                        for gt in range(16):
                            S.op('pe', lambda e: e.matmul(pA[:], lhsT=bw[:, gt, :], rhs=Yh[:, gt, :], start=(gt == 0), stop=(gt == 15)),
                                 reads=[bn, 'Yh%d' % gt], writes=[pAn], signal=(gt == 15))
                        for gt in range(16, 32):
                            S.op('pe', lambda e: e.matmul(pB[:], lhsT=bw[:, gt, :], rhs=Yh[:, gt, :], start=(gt == 16), stop=(gt == 31)),
                                 reads=[bn, 'Yh%d' % gt], writes=[pBn], signal=(gt == 31))
                        S.op('act', lambda e: e.copy(out=Ac[par][:], in_=pA[:]), reads=[pAn], writes=['Ac' + sfx])
                        S.op('dve', lambda e: e.tensor_tensor(out=dd[par][:], in0=Ac[par][:], in1=pB[:], op=ALU.subtract), reads=['Ac' + sfx, pBn], writes=['dd' + sfx])
                        S.op('dve', lambda e: e.tensor_tensor(out=ss[par][:], in0=Ac[par][:], in1=pB[:], op=ALU.add), reads=['Ac' + sfx, pBn], writes=['ss' + sfx])
                        if o == 0:
                            S.op('pool', lambda e: e.tensor_tensor(out=dd[par][:], in0=dd[par][:], in1=x1[:, 8 + jt, :], op=ALU.mult), reads=['dd' + sfx, 'x1'], writes=['dd' + sfx])
                            S.op('pool', lambda e: e.tensor_tensor(out=ss[par][:], in0=ss[par][:], in1=x1r[:, jt, :], op=ALU.mult), reads=['ss' + sfx, 'x1r'], writes=['ss' + sfx])
                            S.op('dve', lambda e: e.tensor_tensor(out=vp[:, jt, :], in0=dd[par][:], in1=ss[par][:], op=ALU.add), reads=['dd' + sfx, 'ss' + sfx], writes=['vp%d' % jt])
                            S.op('dve', lambda e: e.tensor_tensor(out=vm[:, jt, :], in0=dd[par][:], in1=ss[par][:], op=ALU.subtract), reads=['dd' + sfx, 'ss' + sfx], writes=['vm%d' % jt])
                        else:
                            if pend is None:
                                pend = []
                            npend = []
                            for lag_, args in pend:
                                if lag_ <= 1:
                                    emit_T4(*args)
                                else:
                                    npend.append((lag_ - 1, args))
                            pend = npend
                            for hl in range(2):
                                hs = str(hl)
                                ysrc, ysn = (dd[par], 'dd' + sfx) if hl == 0 else (ss[par], 'ss' + sfx)
                                ck = 8 + jt if hl == 0 else 7 - jt
                                tok0 = b * L + ck * 128
                                xg, xgn = (x2[:, 8 + jt, :], 'x2') if hl == 0 else (x2r[:, jt, :], 'x2r')
                                sg = sgt[hl]
                                sgn = 'sgt' + hs
                                S.dma('sp', sg[:], SG_s.ap()[tok0: tok0 + 128, :], writes=[sgn])
                                S.op('pool', lambda e: e.tensor_tensor(out=ysrc[:], in0=ysrc[:], in1=xg, op=ALU.mult), reads=[ysn, xgn], writes=[ysn])
                                S.op('dve', lambda e: e.scalar_tensor_tensor(out=yq2[hl][:], in0=ysrc[:], scalar=1.0, in1=ysrc[:], op0=ALU.mult, op1=ALU.mult, accum_out=ssh2[hl][:, 0:1]),
                                     reads=[ysn], writes=['yq4' + hs, 'ssh' + hs])
                                S.op('act', lambda e: e.activation(out=ssh2[hl][:], in_=ssh2[hl][:], func=AF.Sqrt, scale=1.0 / 512, bias=eps_t[:, 0:1]),
                                     reads=['ssh' + hs, 'eps'], writes=['ssh' + hs])
                                S.op('dve', lambda e: e.reciprocal(out=ssh2[hl][:], in_=ssh2[hl][:]), reads=['ssh' + hs], writes=['ssh' + hs])
                                S.op('dve', lambda e: e.scalar_tensor_tensor(out=yq2[hl][:], in0=ysrc[:], scalar=ssh2[hl][:, 0:1], in1=hyg[:], op0=ALU.mult, op1=ALU.mult),
                                     reads=[ysn, 'ssh' + hs, 'hyg', 'yq4' + hs], writes=['yq4' + hs])
                                if hl == 0:
                                    S.op('pool', lambda e: e.tensor_tensor(out=yhb2[hl][:], in0=yq2[hl][:], in1=sg[:], op=ALU.mult), reads=['yq4' + hs, sgn], writes=['yhb' + hs])
                                else:
                                    S.op('pe', lambda e: e.matmul(Rb[:], lhsT=Jm[:], rhs=sg[:], start=True, stop=True), reads=['Jm', sgn], writes=['Rb'])
                                    ylo = yhb_lo[jt % 2]
                                    ylon = 'yhblo%d' % (jt % 2)
                                    S.op('dve', lambda e: e.tensor_tensor(out=ylo[:], in0=yq2[hl][:], in1=Rb[:], op=ALU.mult), reads=['yq4' + hs, 'Rb'], writes=[ylon])
                                if hl == 0:
                                    pend.append((1, (b, tok0, 0, yhb2[0], 'yhb0')))
                                else:
                                    pend.append((2, (b, tok0, 1, ylo, ylon)))
                    if pend is not None:
                        for lag_, args in pend:
                            emit_T4(*args)
            S.barrier()

        with contextlib.ExitStack() as P:
            Wo = sb(P, 'Wo', [128, 8, D], BF16)
            wst = [sb(P, 'wso%d' % i, [128, D], F32) for i in range(2)]
            yt = [sb(P, 'yt%d' % i, [128, 8, 128], BF16) for i in range(3)]
            xr = [sb(P, 'xr%d' % i, [128, D], F32) for i in range(3)]
            ot = [sb(P, 'ot%d' % i, [128, D], F32) for i in range(2)]
            psO = [ps(P, 'psW%d' % i, [128, 512]) for i in range(4)]
            for kc in range(8):
                wb = wst[kc % 2]
                wn = 'wso%d' % (kc % 2)
                S.dma('sp', wb[:], wout_d.ap()[kc * 128:(kc + 1) * 128, :], writes=[wn])
                if kc % 2 == 0:
                    S.op('act', lambda e: e.copy(out=Wo[:, kc, :], in_=wb[:]), reads=[wn], writes=['Wo'])
                else:
                    S.op('dve', lambda e: e.tensor_copy(out=Wo[:, kc, :], in_=wb[:]), reads=[wn], writes=['Wo'])
            for c in range(NB * NT):
                y_ = yt[c % 3]
                yn = 'yt%d' % (c % 3)
                x_ = xr[c % 3]
                xn = 'xr%d' % (c % 3)
                o_ = ot[c % 2]
                on = 'ot%d' % (c % 2)
                S.dma('sp', y_[:], yT_s.ap()[c], writes=[yn])
                S.dma('sp', x_[:], x_d.ap()[c * 128:(c + 1) * 128, :], writes=[xn])
                for hf in range(2):
                    po = psO[(c % 2) * 2 + hf]
                    pon = 'psW%d' % ((c % 2) * 2 + hf)
                    for fc in range(8):
                        S.op('pe', lambda e: e.matmul(po[:], lhsT=y_[:, fc, :], rhs=Wo[:, fc, hf * 512:(hf + 1) * 512], start=(fc == 0), stop=(fc == 7)),
                             reads=[yn, 'Wo'], writes=[pon], signal=(fc == 7))
                    S.op('dve', lambda e: e.tensor_tensor(out=o_[:, hf * 512:(hf + 1) * 512], in0=po[:], in1=x_[:, hf * 512:(hf + 1) * 512], op=ALU.add),
                         reads=[pon, xn], writes=[on])
                S.dma('act', out_d.ap()[c * 128:(c + 1) * 128, :], o_[:], reads=[on])
            S.finish('sp')
            S.finish('pool')
            S.finish('act')
    return nc


_NC = None


def kernel(x, norm_g, w_in, conv_w, conv_b, filt_w1, filt_b1, filt_w2, filt_b2, filt_w3, filt_b3,
           filt_w4, filt_sin_freq, hyena_bias, q_norm_g, k_norm_g, attn_sink, hy_out_norm_g,
           attn_out_norm_g, w_out):
    global _NC
    f32 = lambda a: np.ascontiguousarray(np.asarray(a, dtype=np.float32))
    x = f32(x)
    C = _consts()
    shared = dict(
        w_in=f32(w_in)[0], w_out=f32(w_out)[0],
        gcol=np.ascontiguousarray(f32(norm_g)[0].reshape(8, 128).T),
        cwc=np.ascontiguousarray(np.concatenate([f32(conv_w)[0], f32(conv_b)], axis=0).reshape(4, 12, 128).transpose(2, 1, 0)).reshape(128, 48),
        filt_w1=f32(filt_w1)[0], filt_w2=f32(filt_w2)[0], filt_w3=f32(filt_w3)[0], filt_w4=f32(filt_w4)[0],
        fcols=np.ascontiguousarray(np.stack([f32(filt_b1)[0], f32(filt_b2)[0], f32(filt_b3)[0], f32(filt_sin_freq)[0]], axis=1)),
        hyena_bias=f32(hyena_bias)[0].reshape(1, 1024),
        q_norm_g=f32(q_norm_g)[0].reshape(1, 64), k_norm_g=f32(k_norm_g)[0].reshape(1, 64),
        attn_sink=f32(attn_sink)[0].reshape(1, 8),
        hy_out_norm_g=f32(hy_out_norm_g)[0].reshape(1, 512), attn_out_norm_g=f32(attn_out_norm_g)[0].reshape(1, 512),
        zT=C['zT'], decay=C['decay'],
        Fw_t=C['Fw_t'].reshape(32, 128, 32 * 128), Fh_t=C['Fh_t'].reshape(16, 128, 2 * 8 * 128), Bh_t=C['Bh_t'].reshape(8, 128, 32 * 128), jmat=C['jmat'],
        rope_cs=C['rope_cs'].reshape(128, 2 * 16 * 32), ident=C['ident'], masks=C['masks'].reshape(128, 256),
    )
    in_maps = []
    for c in range(NCORES):
        xc = x[c * NB:(c + 1) * NB].reshape(NB * L, D)
        m = dict(shared)
        m['x'] = np.ascontiguousarray(xc)
        m['xT'] = np.ascontiguousarray(xc.T)
        in_maps.append(m)
    if _NC is None:
        _NC = build_nc()
    res = run_bass_kernel_spmd(_NC, in_maps, core_ids=list(range(NCORES)))
    kernel.last_results = res
    out = np.concatenate([r['out'].reshape(NB, L, D) for r in res.results], axis=0)
    return out.astype(np.float32)
```
